# Optimizing a Trainium2 kernel written in Bass

```python
import math
import jax, jax.numpy as jnp
from jax import lax
import numpy as np

D_MODEL = 2048
BATCH = 4
SEQ = 2048
DEPTH = 4
DEC_BATCH = 32
DEC_SEQ = 1
PAST_LEN = 16384
PAGE_SIZE = 128

H_A = D_MODEL // 128
HD_A = 64
KV_A = H_A // 8
WINDOW = 128
ROPE_THETA = 10000.0
H_B = 4
DK_B = D_MODEL // 16
DV_B = D_MODEL // 16
H_C = 4
DK_C = D_MODEL // 32
DV_C = D_MODEL // 16
GLA_RANK = 16
GLA_NORMALIZER = 16.0
CHUNK = 64
W_A = H_A * HD_A
W_B = H_B * DV_B
W_C = H_C * DV_C
MIX = W_A + W_B + W_C
N_KEYS = 128
N_EXPERTS = N_KEYS * N_KEYS
PEER_HEADS = 8
PEER_TOPK = 16
D_QUERY = 256
D_HALF = D_QUERY // 2
PEER_BLOCK = 128
ALPHA = (2 * DEPTH) ** 0.25
BETA = (8 * DEPTH) ** -0.25
LN_EPS = 1e-5
RMS_EPS = 1e-6
MASK_VALUE = -1e30
F_FLOOR = 1e-30
VALUE_PARTS = ('v_a', 'i_b', 'v_c')

kernel_name = 'hymba_swa_hgrn2_gla_peer_step'


def _proj_layout():
    return [('q_a', W_A), ('k_a', KV_A * HD_A), ('v_a', KV_A * HD_A),
            ('q_b', H_B * DK_B), ('f_b', H_B * DK_B), ('i_b', W_B), ('g_b', W_B),
            ('q_c', H_C * DK_C), ('k_c', H_C * DK_C), ('v_c', W_C), ('g_c', W_C), ('a_c', GLA_RANK)]


def _split_points():
    sizes = [sz for _, sz in _proj_layout()]
    return [int(s) for s in np.cumsum(sizes)[:-1]]


def layer_norm(x, g, b):
    xf = x.astype(jnp.float32)
    mu = jnp.mean(xf, -1, keepdims=True)
    var = jnp.mean(jnp.square(xf - mu), -1, keepdims=True)
    return ((xf - mu) * lax.rsqrt(var + LN_EPS)).astype(x.dtype) * g + b


def rms_norm(x, w):
    xf = x.astype(jnp.float32)
    return (xf * lax.rsqrt(jnp.mean(xf * xf, -1, keepdims=True) + RMS_EPS)).astype(x.dtype) * w


def rope(x, pos):
    half = HD_A // 2
    inv = ROPE_THETA ** (-jnp.arange(half, dtype=jnp.float32) / half)
    ang = pos.astype(jnp.float32)[:, None] * inv[None, :]
    cos = jnp.cos(ang)[None, :, None, :]
    sin = jnp.sin(ang)[None, :, None, :]
    xf = x.astype(jnp.float32)
    x1, x2 = xf[..., :half], xf[..., half:]
    return jnp.concatenate([x1 * cos - x2 * sin, x2 * cos + x1 * sin], -1).astype(x.dtype)


def sink_softmax(s, valid, sink):
    s = jnp.where(valid, s, MASK_VALUE)
    m = jnp.maximum(jnp.max(s, -1, keepdims=True), sink)
    p = jnp.where(valid, jnp.exp(s - m), 0.0)
    return p / (jnp.sum(p, -1, keepdims=True) + jnp.exp(sink - m))


def window_attn_prompt(q, k, v, sinks):
    Bn, L = q.shape[:2]
    G = H_A // KV_A
    nb = L // WINDOW
    qb = q.reshape(Bn, nb, WINDOW, KV_A, G, HD_A)
    kb = k.reshape(Bn, nb, WINDOW, KV_A, HD_A)
    vb = v.reshape(Bn, nb, WINDOW, KV_A, HD_A)
    pad = ((0, 0), (1, 0), (0, 0), (0, 0), (0, 0))
    kc = jnp.concatenate([jnp.pad(kb, pad)[:, :-1], kb], axis=2)
    vc = jnp.concatenate([jnp.pad(vb, pad)[:, :-1], vb], axis=2)
    s = jnp.einsum('bnqkgd,bnskd->bnkgqs', qb, kc).astype(jnp.float32) * (HD_A ** -0.5)
    i = jnp.arange(WINDOW)[:, None]
    j = jnp.arange(2 * WINDOW)[None, :]
    diff = i + WINDOW - j
    band = (diff >= 0) & (diff <= WINDOW)
    blk = jnp.arange(nb)[:, None, None]
    valid = band[None] & ((blk > 0) | (j[None] >= WINDOW))
    sink = sinks.astype(jnp.float32).reshape(KV_A, G)[None, None, :, :, None, None]
    p = sink_softmax(s, valid[None, :, None, None], sink)
    o = jnp.einsum('bnkgqs,bnskd->bnqkgd', p.astype(v.dtype), vc)
    return o.reshape(Bn, L, H_A * HD_A)


def window_attn_sample(q, k_new, v_new, k_buf, v_buf, sinks):
    Bn, L = q.shape[:2]
    G = H_A // KV_A
    Wb = k_buf.shape[1]
    kc = jnp.concatenate([k_buf.astype(k_new.dtype), k_new], axis=1)
    vc = jnp.concatenate([v_buf.astype(v_new.dtype), v_new], axis=1)
    qg = q.reshape(Bn, L, KV_A, G, HD_A)
    s = jnp.einsum('bqkgd,bskd->bkgqs', qg, kc).astype(jnp.float32) * (HD_A ** -0.5)
    diff = jnp.arange(L)[:, None] + Wb - jnp.arange(Wb + L)[None, :]
    valid = (diff >= 0) & (diff <= WINDOW)
    sink = sinks.astype(jnp.float32).reshape(KV_A, G)[None, :, :, None, None]
    p = sink_softmax(s, valid, sink)
    o = jnp.einsum('bkgqs,bskd->bqkgd', p.astype(vc.dtype), vc).reshape(Bn, L, H_A * HD_A)
    return o, kc[:, -Wb:], vc[:, -Wb:]


def gated_recurrence(q, k, v, log_a, s0):
    Bn, L, H, dk = q.shape
    dv = v.shape[-1]
    C = math.gcd(L, CHUNK)
    n = L // C

    def chunks(t):
        return t.astype(jnp.float32).reshape(Bn, n, C, H, t.shape[-1]).transpose(1, 0, 3, 2, 4)

    causal = jnp.tril(jnp.ones((C, C), dtype=bool))[:, :, None]

    def step(S, xs):
        qc, kc, vc, ac = xs
        b = jnp.cumsum(ac, axis=2)
        o = jnp.einsum('bhtd,bhde->bhte', qc * jnp.exp(b), S)
        rel = jnp.where(causal, b[:, :, :, None, :] - b[:, :, None, :, :], 0.0)
        decay = jnp.where(causal, jnp.exp(rel), 0.0)
        att = jnp.einsum('bhtd,bhsd,bhtsd->bhts', qc, kc, decay)
        o = o + jnp.einsum('bhts,bhse->bhte', att, vc)
        b_end = b[:, :, -1:, :]
        S = jnp.exp(b_end[:, :, 0, :, None]) * S + jnp.einsum('bhsd,bhse->bhde', kc * jnp.exp(b_end - b), vc)
        return S, o

    S, o = lax.scan(step, s0.astype(jnp.float32), (chunks(q), chunks(k), chunks(v), chunks(log_a)))
    o = o.transpose(1, 0, 3, 2, 4).reshape(Bn, L, H, dv)
    return o.astype(v.dtype), S.astype(s0.dtype)


def peer(x, wq, keys, u, v):
    Bn, L, D = x.shape
    T = Bn * L
    xt = x.reshape(T, D)
    q = (xt @ wq).reshape(T, PEER_HEADS, 2, D_HALF)
    sc = jnp.einsum('thpd,hpnd->thpn', q, keys).astype(jnp.float32)
    s_top, i_top = lax.top_k(sc, PEER_TOPK)
    cand = (s_top[:, :, 0, :, None] + s_top[:, :, 1, None, :]).reshape(T, PEER_HEADS, PEER_TOPK * PEER_TOPK)
    cand_idx = (i_top[:, :, 0, :, None] * N_KEYS + i_top[:, :, 1, None, :]).reshape(T, PEER_HEADS, PEER_TOPK * PEER_TOPK)
    best, sel = lax.top_k(cand, PEER_TOPK)
    idx = jnp.take_along_axis(cand_idx, sel, axis=-1)
    gate = jax.nn.softmax(best, axis=-1)
    tb = math.gcd(T, PEER_BLOCK)
    nb = T // tb

    def block(args):
        xb, ib, gb = args
        h = jnp.einsum('td,thkd->thk', xb, jnp.take(u, ib, axis=0))
        w = (gb * jax.nn.gelu(h.astype(jnp.float32), approximate=False)).astype(xb.dtype)
        return jnp.einsum('thk,thkd->td', w, jnp.take(v, ib, axis=0))

    out = lax.map(block, (xt.reshape(nb, tb, D),
                          idx.reshape(nb, tb, PEER_HEADS, PEER_TOPK),
                          gate.reshape(nb, tb, PEER_HEADS, PEER_TOPK)))
    return out.reshape(Bn, L, D)


def trunk_layer(x, pos, p, lb, kv_buf, s_b, s_c):
    Bn, L, _ = x.shape
    proj = x @ p['w_in']
    (q_a, k_a, v_a, q_b, f_b, i_b, g_b, q_c, k_c, v_c, g_c, a_c) = jnp.split(proj, _split_points(), axis=-1)
    q_a = rope(q_a.reshape(Bn, L, H_A, HD_A), pos)
    k_a = rope(k_a.reshape(Bn, L, KV_A, HD_A), pos)
    v_a = v_a.reshape(Bn, L, KV_A, HD_A)
    if kv_buf is None:
        o_a = window_attn_prompt(q_a, k_a, v_a, p['sinks'])
        k_keep, v_keep = k_a[:, -WINDOW:], v_a[:, -WINDOW:]
    else:
        o_a, k_keep, v_keep = window_attn_sample(q_a, k_a, v_a, kv_buf[0], kv_buf[1], p['sinks'])
    if s_b is None:
        s_b = jnp.zeros((Bn, H_B, DK_B, DV_B), x.dtype)
        s_c = jnp.zeros((Bn, H_C, DK_C, DV_C), x.dtype)
    zf = f_b.astype(jnp.float32).reshape(Bn, L, H_B, DK_B)
    lbh = lb.reshape(H_B, DK_B)
    f_gate = lbh + (1.0 - lbh) * jax.nn.sigmoid(zf)
    log_f = jnp.log(jnp.maximum(f_gate, F_FLOOR))
    key_b = 1.0 - f_gate
    q_bh = jax.nn.silu(q_b.reshape(Bn, L, H_B, DK_B))
    o_b, s_b_new = gated_recurrence(q_bh, key_b, i_b.reshape(Bn, L, H_B, DV_B), log_f, s_b)
    o_b = rms_norm(o_b, p['hgrn_norm_w']).reshape(Bn, L, W_B) * jax.nn.silu(g_b)
    log_a = jax.nn.log_sigmoid((a_c @ p['gla_wa2'] + p['gla_ba']).astype(jnp.float32)) / GLA_NORMALIZER
    q_ch = q_c.reshape(Bn, L, H_C, DK_C) * (DK_C ** -0.5)
    o_c, s_c_new = gated_recurrence(q_ch, k_c.reshape(Bn, L, H_C, DK_C), v_c.reshape(Bn, L, H_C, DV_C),
                                    log_a.reshape(Bn, L, H_C, DK_C), s_c)
    o_c = rms_norm(o_c, p['gla_norm_w']).reshape(Bn, L, W_C) * jax.nn.silu(g_c)
    mix = jnp.concatenate([o_a, o_b, o_c], axis=-1) @ p['w_out']
    x = layer_norm(ALPHA * x + mix, p['ln1_g'], p['ln1_b'])
    ff = peer(x, p['peer_wq'], p['peer_keys'], p['peer_u'], p['peer_v'])
    x = layer_norm(ALPHA * x + ff, p['ln2_g'], p['ln2_b'])
    return x, k_keep, v_keep, s_b_new, s_c_new


def setup_inputs(seed: int = 0) -> dict:
    key = jax.random.key(seed)
    ks = jax.random.split(key, 24)
    f32 = jnp.float32

    def nrm(k, shape, s):
        return jax.random.normal(k, shape, f32) * s

    layout = _proj_layout()
    in_cols = sum(sz for _, sz in layout)
    col_scale = np.concatenate([np.full((sz,), BETA if name in VALUE_PARTS else 1.0, np.float32)
                                for name, sz in layout])
    win_buf = min(WINDOW, PAST_LEN)
    return {
        'x_prompt': nrm(ks[0], (BATCH, SEQ, D_MODEL), 1.0),
        'x_sample': nrm(ks[1], (DEC_BATCH, DEC_SEQ, D_MODEL), 1.0),
        'cache_k': nrm(ks[2], (DEPTH, DEC_BATCH, win_buf, KV_A, HD_A), 1.0),
        'cache_v': nrm(ks[3], (DEPTH, DEC_BATCH, win_buf, KV_A, HD_A), 1.0),
        'state_hgrn': nrm(ks[4], (DEPTH, DEC_BATCH, H_B, DK_B, DV_B), 0.5),
        'state_gla': nrm(ks[5], (DEPTH, DEC_BATCH, H_C, DK_C, DV_C), 0.5),
        'w_in': nrm(ks[6], (DEPTH, D_MODEL, in_cols), D_MODEL ** -0.5) * jnp.asarray(col_scale),
        'w_out': nrm(ks[7], (DEPTH, MIX, D_MODEL), BETA * MIX ** -0.5),
        'attn_sinks': nrm(ks[8], (DEPTH, H_A), 0.5),
        'hgrn_norm_w': 1.0 + nrm(ks[9], (DEPTH, H_B, DV_B), 0.02),
        'lb_logits': nrm(ks[10], (DEPTH, H_B * DK_B), 0.1),
        'gla_wa2': nrm(ks[11], (DEPTH, GLA_RANK, H_C * DK_C), GLA_RANK ** -0.5),
        'gla_ba': nrm(ks[12], (DEPTH, H_C * DK_C), 0.02),
        'gla_norm_w': 1.0 + nrm(ks[13], (DEPTH, H_C, DV_C), 0.02),
        'ln1_g': 1.0 + nrm(ks[14], (DEPTH, D_MODEL), 0.02),
        'ln1_b': nrm(ks[15], (DEPTH, D_MODEL), 0.02),
        'ln2_g': 1.0 + nrm(ks[16], (DEPTH, D_MODEL), 0.02),
        'ln2_b': nrm(ks[17], (DEPTH, D_MODEL), 0.02),
        'peer_wq': nrm(ks[18], (DEPTH, D_MODEL, PEER_HEADS * D_QUERY), D_MODEL ** -0.5),
        'peer_keys': nrm(ks[19], (DEPTH, PEER_HEADS, 2, N_KEYS, D_HALF), D_HALF ** -0.5),
        'peer_u': nrm(ks[20], (DEPTH, N_EXPERTS, D_MODEL), D_MODEL ** -0.5),
        'peer_v': nrm(ks[21], (DEPTH, N_EXPERTS, D_MODEL), BETA * PEER_HEADS ** -0.5),
    }


def reference(x_prompt, x_sample, cache_k, cache_v, state_hgrn, state_gla, w_in, w_out, attn_sinks,
              hgrn_norm_w, lb_logits, gla_wa2, gla_ba, gla_norm_w, ln1_g, ln1_b, ln2_g, ln2_b,
              peer_wq, peer_keys, peer_u, peer_v):
    sm = jax.nn.softmax(lb_logits.astype(jnp.float32), axis=0)
    lower = jnp.cumsum(sm, axis=0) - sm[0:1]
    pos_p = jnp.arange(x_prompt.shape[1], dtype=jnp.int32)
    pos_s = PAST_LEN + jnp.arange(x_sample.shape[1], dtype=jnp.int32)
    hp, hs = x_prompt, x_sample
    kp_l, vp_l, bp_l, cp_l, ks_l, vs_l, bs_l, cs_l = [], [], [], [], [], [], [], []
    for l in range(DEPTH):
        p = {'w_in': w_in[l], 'w_out': w_out[l], 'sinks': attn_sinks[l], 'hgrn_norm_w': hgrn_norm_w[l],
             'gla_wa2': gla_wa2[l], 'gla_ba': gla_ba[l], 'gla_norm_w': gla_norm_w[l],
             'ln1_g': ln1_g[l], 'ln1_b': ln1_b[l], 'ln2_g': ln2_g[l], 'ln2_b': ln2_b[l],
             'peer_wq': peer_wq[l], 'peer_keys': peer_keys[l], 'peer_u': peer_u[l], 'peer_v': peer_v[l]}
        hp, kp, vp, bp, cp = trunk_layer(hp, pos_p, p, lower[l], None, None, None)
        hs, ks_, vs_, bs_, cs_ = trunk_layer(hs, pos_s, p, lower[l], (cache_k[l], cache_v[l]),
                                             state_hgrn[l], state_gla[l])
        kp_l.append(kp); vp_l.append(vp); bp_l.append(bp); cp_l.append(cp)
        ks_l.append(ks_); vs_l.append(vs_); bs_l.append(bs_); cs_l.append(cs_)
    new_k_prompt = jnp.stack(kp_l)
    new_v_prompt = jnp.stack(vp_l)
    new_hgrn_prompt = jnp.stack(bp_l)
    new_gla_prompt = jnp.stack(cp_l)
    new_k_sample = jnp.stack(ks_l)
    new_v_sample = jnp.stack(vs_l)
    new_hgrn_sample = jnp.stack(bs_l)
    new_gla_sample = jnp.stack(cs_l)
    return (hp, hs, new_k_prompt, new_v_prompt, new_hgrn_prompt, new_gla_prompt,
            new_k_sample, new_v_sample, new_hgrn_sample, new_gla_sample)
```

```python
import math
import os
import numpy as np
from contextlib import ExitStack
import concourse.bass as bass
import concourse.mybir as mybir
from concourse.bass_utils import run_bass_kernel_spmd

F32 = mybir.dt.float32
F32R = mybir.dt.float32r
I32 = mybir.dt.int32
U32 = mybir.dt.uint32
AF = mybir.ActivationFunctionType
ALU = mybir.AluOpType
AX = mybir.AxisListType

D = 2048
DEPTH = int(os.environ.get("KDEPTH", "4"))
SEQ = 2048
NG = int(os.environ.get("KNG", "8"))
PAST = 16384
ALPHA = (2 * 4) ** 0.25
NU = 36
U_TM, U_FM, U_WO, U_WQ = 0, 9, 20, 28
NEG = -1.0e30
NCORES = 4
NS = 32 // NCORES


class Buf:
    __slots__ = ("name", "w", "r")

    def __init__(self, name):
        self.name = name
        self.w = None
        self.r = []


class Fw:
    ENG = ("pe", "dve", "act", "pool", "sp")

    def __init__(self, nc, es, n_dma_sems=12):
        self.nc = nc
        self.ops = {e: [] for e in self.ENG}
        self.sems = {}
        self.cnt = {}
        self.waited = {e: {} for e in self.ENG}
        for e in self.ENG:
            self.sems[e] = es.enter_context(nc.semaphore("s_" + e))
            self.cnt[e] = 0
        self.dpool = {}
        for q in ("sp", "act", "pool"):
            lst = []
            for i in range(n_dma_sems):
                k = "d_%s_%d" % (q, i)
                self.sems[k] = es.enter_context(nc.semaphore(k))
                self.cnt[k] = 0
                lst.append(k)
            self.dpool[q] = [lst, 0]
        self.nbuf = 0
        self.out_events = []
        self.ninst = 0

    def buf(self, name=None):
        self.nbuf += 1
        return Buf(name or ("b%d" % self.nbuf))

    def _need(self, eng, ev, waits):
        if ev is None:
            return
        k, v = ev
        if self.waited[eng].get(k, 0) >= v:
            return
        if waits.get(k, 0) < v:
            waits[k] = v

    def _deps(self, eng, reads, writes):
        waits = {}
        for b in reads:
            self._need(eng, b.w, waits)
        for b in writes:
            self._need(eng, b.w, waits)
            for ev in b.r:
                self._need(eng, ev, waits)
        return waits

    def _commit(self, ev, reads, writes):
        for b in reads:
            b.r.append(ev)
            if len(b.r) > 48:
                d = {}
                for k, v in b.r:
                    if d.get(k, 0) < v:
                        d[k] = v
                b.r = list(d.items())
        for b in writes:
            b.w = ev
            b.r = []

    def op(self, eng, fn, reads=(), writes=()):
        waits = self._deps(eng, reads, writes)
        for k, v in waits.items():
            self.waited[eng][k] = v
        self.cnt[eng] += 1
        val = self.cnt[eng]
        sem = self.sems[eng]
        wl = [(self.sems[k], v) for k, v in waits.items()]
        self.ninst += 1 + len(wl)

        def emit(e):
            for s, v in wl:
                e.wait_ge(s, v)
            fn(e).then_inc(sem, 1)
        self.ops[eng].append(emit)
        ev = (eng, val)
        self._commit(ev, reads, writes)
        return ev

    def dma(self, q, fn, reads=(), writes=(), out=False):
        lst, idx = self.dpool[q]
        k = lst[idx % len(lst)]
        self.dpool[q][1] = idx + 1
        waits = self._deps(q, reads, writes)
        prev = self.cnt[k]
        if prev > 0 and self.waited[q].get(k, 0) < prev and waits.get(k, 0) < prev:
            waits[k] = prev
        for kk, v in waits.items():
            self.waited[q][kk] = v
        self.cnt[k] = prev + 16
        val = self.cnt[k]
        sem = self.sems[k]
        wl = [(self.sems[kk], v) for kk, v in waits.items()]
        self.ninst += 1 + len(wl)

        def emit(e):
            for s, v in wl:
                e.wait_ge(s, v)
            fn(e).then_inc(sem, 16)
        self.ops[q].append(emit)
        ev = (k, val)
        self._commit(ev, reads, writes)
        if out:
            self.out_events.append(ev)
        return ev

    def finish(self):
        waits = {}
        for ev in self.out_events:
            self._need("sp", ev, waits)
        wl = [(self.sems[k], v) for k, v in waits.items()]

        def emit(e):
            for s, v in wl:
                e.wait_ge(s, v)
        self.ops["sp"].append(emit)

    def emit_all(self):
        nc = self.nc
        ops = self.ops
        with nc.Block() as block:
            @block.tensor
            def _(e):
                for f in ops["pe"]:
                    f(e)

            @block.vector
            def _(e):
                for f in ops["dve"]:
                    f(e)

            @block.scalar
            def _(e):
                for f in ops["act"]:
                    f(e)

            @block.gpsimd
            def _(e):
                for f in ops["pool"]:
                    f(e)

            @block.sync
            def _(e):
                for f in ops["sp"]:
                    f(e)


def build_program():
    nc = bass.Bass("TRN2", target_bir_lowering=False)
    es = ExitStack()

    def DI(n, s, dt=F32):
        return nc.dram_tensor(n, list(s), dt, kind="ExternalInput").ap()

    def DO(n, s, dt=F32):
        return nc.dram_tensor(n, list(s), dt, kind="ExternalOutput").ap()

    xp = DI("xp", [SEQ, D]); xsm = DI("xsm", [128, D])
    wst = DI("wst", [4, NU, 128, 4096], F32R)
    pu = DI("pu", [4 * 16384, D], F32R); pv = DI("pv", [4 * 16384, D], F32R)
    keysT = DI("keysT", [4, 128, 16 * 128], F32R)
    ck = DI("ck", [4, NS, 128, 128]); cv = DI("cv", [4, NS, 128, 128]); cvr = DI("cvr", [4, NS, 128, 128], F32R)
    ckd = DI("ckd", [4, NS, 128, 256])
    sh = DI("sh", [4, NS, 4, 128, 128]); sg = DI("sg", [4, NS, 2, 128, 128])
    sinks = DI("sinks", [4, 16]); hnw = DI("hnw", [4, 512]); gnw = DI("gnw", [4, 512])
    lbT = DI("lbT", [128, 16])
    wa2 = DI("wa2", [4, 16, 256], F32R); nbaT = DI("nbaT", [128, 8])
    l1g = DI("l1g", [4, D]); l1b = DI("l1b", [4, D]); l2g = DI("l2g", [4, D]); l2b = DI("l2b", [4, D])
    c_ident = DI("c_ident", [128, 128], F32R); c_perm = DI("c_perm", [128, 128], F32R)
    c_cs = DI("c_cs", [128, 2, SEQ + 128])
    c_band = DI("c_band", [2, 128, 256]); c_caus = DI("c_caus", [128, 128], U32)
    c_reset = DI("c_reset", [128, 256]); c_misc = DI("c_misc", [128, 64])

    y_p = DO("y_p", [SEQ, D]); y_s = DO("y_s", [128, D])
    nkp = DO("nkp", [4, 128, 128]); nvp = DO("nvp", [4, 128, 128])
    nhp = DO("nhp", [4, 4, 128, 128]); ngp = DO("ngp", [4, 2, 128, 128])
    nks = DO("nks", [4, NS, 128, 128]); nvs = DO("nvs", [4, NS, 128, 128])
    nhs = DO("nhs", [4, NS, 4, 128, 128]); ngs = DO("ngs", [4, NS, 2, 128, 128])

    with es:
        fw = Fw(nc, es)

        def S(n, s, dt=F32):
            return es.enter_context(nc.sbuf_tensor(n, list(s), dt))

        def f32(ap):
            return ap.bitcast(F32)

        def mm(out, lhsT, rhs, st, sp, R, W):
            return fw.op("pe", lambda e: e.matmul(out, lhsT, rhs, start=st, stop=sp), R, W)

        def V(fn, R, W):
            return fw.op("dve", fn, R, W)

        def A(fn, R, W):
            return fw.op("act", fn, R, W)

        def G(fn, R, W):
            return fw.op("pool", fn, R, W)

        def act(out, in_, func, R, W, bias=None, scale=None, accum=None):
            kw = {}
            if bias is not None:
                kw["bias"] = bias
            if scale is not None:
                kw["scale"] = scale
            if accum is not None:
                kw["accum_out"] = accum
            return A(lambda e: e.activation(out=out, in_=in_, func=func, **kw), R, W)

        def tsc(out, in0, s1, s2, op0, op1, R, W, eng="dve"):
            if s2 is None:
                return fw.op(eng, lambda e: e.tensor_scalar(out=out, in0=in0, scalar1=s1, scalar2=None, op0=op0), R, W)
            return fw.op(eng, lambda e: e.tensor_scalar(out=out, in0=in0, scalar1=s1, scalar2=s2, op0=op0, op1=op1), R, W)

        def tt(out, in0, in1, op, R, W, eng="dve"):
            return fw.op(eng, lambda e: e.tensor_tensor(out=out, in0=in0, in1=in1, op=op), R, W)

        def stt(out, in0, sc, in1, op0, op1, R, W, accum=None):
            if accum is None:
                return V(lambda e: e.scalar_tensor_tensor(out=out, in0=in0, scalar=sc, in1=in1, op0=op0, op1=op1), R, W)
            return V(lambda e: e.scalar_tensor_tensor(out=out, in0=in0, scalar=sc, in1=in1, op0=op0, op1=op1,
                                                      accum_out=accum), R, W)

        cp_rr = [0]

        def cp(out, in_, R, W):
            cp_rr[0] ^= 1
            if cp_rr[0]:
                return A(lambda e: e.copy(out, in_), R, W)
            return V(lambda e: e.tensor_copy(out, in_), R, W)

        dq_rr = [0]

        def ld(out, in_, W, R=()):
            dq_rr[0] ^= 1
            return fw.dma("sp" if dq_rr[0] else "act", lambda e: e.dma_start(out=out, in_=in_), R, W)

        def ldr(out, in_, W, R=()):
            return fw.dma("pool", lambda e: e.dma_start(out=out, in_=in_), R, W)

        def st(out, in_, R):
            return fw.dma("sp", lambda e: e.dma_start(out=out, in_=in_), R, (), out=True)

        PA = es.enter_context(nc.psum_tensor("PA", [128, 2048], F32))
        PB = es.enter_context(nc.psum_tensor("PB", [128, 2048], F32))
        pbuf = [fw.buf("pb%d" % i) for i in range(8)]
        pb_rr = [0]

        def bank(k):
            t = PA if k < 4 else PB
            return t[:, (k % 4) * 512:(k % 4) * 512 + 512]

        def nb():
            k = pb_rr[0] % 8
            pb_rr[0] += 1
            return bank(k), pbuf[k]

        identR = S("identR", [128, 128], F32R); ident = f32(identR[:]); b_ident = fw.buf()
        permR = S("permR", [128, 128], F32R); b_perm = fw.buf()
        band = S("band", [128, 2, 256]); b_band = fw.buf()
        caus = S("caus", [128, 128], U32); b_caus = fw.buf()
        reset = S("reset", [128, 256]); b_reset = fw.buf()
        misc = S("misc", [128, 64]); b_misc = fw.buf()
        zeros = S("zeros", [128, 128]); b_zeros = fw.buf()
        lbt = S("lbt", [128, 16]); lbw = S("lbw", [128, 16]); lbs = S("lbs", [128, 4]); b_lb = fw.buf()
        lowb = S("lowb", [128, 16]); oml = S("oml", [128, 16])
        nba = S("nba", [128, 8]); b_nba = fw.buf()
        x_g = S("x_g", [128, 2, D]); b_x = [fw.buf(), fw.buf()]
        xT = S("xT", [128, 16, 256], F32R); b_xT = fw.buf()
        NRING = 2
        ring = S("ring", [128, NRING, 4096], F32R); b_ring = [fw.buf() for _ in range(NRING)]
        big = S("big", [128, 4, D], F32R); b_big = [fw.buf() for _ in range(4)]
        lng = S("lng", [128, D]); lnb = S("lnb", [128, D]); b_lng = fw.buf(); b_lnb = fw.buf()
        tm = S("tm", [128, 2, 2176], F32R); b_tm = [fw.buf(), fw.buf()]
        cs = S("cs", [128, 2, 256]); b_cs = fw.buf()
        sink_bc = S("sink_bc", [128, 16]); b_sink = fw.buf()
        nwb = S("nwb", [128, 2, 512]); b_nwb = fw.buf()
        wa2s = S("wa2s", [16, 256], F32R); b_wa2 = fw.buf()
        acT = S("acT", [16, 256], F32R); b_acT = fw.buf()
        kTd = S("kTd", [128, 2, 384], F32R); b_kTd = fw.buf()
        vbuf = S("vbuf", [128, 3, 128], F32R); b_vbuf = fw.buf()
        kcar = S("kcar", [128, 4, 2, 128], F32R); vcar = S("vcar", [128, 4, 128], F32R)
        b_kcar = [fw.buf() for _ in range(4)]; b_vcar = [fw.buf() for _ in range(4)]
        qraw = S("qraw", [128, 256], F32R); b_qraw = fw.buf()
        qrot = S("qrot", [128, 256], F32R); b_qrot = fw.buf()
        t1 = S("t1", [128, 256]); b_t1 = fw.buf()
        ssb = S("ssb", [128, 256]); b_ssb = fw.buf()
        psb = S("psb", [128, 256]); b_psb = fw.buf()
        pT = S("pT", [128, 2, 128], F32R); b_pT = fw.buf()
        sm = S("sm", [128, 16]); b_sm = fw.buf()
        ktok = S("ktok", [128, 2, 128]); b_ktok = fw.buf()
        rt = S("rt", [128, 10, 256]); b_rt = [fw.buf() for _ in range(10)]
        qt = S("qt", [128, 256], F32R); kt = S("kt", [128, 256], F32R); b_qt = fw.buf(); b_kt = fw.buf()
        kh = S("kh", [128, 2, 128], F32R); b_kh = fw.buf()
        attm = S("attm", [128, 128], F32R); b_attm = fw.buf()
        atf = S("atf", [128, 128]); b_atf = fw.buf()
        Ssc = S("Ssc", [128, 128], F32R); b_Ssc = fw.buf()
        cols = S("cols", [128, 8]); b_cols = fw.buf()
        Sb = S("Sb", [128, 4, 4, 128]); b_Sb = [[fw.buf() for _ in range(4)] for _ in range(4)]
        Sc = S("Sc", [128, 4, 2, 128]); b_Sc = [[fw.buf() for _ in range(2)] for _ in range(4)]
        b_ob = [[b_big[0], b_big[1]], [b_big[0], b_big[1]]]

        def obv(mxr, t):
            return big[:, t, 1024 + mxr * 512:1536 + mxr * 512]
        sS = rt[:, 5, 0:128]; b_sS = b_rt[5]
        sSn = rt[:, 6, 0:128]; b_sSn = b_rt[6]
        sSr = big[:, 3, 1280:1408]; b_sSr = b_big[3]
        skd = rt[:, 7, :]; b_skd = b_rt[7]
        b_skT = b_big[2]
        b_sv = b_big[1]
        sv = big[:, 1, 0:NS * 128].rearrange("p (b n) -> p b n", b=NS)

        def skTv(b, kv):
            lo = (b % 4) * 260 + kv * 130
            return big[:, 2 + b // 4, lo:lo + 130]
        slh = S("slh", [128, 16], F32R); b_slh = fw.buf()
        osel = S("osel", [16, 64]); b_osel = fw.buf()
        pq = S("pq", [128, 256], F32R); b_pq = fw.buf()
        tv = S("tv", [128, 16, 16]); ti = S("ti", [128, 16, 16], U32); b_tv = fw.buf(); b_ti = fw.buf()
        tif = S("tif", [128, 16, 16]); b_tif = fw.buf()
        bv = S("bv", [128, 8, 16]); bi = S("bi", [128, 8, 16], U32); b_bv = fw.buf(); b_bi = fw.buf()
        bt = S("bt", [128, 6, 128]); bti = S("bti", [128, 2, 128], U32); b_bt = fw.buf()
        idxf = S("idxf", [128, 128]); gate = S("gate", [128, 2, 128]); b_idxf = fw.buf(); b_gate = fw.buf()
        idxT = S("idxT", [128, 2, 128], I32); gateT = S("gateT", [128, 2, 128]); b_idxT = fw.buf(); b_gateT = fw.buf()
        hT = S("hT", [128, 128]); wT = S("wT", [128, 128]); b_hT = fw.buf(); b_wT = fw.buf()
        wz = S("wz", [128, 4, 255], F32R); b_wz = [fw.buf() for _ in range(4)]
        qz = wz; b_qz = b_wz
        lnst = S("lnst", [128, 8]); b_lnst = fw.buf()

        ldr(identR[:], c_ident, [b_ident]); ldr(permR[:], c_perm, [b_perm])
        ld(band[:], c_band.rearrange("a p k -> p a k"), [b_band]); ld(caus[:], c_caus, [b_caus])
        ld(reset[:], c_reset, [b_reset]); ld(misc[:], c_misc, [b_misc])
        ld(lbt[:], lbT, [b_lb]); ld(nba[:], nbaT, [b_nba])
        G(lambda e: e.memset(zeros[:], 0.0), (), [b_zeros])
        V(lambda e: e.tensor_copy(wz[:].rearrange("p a b -> p (a b)"), zeros[:, 0:1].to_broadcast([128, 4 * 255])), [b_zeros], b_wz)
        G(lambda e: e.memset(Sb[:], 0.0), (), [b for r in b_Sb for b in r])
        G(lambda e: e.memset(Sc[:], 0.0), (), [b for r in b_Sc for b in r])
        V(lambda e: e.tensor_copy(kcar[:].rearrange("p a b c -> p (a b c)"), zeros[:, 0:1].to_broadcast([128, 1024])), [b_zeros], b_kcar)
        V(lambda e: e.tensor_copy(vcar[:].rearrange("p a b -> p (a b)"), zeros[:, 0:1].to_broadcast([128, 512])), [b_zeros], b_vcar)
        iota16 = misc[:, 0:16]
        hmask = misc[:, 16:18]
        kvs = misc[:, 18:20]
        tsc(nba[:], nba[:], -1.0, None, ALU.mult, None, [b_nba], [b_nba])
        act(lbw[:], lbt[:], AF.Exp, [b_lb], [b_lb])
        V(lambda e: e.reduce_sum(out=lbs[:], in_=lbw[:].rearrange("p (h l) -> p h l", l=4), axis=AX.X), [b_lb], [b_lb])
        V(lambda e: e.reciprocal(out=lbs[:], in_=lbs[:]), [b_lb], [b_lb])
        tt(lbw[:].rearrange("p (h l) -> p h l", l=4), lbw[:].rearrange("p (h l) -> p h l", l=4),
           lbs[:].unsqueeze(2).to_broadcast([128, 4, 4]), ALU.mult, [b_lb], [b_lb])
        lw3 = lbw[:].rearrange("p (h l) -> p h l", l=4)
        lo3 = lowb[:].rearrange("p (h l) -> p h l", l=4)
        G(lambda e: e.memset(lowb[:], 0.0), (), [b_lb])
        for l in range(1, 4):
            tt(lo3[:, :, l:l + 1], lo3[:, :, l - 1:l], lw3[:, :, l:l + 1], ALU.add, [b_lb], [b_lb])
        tsc(oml[:], lowb[:], -1.0, 1.0, ALU.mult, ALU.add, [b_lb], [b_lb])

        for l in range(DEPTH):
            for b in range(NS):
                fw.dma("act", lambda e, l=l, b=b: e.dma_start(out=nks[l, b, 0:127, :], in_=ck[l, b, 1:128, :]), (), (), out=True)
                fw.dma("act", lambda e, l=l, b=b: e.dma_start(out=nvs[l, b, 0:127, :], in_=cv[l, b, 1:128, :]), (), (), out=True)

        ucount = [0]

        def load_unit(l, u):
            slot = ucount[0] % NRING
            ucount[0] += 1
            ldr(ring[:, slot, :], wst[l, u], [b_ring[slot]])
            return ring[:, slot, :].rearrange("p (c n) -> p c n", n=256), b_ring[slot]

        def transpose_to_xT(src_f32, bsrc, T):
            for t in range(T):
                for c4 in range(4):
                    pk, pkb = nb()
                    for cc in range(4):
                        c = c4 * 4 + cc
                        fw.op("pe", lambda e, t=t, c=c, cc=cc, pk=pk: e.transpose(pk[:, cc * 128:(cc + 1) * 128],
                                                                               src_f32(t)[:, c * 128:(c + 1) * 128], ident),
                              [bsrc[t], b_ident], [pkb])
                    cp(xT[:, c4 * 4:(c4 + 1) * 4, t * 128:(t + 1) * 128],
                       pk.rearrange("p (c n) -> p c n", n=128), [pkb], [b_xT])

        def fm_block(wv, wb, blk, N, ncols=128):
            pk, pkb = nb()
            for c in range(16):
                mm(pk[0:ncols, 0:N], wv[:, c, blk * 128:blk * 128 + ncols], xT[:, c, 0:N], c == 0, c == 15,
                   [wb, b_xT], [pkb])
            return pk, pkb

        def layer_norm(t, gl, bl, l):
            xs = x_g[:, t, :]
            bx = b_x[t]
            V(lambda e: e.memset(lnst[:, 0:2], 0.0), (), [b_lnst])
            V(lambda e: e.reduce_sum(out=lnst[:, 0:1], in_=xs, axis=AX.X), [bx], [b_lnst])
            act(big[:, 3, :], xs, AF.Square, [bx], [b_big[3], b_lnst], accum=lnst[:, 1:2])
            tsc(lnst[:, 2:3], lnst[:, 0:1], 1.0 / D, None, ALU.mult, None, [b_lnst], [b_lnst])
            tt(lnst[:, 3:4], lnst[:, 2:3], lnst[:, 2:3], ALU.mult, [b_lnst], [b_lnst])
            stt(lnst[:, 4:5], lnst[:, 1:2], 1.0 / D, lnst[:, 3:4], ALU.mult, ALU.subtract, [b_lnst], [b_lnst])
            tsc(lnst[:, 5:6], lnst[:, 4:5], 1e-5, None, ALU.add, None, [b_lnst], [b_lnst])
            act(lnst[:, 5:6], lnst[:, 5:6], AF.Sqrt, [b_lnst], [b_lnst])
            V(lambda e: e.reciprocal(out=lnst[:, 5:6], in_=lnst[:, 5:6]), [b_lnst], [b_lnst])
            tsc(xs, xs, lnst[:, 2:3], lnst[:, 5:6], ALU.subtract, ALU.mult, [bx, b_lnst], [bx])
            tt(xs, xs, lng[:], ALU.mult, [bx, b_lng], [bx])
            tt(xs, xs, lnb[:], ALU.add, [bx, b_lnb], [bx])

        def layer_group(l, T, sample, g):
            N = T * 128
            ld(sink_bc[:], sinks[l:l + 1, :].partition_broadcast(128), [b_sink])
            ld(nwb[:, 0, :], hnw[l:l + 1, :].partition_broadcast(128), [b_nwb])
            ld(nwb[:, 1, :], gnw[l:l + 1, :].partition_broadcast(128), [b_nwb])
            ldr(wa2s[:], wa2[l], [b_wa2])
            ld(lng[:], l1g[l:l + 1, :].partition_broadcast(128), [b_lng])
            ld(lnb[:], l1b[l:l + 1, :].partition_broadcast(128), [b_lnb])
            transpose_to_xT(lambda t: x_g[:, t, :], b_x, T)
            for u in range(9):
                wv, wb = load_unit(l, U_TM + u)
                for t in range(T):
                    if u < 8:
                        pk, pkb = nb()
                        for c in range(16):
                            mm(pk[:, 0:256], xT[:, c, t * 128:(t + 1) * 128], wv[:, c, :], c == 0, c == 15, [b_xT, wb], [pkb])
                        cp(tm[:, t, u * 256:(u + 1) * 256], pk[:, 0:256], [pkb], [b_tm[t]])
                    else:
                        pk, pkb = nb()
                        for c in range(16):
                            mm(pk[:, 0:128], xT[:, c, t * 128:(t + 1) * 128], wv[:, c, 0:128], c == 0, c == 15, [b_xT, wb], [pkb])
                        cp(tm[:, t, 2048:2176], pk[:, 0:128], [pkb], [b_tm[t]])
                if u == 8:
                    pk, pkb = fm_block(wv, wb, 1, N, 16)
                    cp(acT[:, 0:N], pk[0:16, 0:N], [pkb], [b_acT])
            for t in range(T):
                for lo in (640, 1664):
                    act(tm[:, t, lo:lo + 512], f32(tm[:, t, lo:lo + 512]), AF.Silu, [b_tm[t]], [b_tm[t]])
            wv, wb = load_unit(l, U_FM + 0)
            if not sample:
                V(lambda e: e.tensor_copy(kTd[:, :, 0:128], kcar[:, l, :, :]), [b_kcar[l]], [b_kTd])
                V(lambda e: e.tensor_copy(vbuf[:, 0, :], vcar[:, l, :]), [b_vcar[l]], [b_vbuf])
                for t in range(T):
                    cp(vbuf[:, 1 + t, :], f32(tm[:, t, 0:128]), [b_tm[t]], [b_vbuf])
            for kv in range(2):
                pk, pkb = fm_block(wv, wb, kv, N)
                cp(qraw[:, 0:N], pk[:, 0:N], [pkb], [b_qraw])
                p2, p2b = nb()
                mm(p2[:, 0:N], permR[:], qraw[:, 0:N], True, True, [b_perm, b_qraw], [p2b])
                tt(t1[:, 0:N], f32(qraw[:, 0:N]), cs[:, 0, 0:N], ALU.mult, [b_qraw, b_cs], [b_t1])
                tt(qrot[:, 0:N], p2[:, 0:N], cs[:, 1, 0:N], ALU.mult, [p2b, b_cs], [b_qrot])
                tt(kTd[:, kv, 128:128 + N], t1[:, 0:N], f32(qrot[:, 0:N]), ALU.add, [b_t1, b_qrot], [b_kTd])
            for kv in range(2):
                pk, pkb = nb()
                fw.op("pe", lambda e, kv=kv, pk=pk: e.transpose(pk[:, 0:128], f32(kTd[:, kv, 128 + (T - 1) * 128:128 + T * 128]), ident),
                      [b_kTd, b_ident], [pkb])
                cp(ktok[:, kv, 0:64], pk[:, 0:64], [pkb], [b_ktok])
            if not sample:
                if g == NG - 1:
                    st(nkp[l].rearrange("p (k d) -> p k d", d=64), ktok[:, :, 0:64], [b_ktok])
                    st(nvp[l], f32(tm[:, T - 1, 0:128]), [b_tm[T - 1]])
                V(lambda e: e.tensor_copy(kcar[:, l, :, :], kTd[:, :, 128 + (T - 1) * 128:128 + T * 128]), [b_kTd], [b_kcar[l]])
                V(lambda e: e.tensor_copy(vcar[:, l, :], tm[:, T - 1, 0:128]), [b_tm[T - 1]], [b_vcar[l]])
            else:
                for b in range(NS):
                    ld(skd[:], ckd[l, b], [b_skd])
                    ldr(sv[:, b, :], cvr[l, b], [b_sv])
                    for kv in range(2):
                        pk, pkb = nb()
                        fw.op("pe", lambda e, pk=pk, kv=kv: e.transpose(pk[:, 0:128], skd[:, kv * 128:(kv + 1) * 128], ident), [b_skd, b_ident], [pkb])
                        cp(skTv(b, kv)[:, 0:128], pk[:, 0:128], [pkb], [b_big[2 + b // 4]])
                        cp(skTv(b, kv)[:, 128:129], f32(kTd[:, kv, 128 + b:129 + b]), [b_kTd], [b_big[2 + b // 4]])
                        cp(skTv(b, kv)[:, 129:130], f32(kTd[:, kv, 128 + b:129 + b]), [b_kTd], [b_big[2 + b // 4]])
                for b in range(NS):
                    st(nks[l, b, 127:128, :].rearrange("p (k d) -> p k d", d=64), ktok[b:b + 1, :, 0:64], [b_ktok])
                    st(nvs[l, b, 127:128, :], f32(tm[b:b + 1, 0, 0:128]), [b_tm[0]])
            for j in range(8):
                if j % 2 == 0:
                    wv, wb = load_unit(l, U_FM + 1 + j // 2)
                pk, pkb = fm_block(wv, wb, j % 2, N)
                cp(qraw[:, 0:N], pk[:, 0:N], [pkb], [b_qraw])
                p2, p2b = nb()
                mm(p2[:, 0:N], permR[:], qraw[:, 0:N], True, True, [b_perm, b_qraw], [p2b])
                tt(t1[:, 0:N], f32(qraw[:, 0:N]), cs[:, 0, 0:N], ALU.mult, [b_qraw, b_cs], [b_t1])
                tt(ssb[:, 0:N], p2[:, 0:N], cs[:, 1, 0:N], ALU.mult, [p2b, b_cs], [b_ssb])
                tt(qrot[:, 0:N], t1[:, 0:N], ssb[:, 0:N], ALU.add, [b_t1, b_ssb], [b_qrot])
                kv = j // 4
                if not sample:
                    for t in range(T):
                        for hh in range(2):
                            h = 2 * j + hh
                            ps = slice(hh * 64, hh * 64 + 64)
                            pk, pkb = nb()
                            mm(pk[:, 0:256], qrot[ps, t * 128:(t + 1) * 128], kTd[ps, kv, t * 128:t * 128 + 256], True, True,
                               [b_qrot, b_kTd], [pkb])
                            mi = 0 if (g == 0 and t == 0) else 1
                            stt(ssb[:], pk[:, 0:256], 0.125, band[:, mi, :], ALU.mult, ALU.add, [pkb, b_band], [b_ssb])
                            V(lambda e: e.reduce_max(out=sm[:, 0:1], in_=ssb[:], axis=AX.X), [b_ssb], [b_sm])
                            tsc(sm[:, 1:2], sm[:, 0:1], sink_bc[:, h:h + 1], -1.0, ALU.max, ALU.mult, [b_sm, b_sink], [b_sm])
                            V(lambda e: e.memset(sm[:, 2:3], 0.0), (), [b_sm])
                            act(psb[:], ssb[:], AF.Exp, [b_ssb, b_sm], [b_psb, b_sm], bias=sm[:, 1:2], accum=sm[:, 2:3])
                            act(sm[:, 3:4], sink_bc[:, h:h + 1], AF.Exp, [b_sink, b_sm], [b_sm], bias=sm[:, 1:2])
                            tt(sm[:, 4:5], sm[:, 2:3], sm[:, 3:4], ALU.add, [b_sm], [b_sm])
                            V(lambda e: e.reciprocal(out=sm[:, 5:6], in_=sm[:, 4:5]), [b_sm], [b_sm])
                            p3, p3b = nb()
                            for blk in range(2):
                                fw.op("pe", lambda e, blk=blk, p3=p3: e.transpose(p3[:, blk * 128:(blk + 1) * 128],
                                                                                  psb[:, blk * 128:(blk + 1) * 128], ident),
                                      [b_psb, b_ident], [p3b])
                            cp(pT[:], p3[:, 0:256].rearrange("p (b n) -> p b n", n=128), [p3b], [b_pT])
                            p4, p4b = nb()
                            for blk in range(2):
                                mm(p4[:, 0:64], pT[:, blk, :], vbuf[:, t + blk, kv * 64:(kv + 1) * 64], blk == 0, blk == 1,
                                   [b_pT, b_vbuf], [p4b])
                            tsc(big[:, t, h * 64:(h + 1) * 64], p4[:, 0:64], sm[:, 5:6], None, ALU.mult, None,
                                [p4b, b_sm], [b_big[t]])
                else:
                    sample_attn_block(l, j)
            for h in range(4):
                wv, wb = load_unit(l, U_FM + 5 + h)
                pq_, pqb = fm_block(wv, wb, 0, N)
                pf_, pfb = fm_block(wv, wb, 1, N)
                R = rt
                act(R[:, 0, 0:N], pq_[:, 0:N], AF.Silu, [pqb], [b_rt[0]])
                act(R[:, 1, 0:N], pf_[:, 0:N], AF.Sigmoid, [pfb], [b_rt[1]])
                tsc(R[:, 1, 0:N], R[:, 1, 0:N], oml[:, h * 4 + l:h * 4 + l + 1], lowb[:, h * 4 + l:h * 4 + l + 1],
                    ALU.mult, ALU.add, [b_rt[1], b_lb], [b_rt[1]])
                tsc(R[:, 2, 0:N], R[:, 1, 0:N], -1.0, 1.0, ALU.mult, ALU.add, [b_rt[1]], [b_rt[2]])
                if sample:
                    sample_rec(l, h, None, R[:, 0, :], b_rt[0], R[:, 1, :], b_rt[1], R[:, 2, :], b_rt[2], 640 - 512)
                else:
                    tsc(R[:, 3, 0:N], R[:, 1, 0:N], 1e-30, None, ALU.max, None, [b_rt[1]], [b_rt[3]])
                    act(R[:, 3, 0:N], R[:, 3, 0:N], AF.Ln, [b_rt[3]], [b_rt[3]])
                    chunk_rec(l, N, T, R[:, 0, :], b_rt[0], R[:, 2, :], b_rt[2], 3,
                              [(0, 128, Sb[:, l, h, :], b_Sb[l][h], 128 + h * 128, 0, h)], 1.0)
            for j in range(2):
                wv, wb = load_unit(l, U_FM + 9 + j)
                pq_, pqb = fm_block(wv, wb, 0, N)
                pk_, pkb_ = fm_block(wv, wb, 1, N)
                R = rt
                pz, pzb = nb()
                mm(pz[:, 0:N], wa2s[:, j * 128:(j + 1) * 128], acT[:, 0:N], True, True, [b_wa2, b_acT], [pzb])
                act(R[:, 3, 0:N], pz[:, 0:N], AF.Exp, [pzb, b_nba], [b_rt[3]], bias=nba[:, l * 2 + j:l * 2 + j + 1], scale=-1.0)
                act(R[:, 3, 0:N], R[:, 3, 0:N], AF.Ln, [b_rt[3]], [b_rt[3]], bias=1.0)
                tsc(R[:, 3, 0:N], R[:, 3, 0:N], -1.0 / 16.0, None, ALU.mult, None, [b_rt[3]], [b_rt[3]])
                tsc(R[:, 0, 0:N], pq_[:, 0:N], 0.125, None, ALU.mult, None, [pqb], [b_rt[0]])
                cp(R[:, 2, 0:N], pk_[:, 0:N], [pkb_], [b_rt[2]])
                if sample:
                    act(R[:, 1, 0:N], R[:, 3, 0:N], AF.Exp, [b_rt[3]], [b_rt[1]])
                    sample_rec(l, None, j, R[:, 0, :], b_rt[0], R[:, 1, :], b_rt[1], R[:, 2, :], b_rt[2], 1152)
                else:
                    chunk_rec(l, N, T, R[:, 0, :], b_rt[0], R[:, 2, :], b_rt[2], 3,
                              [(0, 64, Sc[0:64, l, j, :], b_Sc[l][j], 1152 + (2 * j) * 128, 1, 2 * j),
                               (64, 64, Sc[64:128, l, j, :], b_Sc[l][j], 1152 + (2 * j + 1) * 128, 1, 2 * j + 1)], 1.0)
            for t in range(T):
                for mxr in range(2):
                    ovw = obv(mxr, t)
                    o3 = ovw.rearrange("p (h d) -> p h d", d=128)
                    bo = b_ob[mxr][t]
                    rsq = rt[:, 4:6, :].rearrange("p a n -> p (a n)")
                    tt(rsq, f32(ovw), f32(ovw), ALU.mult, [bo], [b_rt[4], b_rt[5]])
                    V(lambda e, rsq=rsq: e.reduce_sum(out=sm[:, 8:12], in_=rsq.rearrange("p (h d) -> p h d", d=128), axis=AX.X),
                      [b_rt[4], b_rt[5]], [b_sm])
                    tsc(sm[:, 8:12], sm[:, 8:12], 1.0 / 128.0, 1e-6, ALU.mult, ALU.add, [b_sm], [b_sm])
                    act(sm[:, 8:12], sm[:, 8:12], AF.Sqrt, [b_sm], [b_sm])
                    V(lambda e: e.reciprocal(out=sm[:, 8:12], in_=sm[:, 8:12]), [b_sm], [b_sm])
                    tt(o3, f32(o3), sm[:, 8:12].unsqueeze(2).to_broadcast([128, 4, 128]), ALU.mult, [bo, b_sm], [bo])
                    tt(ovw, f32(ovw), nwb[:, mxr, :], ALU.mult, [bo, b_nwb], [bo])
                    glo = 640 if mxr == 0 else 1664
                    tt(ovw, f32(ovw), f32(tm[:, t, glo:glo + 512]), ALU.mult, [bo, b_tm[t]], [bo])
            transpose_to_xT(lambda t: f32(big[:, t, :]), b_big, T)
            for u in range(8):
                wv, wb = load_unit(l, U_WO + u)
                for t in range(T):
                    pk, pkb = nb()
                    for c in range(16):
                        mm(pk[:, 0:256], xT[:, c, t * 128:(t + 1) * 128], wv[:, c, :], c == 0, c == 15, [b_xT, wb], [pkb])
                    stt(x_g[:, t, u * 256:(u + 1) * 256], x_g[:, t, u * 256:(u + 1) * 256], ALPHA, pk[:, 0:256], ALU.mult, ALU.add,
                        [b_x[t], pkb], [b_x[t]])
            for t in range(T):
                layer_norm(t, lng, lnb, l)
            transpose_to_xT(lambda t: x_g[:, t, :], b_x, T)
            ldr(ring[:, NRING - 1, 0:2048], keysT[l], [b_ring[NRING - 1]])
            peer(l, T, sample)

        def chunk_rec(l, N, T, qv, bq, kv_, bk, lai, heads, _):
            R = rt
            la = R[:, lai, :]
            bla = b_rt[lai]
            bc = R[:, 4, :]; bbc = b_rt[4]
            V(lambda e: e.tensor_tensor_scan(out=bc[:, 0:N], data0=reset[:, 0:N], data1=la[:, 0:N], initial=0.0,
                                             op0=ALU.mult, op1=ALU.add), [b_reset, bla], [bbc])
            for t in range(T):
                lo = t * 128
                tsc(R[:, 5, lo:lo + 128], bc[:, lo:lo + 128], bc[:, lo + 64:lo + 65], None, ALU.subtract, None, [bbc], [b_rt[5]])
                act(R[:, 6, lo:lo + 128], bc[:, lo:lo + 128], AF.Exp, [bbc], [b_rt[6]], bias=bc[:, lo + 127:lo + 128], scale=-1.0)
                act(cols[:, 2 * t:2 * t + 1], bc[:, lo + 64:lo + 65], AF.Exp, [bbc], [b_cols])
                act(cols[:, 2 * t + 1:2 * t + 2], bc[:, lo + 127:lo + 128], AF.Exp, [bbc], [b_cols])
            act(R[:, 7, 0:N], R[:, 5, 0:N], AF.Exp, [b_rt[5]], [b_rt[7]])
            act(R[:, 8, 0:N], R[:, 5, 0:N], AF.Exp, [b_rt[5]], [b_rt[8]], scale=-1.0)
            tt(qt[:, 0:N], qv[:, 0:N], R[:, 7, 0:N], ALU.mult, [bq, b_rt[7]], [b_qt])
            tt(kt[:, 0:N], kv_[:, 0:N], R[:, 8, 0:N], ALU.mult, [bk, b_rt[8]], [b_kt])
            tt(R[:, 9, 0:N], kv_[:, 0:N], R[:, 6, 0:N], ALU.mult, [bk, b_rt[6]], [b_rt[9]])
            for t in range(T):
                lo = t * 128
                pk, pkb = nb()
                fw.op("pe", lambda e, pk=pk, lo=lo: e.transpose(pk[:, 0:128], R[:, 9, lo:lo + 128], ident), [b_rt[9], b_ident], [pkb])
                cp(kh[:, t, :], pk[:, 0:128], [pkb], [b_kh])
            for t in range(T):
                lo = t * 128
                for (p0, pn, Sap, Sbuf_, vlo, mxr, hidx) in heads:
                    ps = slice(p0, p0 + pn)
                    tsc(Ssc[ps, :], Sap, cols[ps, 2 * t:2 * t + 1], None, ALU.mult, None, [Sbuf_, b_cols], [b_Ssc])
                    pk, pkb = nb()
                    mm(pk[:, 0:128], kt[ps, lo:lo + 128], qt[ps, lo:lo + 128], True, True, [b_kt, b_qt], [pkb])
                    V(lambda e, pk=pk: e.select(out=atf[:], mask=caus[:], on_true=pk[:, 0:128], on_false=zeros[:]),
                      [pkb, b_caus, b_zeros], [b_atf])
                    cp(attm[:], atf[:], [b_atf], [b_attm])
                    p2, p2b = nb()
                    mm(p2[:, 0:128], qt[ps, lo:lo + 128], Ssc[ps, :], True, False, [b_qt, b_Ssc], [p2b])
                    mm(p2[:, 0:128], attm[:], tm[:, t, vlo:vlo + 128], False, True, [b_attm, b_tm[t]], [p2b])
                    cp(obv(mxr, t)[:, (hidx % 4) * 128:(hidx % 4) * 128 + 128], p2[:, 0:128], [p2b], [b_ob[mxr][t]])
                    p3, p3b = nb()
                    mm(p3[:, 0:128], kh[:, t, :], tm[:, t, vlo:vlo + 128], True, True, [b_kh, b_tm[t]], [p3b])
                    stt(Sap, Sap, cols[ps, 2 * t + 1:2 * t + 2], p3[ps, 0:128], ALU.mult, ALU.add, [Sbuf_, b_cols, p3b], [Sbuf_])

        def sample_rec(l, h, j, qv, bq, av, ba, kv_, bk, _unused):
            hg = h is not None
            vlo = 128 if hg else 1152
            accs = []
            nh = 1 if hg else 2
            for hh in range(nh):
                accs.append((bank(6 + hh), pbuf[6 + hh]))
            for b in range(NS):
                src = sh[l, b, h] if hg else sg[l, b, j]
                ld(sS[:], src, [b_sS])
                pv_, pvb = bank(b % 6), pbuf[b % 6]
                mm(pv_[:, 0:512], identR[:, b:b + 1].to_broadcast([128, 128]), tm[:, 0, vlo:vlo + 512], True, True,
                   [b_ident, b_tm[0]], [pvb])
                tsc(sSn[:], sS[:], av[:, b:b + 1], None, ALU.mult, None, [b_sS, ba], [b_sSn])
                for hh in range(nh):
                    ps = slice(0, 128) if hg else slice(hh * 64, hh * 64 + 64)
                    hd = h if hg else 2 * j + hh
                    stt(sSn[ps, :], pv_[ps, hd * 128:(hd + 1) * 128], kv_[ps, b:b + 1], sSn[ps, :], ALU.mult, ALU.add,
                        [pvb, bk, b_sSn], [b_sSn])
                dst = nhs[l, b, h] if hg else ngs[l, b, j]
                st(dst, sSn[:], [b_sSn])
                cp(sSr[:], sSn[:], [b_sSn], [b_sSr])
                zi = b % 4
                cp(qz[:, zi, 127:128], qv[:, b:b + 1], [bq], [b_qz[zi]])
                for hh in range(nh):
                    ps = slice(0, 128) if hg else slice(hh * 64, hh * 64 + 64)
                    pk, pkb = accs[hh]
                    mm(pk[:, 0:128], qz[ps, zi, 127 - b:255 - b], sSr[ps, :], b == 0, b == NS - 1, [b_qz[zi], b_sSr], [pkb])
            for hh in range(nh):
                hd = h if hg else 2 * j + hh
                pk, pkb = accs[hh]
                cp(obv(0 if hg else 1, 0)[:, (hd % 4) * 128:(hd % 4) * 128 + 128], pk[:, 0:128], [pkb], [b_ob[0 if hg else 1][0]])

        def sample_attn_block(l, j):
            kv = j // 4
            for b in range(NS):
                tt(slh[:, 0:2], f32(qrot[:, b:b + 1]).to_broadcast([128, 2]), hmask, ALU.mult, [b_qrot, b_misc], [b_slh])
                p1, p1b = nb()
                mm(p1[0:2, 0:130], slh[:, 0:2], skTv(b, kv), True, True, [b_slh, b_big[2 + b // 4]], [p1b])
                tsc(ssb[0:2, 0:129], p1[0:2, 0:129], 0.125, None, ALU.mult, None, [p1b], [b_ssb])
                V(lambda e: e.reduce_max(out=sm[0:2, 0:1], in_=ssb[0:2, 0:129], axis=AX.X), [b_ssb], [b_sm])
                tt(sm[0:2, 6:8], sink_bc[0:2, 2 * j:2 * j + 2], misc[0:2, 20:22], ALU.mult, [b_sink, b_misc], [b_sm])
                V(lambda e: e.reduce_sum(out=sm[0:2, 7:8], in_=sm[0:2, 6:8], axis=AX.X), [b_sm], [b_sm])
                tsc(sm[0:2, 1:2], sm[0:2, 0:1], sm[0:2, 7:8], -1.0, ALU.max, ALU.mult, [b_sm], [b_sm])
                V(lambda e: e.memset(sm[0:2, 2:3], 0.0), (), [b_sm])
                act(psb[0:2, 0:129], ssb[0:2, 0:129], AF.Exp, [b_ssb, b_sm], [b_psb, b_sm], bias=sm[0:2, 1:2], accum=sm[0:2, 2:3])
                act(sm[0:2, 3:4], sm[0:2, 7:8], AF.Exp, [b_sm], [b_sm], bias=sm[0:2, 1:2])
                tt(sm[0:2, 4:5], sm[0:2, 2:3], sm[0:2, 3:4], ALU.add, [b_sm], [b_sm])
                V(lambda e: e.reciprocal(out=sm[0:2, 5:6], in_=sm[0:2, 4:5]), [b_sm], [b_sm])
                p3, p3b = nb()
                fw.op("pe", lambda e, p3=p3: e.transpose(p3[:, 0:2], psb[0:2, 0:128], ident[0:2, 0:2]), [b_psb, b_ident], [p3b])
                cp(pT[:, 0, 0:2], p3[:, 0:2], [p3b], [b_pT])
                p4, p4b = nb()
                mm(p4[0:2, 0:128], pT[:, 0, 0:2], sv[:, b, :], True, True, [b_pT, b_sv], [p4b])
                p5, p5b = nb()
                mm(p5[0:2, 0:128], identR[:, b:b + 1].to_broadcast([128, 2]), tm[:, 0, 0:128], True, True, [b_ident, b_tm[0]], [p5b])
                cp(ssb[0:2, 0:128], p4[0:2, 0:128], [p4b], [b_ssb])
                stt(t1[0:2, 0:128], p5[0:2, 0:128], psb[0:2, 128:129], ssb[0:2, 0:128], ALU.mult, ALU.add, [p5b, b_ssb, b_psb], [b_t1])
                tsc(osel[0:2, :], t1[0:2, kv * 64:(kv + 1) * 64], sm[0:2, 5:6], None, ALU.mult, None, [b_t1, b_sm], [b_osel])
                for hh in range(2):
                    fw.dma("pool", lambda e, b=b, j=j, hh=hh: e.dma_start(out=big[b:b + 1, 0, (2 * j + hh) * 64:(2 * j + hh + 1) * 64],
                                                                      in_=osel[hh:hh + 1, :]), [b_osel], [b_big[0]])

        def peer(l, T, sample):
            keyv = ring[:, NRING - 1, 0:2048].rearrange("p (g n) -> p g n", n=128)
            bkey = b_ring[NRING - 1]
            for u in range(8):
                slot = u % (NRING - 1)
                ldr(ring[:, slot, :], wst[l, U_WQ + u], [b_ring[slot]])
                wv = ring[:, slot, :].rearrange("p (c n) -> p c n", n=256)
                for blk in range(2):
                    hp = u * 2 + blk
                    pk, pkb = fm_block(wv, b_ring[slot], blk, T * 128)
                    cp(pq[:, 0:T * 128], pk[:, 0:T * 128], [pkb], [b_pq])
                    for t in range(T):
                        p2, p2b = nb()
                        mm(p2[:, 0:128], pq[:, t * 128:(t + 1) * 128], keyv[:, hp, :], True, True, [b_pq, bkey], [p2b])
                        cp(big[:, t, hp * 128:(hp + 1) * 128], p2[:, 0:128], [p2b], [b_big[t]])
            for t in range(T):
                sc = f32(big[:, t, :]); sc2 = lnb[:]; sc2w = lnb[:]; cand = lng[:]
                bsc = b_big[t]
                sc3 = sc.rearrange("p (g n) -> p g n", n=128)
                s23 = sc2.rearrange("p (g n) -> p g n", n=128)
                s23w = sc2w.rearrange("p (g n) -> p g n", n=128)
                for hp in range(16):
                    V(lambda e, hp=hp, sc3=sc3: e.max(out=tv[:, hp, 0:8], in_=sc3[:, hp, :]), [bsc], [b_tv])
                    V(lambda e, hp=hp, sc3=sc3: e.max_index(out=ti[:, hp, 0:8], in_max=tv[:, hp, 0:8], in_values=sc3[:, hp, :]), [bsc, b_tv], [b_ti])
                    V(lambda e, hp=hp, sc3=sc3, s23=s23w: e.match_replace(out=s23[:, hp, :], in_to_replace=tv[:, hp, 0:8], in_values=sc3[:, hp, :], imm_value=NEG),
                      [bsc, b_tv], [b_lnb])
                    V(lambda e, hp=hp, s23=s23: e.max(out=tv[:, hp, 8:16], in_=s23[:, hp, :]), [b_lnb], [b_tv])
                    V(lambda e, hp=hp, s23=s23: e.max_index(out=ti[:, hp, 8:16], in_max=tv[:, hp, 8:16], in_values=s23[:, hp, :]), [b_lnb, b_tv], [b_ti])
                V(lambda e: e.tensor_copy(tif[:], ti[:]), [b_ti], [b_tif])
                tv4 = tv[:].rearrange("p (h two) k -> p h two k", two=2)
                tif4 = tif[:].rearrange("p (h two) k -> p h two k", two=2)
                c4 = cand.rearrange("p (h i j) -> p h i j", i=16, j=16)
                tt(c4, tv4[:, :, 0, :].unsqueeze(3).to_broadcast([128, 8, 16, 16]),
                   tv4[:, :, 1, :].unsqueeze(2).to_broadcast([128, 8, 16, 16]), ALU.add, [b_tv], [b_lng])
                c3 = cand.rearrange("p (h n) -> p h n", n=256)
                o3 = sc2.rearrange("p (h n) -> p h n", n=256)
                o3w = sc2w.rearrange("p (h n) -> p h n", n=256)
                for h in range(8):
                    V(lambda e, h=h, c3=c3: e.max(out=bv[:, h, 0:8], in_=c3[:, h, :]), [b_lng], [b_bv])
                    V(lambda e, h=h, c3=c3: e.max_index(out=bi[:, h, 0:8], in_max=bv[:, h, 0:8], in_values=c3[:, h, :]), [b_lng, b_bv], [b_bi])
                    V(lambda e, h=h, c3=c3, o3=o3w: e.match_replace(out=o3[:, h, :], in_to_replace=bv[:, h, 0:8], in_values=c3[:, h, :], imm_value=NEG),
                      [b_lng, b_bv], [b_lnb])
                    V(lambda e, h=h, o3=o3: e.max(out=bv[:, h, 8:16], in_=o3[:, h, :]), [b_lnb], [b_bv])
                    V(lambda e, h=h, o3=o3: e.max_index(out=bi[:, h, 8:16], in_max=bv[:, h, 8:16], in_values=o3[:, h, :]), [b_lnb, b_bv], [b_bi])
                g3 = gate[:, t, :].rearrange("p (h k) -> p h k", k=16)
                tt(g3, bv[:], bv[:, :, 0:1].to_broadcast([128, 8, 16]), ALU.subtract, [b_bv], [b_gate])
                act(gate[:, t, :], gate[:, t, :], AF.Exp, [b_gate], [b_gate])
                V(lambda e, g3=g3: e.reduce_sum(out=sm[:, 8:16], in_=g3, axis=AX.X), [b_gate], [b_sm])
                V(lambda e: e.reciprocal(out=sm[:, 8:16], in_=sm[:, 8:16]), [b_sm], [b_sm])
                tt(g3, g3, sm[:, 8:16].unsqueeze(2).to_broadcast([128, 8, 16]), ALU.mult, [b_gate, b_sm], [b_gate])
                bi2 = bi[:].rearrange("p h k -> p (h k)")
                V(lambda e, bi2=bi2: e.tensor_single_scalar(out=bti[:, 0, :], in_=bi2, scalar=4, op=ALU.logical_shift_right), [b_bi], [b_bt])
                V(lambda e, bi2=bi2: e.tensor_single_scalar(out=bti[:, 1, :], in_=bi2, scalar=15, op=ALU.bitwise_and), [b_bi], [b_bt])
                V(lambda e: e.tensor_copy(bt[:, 0:2, :], bti[:]), [b_bt], [b_bt])
                for which in range(2):
                    posf = bt[:, which, :].rearrange("p (h k) -> p h k", k=16)
                    e4 = cand.rearrange("p (h k i) -> p h k i", k=16, i=16)
                    tt(e4, posf.unsqueeze(3).to_broadcast([128, 8, 16, 16]),
                       iota16.unsqueeze(1).unsqueeze(1).to_broadcast([128, 8, 16, 16]), ALU.is_equal, [b_bt, b_misc], [b_lng])
                    tt(e4, e4, tif4[:, :, which, :].unsqueeze(2).to_broadcast([128, 8, 16, 16]), ALU.mult, [b_lng, b_tif], [b_lng])
                    V(lambda e, which=which, e4=e4: e.reduce_sum(out=bt[:, 2 + which, :].rearrange("p (h k) -> p h k", k=16), in_=e4, axis=AX.X),
                      [b_lng], [b_bt])
                stt(idxf[:], bt[:, 2, :], 128.0, bt[:, 3, :], ALU.mult, ALU.add, [b_bt], [b_idxf])
                tsc(idxf[:], idxf[:], float(l * 16384), None, ALU.add, None, [b_idxf], [b_idxf])
                pk, pkb = nb()
                fw.op("pe", lambda e, pk=pk: e.transpose(pk[:, 0:128], idxf[:], ident), [b_idxf, b_ident], [pkb])
                V(lambda e, pk=pk, t=t: e.tensor_copy(idxT[:, t, :], pk[:, 0:128]), [pkb], [b_idxT])
                pk2, pk2b = nb()
                fw.op("pe", lambda e, pk2=pk2, t=t: e.transpose(pk2[:, 0:128], gate[:, t, :], ident), [b_gate, b_ident], [pk2b])
                cp(gateT[:, t, :], pk2[:, 0:128], [pk2b], [b_gateT])
            ld(lng[:], l2g[l:l + 1, :].partition_broadcast(128), [b_lng])
            ld(lnb[:], l2b[l:l + 1, :].partition_broadcast(128), [b_lnb])
            for t in range(T):
                ntok = NS if sample else 128
                xr = xT[:, 0:8, :].rearrange("p c n -> p (c n)")
                cp(xr, x_g[:, t, :], [b_x[t]], [b_xT])
                G(lambda e: e.memset(hT[:], 0.0), (), [b_hT])
                for tok in range(ntok):
                    bslot = tok % 4
                    fw.dma("pool", lambda e, bslot=bslot, tok=tok, t=t: e.indirect_dma_start(
                        out=big[:, bslot, :], out_offset=None, in_=pu,
                        in_offset=bass.IndirectOffsetOnAxis(ap=idxT[:, t, tok:tok + 1], axis=0)), [b_idxT], [b_big[bslot]])
                    PX = PA if tok % 2 == 0 else PB
                    pxb = pbuf[0:4] if tok % 2 == 0 else pbuf[4:8]
                    for c in range(4):
                        mm(PX[:, c * 512:(c + 1) * 512], identR[:, tok:tok + 1].to_broadcast([128, 128]), xr[:, c * 512:(c + 1) * 512],
                           True, True, [b_ident, b_xT], [pxb[c]])
                    stt(big[:, bslot, :], f32(big[:, bslot, :]), 1.0, PX[:, :], ALU.mult, ALU.mult, [b_big[bslot]] + pxb, [b_big[bslot], b_hT],
                        accum=hT[:, tok:tok + 1])
                act(wT[:], hT[:], AF.Gelu, [b_hT], [b_wT])
                tt(wT[:], wT[:], gateT[:, t, :], ALU.mult, [b_wT, b_gateT], [b_wT])
                for tok in range(ntok):
                    bslot = tok % 4
                    fw.dma("pool", lambda e, bslot=bslot, tok=tok, t=t: e.indirect_dma_start(
                        out=big[:, bslot, :], out_offset=None, in_=pv,
                        in_offset=bass.IndirectOffsetOnAxis(ap=idxT[:, t, tok:tok + 1], axis=0)), [b_idxT], [b_big[bslot]])
                    zi = tok % 4
                    cp(wz[:, zi, 127:128], wT[:, tok:tok + 1], [b_wT], [b_wz[zi]])
                    for c in range(4):
                        mm(PA[:, c * 512:(c + 1) * 512], wz[:, zi, 127 - tok:255 - tok], big[:, bslot, c * 512:(c + 1) * 512],
                           tok == 0, tok == ntok - 1, [b_wz[zi], b_big[bslot]], [pbuf[c]])
                for c in range(4):
                    stt(x_g[:, t, c * 512:(c + 1) * 512], x_g[:, t, c * 512:(c + 1) * 512], ALPHA, PA[:, c * 512:(c + 1) * 512],
                        ALU.mult, ALU.add, [b_x[t], pbuf[c]], [b_x[t]])
                layer_norm(t, lng, lnb, l)

        groups = [(False, g) for g in range(NG)] + [(True, 0)]
        for (sample, g) in groups:
            T = 1 if sample else 2
            if sample:
                ld(x_g[:, 0, :], xsm, [b_x[0]])
                ld(cs[:, :, 0:128], c_cs[:, :, SEQ:SEQ + 128], [b_cs])
            else:
                for t in range(2):
                    ld(x_g[:, t, :], xp[g * 256 + t * 128:g * 256 + (t + 1) * 128, :], [b_x[t]])
                ld(cs[:], c_cs[:, :, g * 256:(g + 1) * 256], [b_cs])
            for l in range(DEPTH):
                layer_group(l, T, sample, g)
            if sample:
                st(y_s, x_g[:, 0, :], [b_x[0]])
            else:
                for t in range(2):
                    st(y_p[g * 256 + t * 128:g * 256 + (t + 1) * 128, :], x_g[:, t, :], [b_x[t]])
        for l in range(DEPTH):
            for h in range(4):
                st(nhp[l, h], Sb[:, l, h, :], [b_Sb[l][h]])
            for j in range(2):
                st(ngp[l, j], Sc[:, l, j, :], [b_Sc[l][j]])
        fw.finish()
        print("ops recorded:", fw.ninst, flush=True)
        fw.emit_all()
    return nc


_CACHE = {}


def _consts():
    ident = np.eye(128, dtype=np.float32)
    perm = np.zeros((128, 128), np.float32)
    for m in range(128):
        d = m % 64
        if d < 32:
            perm[m + 32, m] = -1.0
        else:
            perm[m - 32, m] = 1.0
    half = 32
    inv = (10000.0 ** (-np.arange(half, dtype=np.float32) / half)).astype(np.float32)
    pos = np.concatenate([np.arange(SEQ, dtype=np.float32), np.full((128,), PAST, np.float32)])
    ang = pos[None, :] * inv[:, None]
    cos = np.cos(ang).astype(np.float32); sin = np.sin(ang).astype(np.float32)
    cs = np.zeros((128, 2, SEQ + 128), np.float32)
    for p in range(128):
        cs[p, 0] = cos[p % 32]; cs[p, 1] = sin[p % 32]
    i = np.arange(128)[:, None]; jj = np.arange(256)[None, :]
    diff = i + 128 - jj
    bandv = (diff >= 0) & (diff <= 128)
    band = np.zeros((2, 128, 256), np.float32)
    band[1] = np.where(bandv, 0.0, NEG)
    band[0] = np.where(bandv & (jj >= 128), 0.0, NEG)
    caus = (np.arange(128)[:, None] <= np.arange(128)[None, :]).astype(np.uint32)
    reset = np.ones((128, 256), np.float32); reset[:, 0] = 0.0; reset[:, 128] = 0.0
    misc = np.zeros((128, 64), np.float32)
    misc[:, 0:16] = np.arange(16, dtype=np.float32)[None, :]
    misc[0:64, 16] = 1.0; misc[64:128, 17] = 1.0
    misc[0, 20] = 1.0; misc[1, 21] = 1.0
    return dict(c_ident=ident, c_perm=perm, c_cs=cs, c_band=band, c_caus=caus, c_reset=reset, c_misc=misc)


def _weight_stream(w_in, w_out, peer_wq):
    L = w_in.shape[0]
    units = np.zeros((L, NU, 2048, 256), np.float32)
    tmcols = np.concatenate([np.arange(1152, 1280), np.arange(2304, 2816), np.arange(2816, 3328),
                             np.arange(3840, 4352), np.arange(4352, 4864)])
    for l in range(L):
        W = w_in[l]
        tmw = W[:, tmcols]
        for u in range(8):
            units[l, U_TM + u] = tmw[:, u * 256:(u + 1) * 256]
        units[l, U_TM + 8, :, 0:128] = tmw[:, 2048:2176]
        units[l, U_TM + 8, :, 128:144] = W[:, 4864:4880]
        k0 = W[:, 1024:1088]; k1 = W[:, 1088:1152]
        units[l, U_FM + 0] = np.concatenate([k0, k0, k1, k1], 1)
        for u in range(4):
            units[l, U_FM + 1 + u] = W[:, u * 256:(u + 1) * 256]
        for h in range(4):
            units[l, U_FM + 5 + h] = np.concatenate([W[:, 1280 + h * 128:1280 + (h + 1) * 128],
                                                     W[:, 1792 + h * 128:1792 + (h + 1) * 128]], 1)
        for j in range(2):
            units[l, U_FM + 9 + j] = np.concatenate([W[:, 3328 + j * 128:3328 + (j + 1) * 128],
                                                     W[:, 3584 + j * 128:3584 + (j + 1) * 128]], 1)
        for u in range(8):
            units[l, U_WO + u] = w_out[l][:, u * 256:(u + 1) * 256]
            units[l, U_WQ + u] = peer_wq[l][:, u * 256:(u + 1) * 256]
    units = units.reshape(L, NU, 16, 128, 256).transpose(0, 1, 3, 2, 4).reshape(L, NU, 128, 4096)
    return np.ascontiguousarray(units)


def kernel(x_prompt, x_sample, cache_k, cache_v, state_hgrn, state_gla, w_in, w_out, attn_sinks,
           hgrn_norm_w, lb_logits, gla_wa2, gla_ba, gla_norm_w, ln1_g, ln1_b, ln2_g, ln2_b,
           peer_wq, peer_keys, peer_u, peer_v):
    f = lambda a: np.ascontiguousarray(np.asarray(a, dtype=np.float32))
    x_prompt = f(x_prompt); x_sample = f(x_sample); cache_k = f(cache_k); cache_v = f(cache_v)
    state_hgrn = f(state_hgrn); state_gla = f(state_gla)
    if "nc" not in _CACHE:
        _CACHE["nc"] = build_program()
    nc = _CACHE["nc"]
    consts = _consts()
    wst = _weight_stream(f(w_in), f(w_out), f(peer_wq))
    pu = f(peer_u).reshape(4 * 16384, D); pv = f(peer_v).reshape(4 * 16384, D)
    keysT = np.ascontiguousarray(f(peer_keys).reshape(4, 16, 128, 128).transpose(0, 3, 1, 2).reshape(4, 128, 2048))
    lbT = np.ascontiguousarray(f(lb_logits).reshape(4, 4, 128).transpose(2, 1, 0).reshape(128, 16))
    nbaT = np.ascontiguousarray(f(gla_ba).reshape(4, 2, 128).transpose(2, 0, 1).reshape(128, 8))
    shared = dict(wst=wst, pu=pu, pv=pv, keysT=keysT, sinks=f(attn_sinks), hnw=f(hgrn_norm_w).reshape(4, 512),
                  gnw=f(gla_norm_w).reshape(4, 512), lbT=lbT, wa2=f(gla_wa2), nbaT=nbaT,
                  l1g=f(ln1_g), l1b=f(ln1_b), l2g=f(ln2_g), l2b=f(ln2_b), **consts)
    in_maps = []
    for c in range(NCORES):
        sq = c % 4
        sb = slice(NS * c, NS * c + NS)
        xs = np.zeros((128, D), np.float32); xs[0:NS] = x_sample[sb, 0, :]
        ckc = cache_k[:, sb].reshape(4, NS, 128, 128)
        ckd = np.concatenate([ckc[..., 0:64], ckc[..., 0:64], ckc[..., 64:128], ckc[..., 64:128]], -1)
        cvc = np.ascontiguousarray(cache_v[:, sb].reshape(4, NS, 128, 128))
        m = dict(shared)
        m.update(xp=x_prompt[sq], xsm=xs, ck=np.ascontiguousarray(ckc), ckd=np.ascontiguousarray(ckd),
                 cv=cvc, cvr=cvc,
                 sh=np.ascontiguousarray(state_hgrn[:, sb]), sg=np.ascontiguousarray(state_gla[:, sb].reshape(4, NS, 2, 128, 128)))
        in_maps.append(m)
    res = run_bass_kernel_spmd(nc, in_maps, core_ids=list(range(NCORES))).results
    y_p = np.stack([res[c]["y_p"] for c in range(4)]).reshape(4, SEQ, D)
    y_s = np.concatenate([res[c]["y_s"][0:NS] for c in range(NCORES)]).reshape(32, 1, D)
    nkp = np.stack([res[c]["nkp"] for c in range(4)], 1).reshape(4, 4, 128, 2, 64)
    nvp = np.stack([res[c]["nvp"] for c in range(4)], 1).reshape(4, 4, 128, 2, 64)
    nhp = np.stack([res[c]["nhp"] for c in range(4)], 1).reshape(4, 4, 4, 128, 128)
    ngp = np.stack([res[c]["ngp"] for c in range(4)], 1).reshape(4, 4, 4, 64, 128)
    nks = np.concatenate([res[c]["nks"] for c in range(NCORES)], 1).reshape(4, 32, 128, 2, 64)
    nvs = np.concatenate([res[c]["nvs"] for c in range(NCORES)], 1).reshape(4, 32, 128, 2, 64)
    nhs = np.concatenate([res[c]["nhs"] for c in range(NCORES)], 1).reshape(4, 32, 4, 128, 128)
    ngs = np.concatenate([res[c]["ngs"] for c in range(NCORES)], 1).reshape(4, 32, 4, 64, 128)
    return tuple(np.ascontiguousarray(a.astype(np.float32)) for a in (y_p, y_s, nkp, nvp, nhp, ngp, nks, nvs, nhs, ngs))
```

```python
import math
import os
import numpy as np
from contextlib import ExitStack
import concourse.bass as bass
import concourse.mybir as mybir
from concourse.bass_utils import run_bass_kernel_spmd

F32 = mybir.dt.float32
F32R = mybir.dt.float32r
I32 = mybir.dt.int32
U32 = mybir.dt.uint32
AF = mybir.ActivationFunctionType
ALU = mybir.AluOpType
AX = mybir.AxisListType

D = 2048
DEPTH = int(os.environ.get("KDEPTH", "4"))
SEQ = 2048
NG = int(os.environ.get("KNG", "8"))
PAST = 16384
ALPHA = (2 * 4) ** 0.25
NU = 36
U_TM, U_FM, U_WO, U_WQ = 0, 9, 20, 28
NEG = -1.0e30
NCORES = 4
NS = 32 // NCORES


class Buf:
    __slots__ = ("name", "w", "r")

    def __init__(self, name):
        self.name = name
        self.w = None
        self.r = []


class Fw:
    ENG = ("pe", "dve", "act", "pool", "sp")

    def __init__(self, nc, es, n_dma_sems=24):
        self.nc = nc
        self.ops = {e: [] for e in self.ENG}
        self.sems = {}
        self.cnt = {}
        self.waited = {e: {} for e in self.ENG}
        for e in self.ENG:
            self.sems[e] = es.enter_context(nc.semaphore("s_" + e))
            self.cnt[e] = 0
        self.dpool = {}
        for q in ("sp", "act", "pool"):
            lst = []
            for i in range(n_dma_sems):
                k = "d_%s_%d" % (q, i)
                self.sems[k] = es.enter_context(nc.semaphore(k))
                self.cnt[k] = 0
                lst.append(k)
            self.dpool[q] = [lst, 0]
        self.nbuf = 0
        self.out_events = []
        self.ninst = 0

    def buf(self, name=None):
        self.nbuf += 1
        return Buf(name or ("b%d" % self.nbuf))

    def _need(self, eng, ev, waits):
        if ev is None:
            return
        k, v = ev
        if self.waited[eng].get(k, 0) >= v:
            return
        if waits.get(k, 0) < v:
            waits[k] = v

    def _deps(self, eng, reads, writes):
        waits = {}
        for b in reads:
            self._need(eng, b.w, waits)
        for b in writes:
            self._need(eng, b.w, waits)
            for ev in b.r:
                self._need(eng, ev, waits)
        return waits

    def _commit(self, ev, reads, writes):
        for b in reads:
            b.r.append(ev)
            if len(b.r) > 48:
                d = {}
                for k, v in b.r:
                    if d.get(k, 0) < v:
                        d[k] = v
                b.r = list(d.items())
        for b in writes:
            b.w = ev
            b.r = []

    def op(self, eng, fn, reads=(), writes=()):
        waits = self._deps(eng, reads, writes)
        for k, v in waits.items():
            self.waited[eng][k] = v
        self.cnt[eng] += 1
        val = self.cnt[eng]
        sem = self.sems[eng]
        wl = [(self.sems[k], v) for k, v in waits.items()]
        self.ninst += 1 + len(wl)

        def emit(e):
            for s, v in wl:
                e.wait_ge(s, v)
            fn(e).then_inc(sem, 1)
        self.ops[eng].append(emit)
        ev = (eng, val)
        self._commit(ev, reads, writes)
        return ev

    def dma(self, q, fn, reads=(), writes=(), out=False):
        lst, idx = self.dpool[q]
        k = lst[idx % len(lst)]
        self.dpool[q][1] = idx + 1
        waits = self._deps(q, reads, writes)
        prev = self.cnt[k]
        if prev > 0 and self.waited[q].get(k, 0) < prev and waits.get(k, 0) < prev:
            waits[k] = prev
        for kk, v in waits.items():
            self.waited[q][kk] = v
        self.cnt[k] = prev + 16
        val = self.cnt[k]
        sem = self.sems[k]
        wl = [(self.sems[kk], v) for kk, v in waits.items()]
        self.ninst += 1 + len(wl)

        def emit(e):
            for s, v in wl:
                e.wait_ge(s, v)
            fn(e).then_inc(sem, 16)
        self.ops[q].append(emit)
        ev = (k, val)
        self._commit(ev, reads, writes)
        if out:
            self.out_events.append(ev)
        return ev

    def inherit(self, dsts, srcs):
        evs = {}
        for b in srcs:
            for ev in ([b.w] if b.w else []) + list(b.r):
                if evs.get(ev[0], 0) < ev[1]:
                    evs[ev[0]] = ev[1]
        for d in dsts:
            d.r.extend(evs.items())

    def finish(self):
        waits = {}
        for ev in self.out_events:
            self._need("sp", ev, waits)
        wl = [(self.sems[k], v) for k, v in waits.items()]

        def emit(e):
            for s, v in wl:
                e.wait_ge(s, v)
        self.ops["sp"].append(emit)

    def emit_all(self):
        nc = self.nc
        ops = self.ops
        with nc.Block() as block:
            @block.tensor
            def _(e):
                for f in ops["pe"]:
                    f(e)

            @block.vector
            def _(e):
                for f in ops["dve"]:
                    f(e)

            @block.scalar
            def _(e):
                for f in ops["act"]:
                    f(e)

            @block.gpsimd
            def _(e):
                for f in ops["pool"]:
                    f(e)

            @block.sync
            def _(e):
                for f in ops["sp"]:
                    f(e)


def build_program():
    nc = bass.Bass("TRN2", target_bir_lowering=False)
    es = ExitStack()

    def DI(n, s, dt=F32):
        return nc.dram_tensor(n, list(s), dt, kind="ExternalInput").ap()

    def DO(n, s, dt=F32):
        return nc.dram_tensor(n, list(s), dt, kind="ExternalOutput").ap()

    xp = DI("xp", [SEQ, D]); xsm = DI("xsm", [128, D])
    wst = DI("wst", [4, NU, 128, 4096], F32R)
    pu = DI("pu", [4 * 16384, D], F32R); pv = DI("pv", [4 * 16384, D], F32R)
    keysT = DI("keysT", [4, 128, 16 * 128], F32R)
    ck = DI("ck", [4, NS, 128, 128]); cv = DI("cv", [4, NS, 128, 128]); cvr = DI("cvr", [4, NS, 128, 128], F32R)
    ckd = DI("ckd", [4, NS, 128, 256])
    sh = DI("sh", [4, NS, 4, 128, 128]); sg = DI("sg", [4, NS, 2, 128, 128])
    sinks = DI("sinks", [4, 16]); hnw = DI("hnw", [4, 512]); gnw = DI("gnw", [4, 512])
    lbT = DI("lbT", [128, 16])
    wa2 = DI("wa2", [4, 16, 256], F32R); nbaT = DI("nbaT", [128, 8])
    l1g = DI("l1g", [4, D]); l1b = DI("l1b", [4, D]); l2g = DI("l2g", [4, D]); l2b = DI("l2b", [4, D])
    c_ident = DI("c_ident", [128, 128], F32R); c_perm = DI("c_perm", [128, 128], F32R)
    c_cs = DI("c_cs", [128, 2, SEQ + 128])
    c_band = DI("c_band", [2, 128, 256]); c_caus = DI("c_caus", [128, 128], U32)
    c_reset = DI("c_reset", [128, 256]); c_misc = DI("c_misc", [128, 64])

    y_p = DO("y_p", [SEQ, D]); y_s = DO("y_s", [128, D])
    nkp = DO("nkp", [4, 128, 128]); nvp = DO("nvp", [4, 128, 128])
    nhp = DO("nhp", [4, 4, 128, 128]); ngp = DO("ngp", [4, 2, 128, 128])
    nks = DO("nks", [4, NS, 128, 128]); nvs = DO("nvs", [4, NS, 128, 128])
    nhs = DO("nhs", [4, NS, 4, 128, 128]); ngs = DO("ngs", [4, NS, 2, 128, 128])

    with es:
        fw = Fw(nc, es)

        def S(n, s, dt=F32):
            return es.enter_context(nc.sbuf_tensor(n, list(s), dt))

        def f32(ap):
            return ap.bitcast(F32)

        def mm(out, lhsT, rhs, st, sp, R, W):
            return fw.op("pe", lambda e: e.matmul(out, lhsT, rhs, start=st, stop=sp), R, W)

        def V(fn, R, W):
            return fw.op("dve", fn, R, W)

        def A(fn, R, W):
            return fw.op("act", fn, R, W)

        def G(fn, R, W):
            return fw.op("pool", fn, R, W)

        def act(out, in_, func, R, W, bias=None, scale=None, accum=None):
            kw = {}
            if bias is not None:
                kw["bias"] = bias
            if scale is not None:
                kw["scale"] = scale
            if accum is not None:
                kw["accum_out"] = accum
            return A(lambda e: e.activation(out=out, in_=in_, func=func, **kw), R, W)

        def tsc(out, in0, s1, s2, op0, op1, R, W, eng="dve"):
            if s2 is None:
                return fw.op(eng, lambda e: e.tensor_scalar(out=out, in0=in0, scalar1=s1, scalar2=None, op0=op0), R, W)
            return fw.op(eng, lambda e: e.tensor_scalar(out=out, in0=in0, scalar1=s1, scalar2=s2, op0=op0, op1=op1), R, W)

        def tt(out, in0, in1, op, R, W, eng="dve"):
            return fw.op(eng, lambda e: e.tensor_tensor(out=out, in0=in0, in1=in1, op=op), R, W)

        def stt(out, in0, sc, in1, op0, op1, R, W, accum=None):
            if accum is None:
                return V(lambda e: e.scalar_tensor_tensor(out=out, in0=in0, scalar=sc, in1=in1, op0=op0, op1=op1), R, W)
            return V(lambda e: e.scalar_tensor_tensor(out=out, in0=in0, scalar=sc, in1=in1, op0=op0, op1=op1,
                                                      accum_out=accum), R, W)

        cp_rr = [0]

        def cp(out, in_, R, W):
            cp_rr[0] ^= 1
            if cp_rr[0]:
                return A(lambda e: e.copy(out, in_), R, W)
            return V(lambda e: e.tensor_copy(out, in_), R, W)

        dq_rr = [0]

        def ld(out, in_, W, R=()):
            dq_rr[0] ^= 1
            return fw.dma("sp" if dq_rr[0] else "act", lambda e: e.dma_start(out=out, in_=in_), R, W)

        def ldr(out, in_, W, R=()):
            return fw.dma("pool", lambda e: e.dma_start(out=out, in_=in_), R, W)

        def st(out, in_, R):
            return fw.dma("sp", lambda e: e.dma_start(out=out, in_=in_), R, (), out=True)

        PA = es.enter_context(nc.psum_tensor("PA", [128, 2048], F32))
        PB = es.enter_context(nc.psum_tensor("PB", [128, 2048], F32))
        pbuf = [fw.buf("pb%d" % i) for i in range(8)]
        pb_rr = [0]

        def bank(k):
            t = PA if k < 4 else PB
            return t[:, (k % 4) * 512:(k % 4) * 512 + 512]

        def nb():
            k = pb_rr[0] % 8
            pb_rr[0] += 1
            return bank(k), pbuf[k]

        identR = S("identR", [128, 128], F32R); ident = f32(identR[:]); b_ident = fw.buf()
        permR = S("permR", [128, 128], F32R); b_perm = fw.buf()
        band = S("band", [128, 2, 256]); b_band = fw.buf()
        caus = S("caus", [128, 128], U32); b_caus = fw.buf()
        reset = S("reset", [128, 256]); b_reset = fw.buf()
        misc = S("misc", [128, 64]); b_misc = fw.buf()
        zeros = S("zeros", [128, 128]); b_zeros = fw.buf()
        lbt = S("lbt", [128, 16]); lbw = S("lbw", [128, 16]); lbs = S("lbs", [128, 4]); b_lb = fw.buf()
        lowb = S("lowb", [128, 16]); oml = S("oml", [128, 16])
        nba = S("nba", [128, 8]); b_nba = fw.buf()
        x_g = S("x_g", [128, 2, D]); b_x = [fw.buf(), fw.buf()]
        xT = S("xT", [128, 16, 256], F32R); b_xT = fw.buf()
        NRING = 2
        ring = S("ring", [128, NRING, 4096], F32R); b_ring = [fw.buf() for _ in range(NRING)]
        big = S("big", [128, 4, D], F32R); b_big = [fw.buf() for _ in range(4)]
        lng = S("lng", [128, D]); lnb = S("lnb", [128, D]); b_lng = fw.buf(); b_lnb = fw.buf()
        tm = S("tm", [128, 2, 2176], F32R); b_tm = [fw.buf(), fw.buf()]
        cs = S("cs", [128, 2, 256]); b_cs = fw.buf()
        sink_bc = S("sink_bc", [128, 16]); b_sink = fw.buf()
        nwb = S("nwb", [128, 2, 512]); b_nwb = fw.buf()
        wa2s = S("wa2s", [16, 256], F32R); b_wa2 = fw.buf()
        acT = S("acT", [16, 256], F32R); b_acT = fw.buf()
        kTd = S("kTd", [128, 2, 384], F32R); b_kTd = fw.buf()
        vbuf = S("vbuf", [128, 3, 128], F32R); b_vbuf = fw.buf()
        kcar = S("kcar", [128, 4, 2, 128], F32R); vcar = S("vcar", [128, 4, 128], F32R)
        b_kcar = [fw.buf() for _ in range(4)]; b_vcar = [fw.buf() for _ in range(4)]
        qraw = S("qraw", [128, 256], F32R); b_qraw = fw.buf()
        qrot = S("qrot", [128, 256], F32R); b_qrot = fw.buf()
        t1 = S("t1", [128, 256]); b_t1 = fw.buf()
        ssb = S("ssb", [128, 256]); b_ssb = fw.buf()
        psb = S("psb", [128, 256]); b_psb = fw.buf()
        pT = S("pT", [128, 2, 128], F32R); b_pT = fw.buf()
        sm = S("sm", [128, 16]); b_sm = fw.buf()
        ktok = S("ktok", [128, 2, 128]); b_ktok = fw.buf()
        rt = S("rt", [128, 10, 256]); b_rt = [fw.buf() for _ in range(10)]
        qt = S("qt", [128, 256], F32R); kt = S("kt", [128, 256], F32R); b_qt = fw.buf(); b_kt = fw.buf()
        kh = S("kh", [128, 2, 128], F32R); b_kh = fw.buf()
        attm = S("attm", [128, 128], F32R); b_attm = fw.buf()
        atf = S("atf", [128, 128]); b_atf = fw.buf()
        Ssc = S("Ssc", [128, 128], F32R); b_Ssc = fw.buf()
        cols = S("cols", [128, 8]); b_cols = fw.buf()
        Sb = S("Sb", [128, 4, 4, 128]); b_Sb = [[fw.buf() for _ in range(4)] for _ in range(4)]
        Sc = S("Sc", [128, 4, 2, 128]); b_Sc = [[fw.buf() for _ in range(2)] for _ in range(4)]
        b_ob = [[b_big[0], b_big[1]], [b_big[0], b_big[1]]]

        def obv(mxr, t):
            return big[:, t, 1024 + mxr * 512:1536 + mxr * 512]
        sS = rt[:, 5, 0:128]; b_sS = b_rt[5]
        sSn = rt[:, 6, 0:128]; b_sSn = b_rt[6]
        sSr = big[:, 3, 1280:1408]; b_sSr = b_big[3]
        skd = rt[:, 7, :]; b_skd = b_rt[7]
        b_skT = b_big[2]
        b_sv = b_big[1]
        sv = big[:, 1, 0:NS * 128].rearrange("p (b n) -> p b n", b=NS)

        def skTv(b, kv):
            lo = (b % 4) * 260 + kv * 130
            return big[:, 2 + b // 4, lo:lo + 130]
        slh = S("slh", [128, 16], F32R); b_slh = fw.buf()
        osel = S("osel", [16, 64]); b_osel = fw.buf()
        pq = S("pq", [128, 256], F32R); b_pq = fw.buf()
        tv = S("tv", [128, 16, 16]); ti = S("ti", [128, 16, 16], U32); b_tv = fw.buf(); b_ti = fw.buf()
        tif = S("tif", [128, 16, 16]); b_tif = fw.buf()
        bv = S("bv", [128, 8, 16]); bi = S("bi", [128, 8, 16], U32); b_bv = fw.buf(); b_bi = fw.buf()
        bt = S("bt", [128, 6, 128]); bti = S("bti", [128, 2, 128], U32); b_bt = fw.buf()
        idxf = S("idxf", [128, 128]); gate = S("gate", [128, 2, 128]); b_idxf = fw.buf(); b_gate = fw.buf()
        idxT = S("idxT", [128, 2, 128], I32); gateT = S("gateT", [128, 2, 128]); b_idxT = fw.buf(); b_gateT = fw.buf()
        hT = S("hT", [128, 128]); wT = S("wT", [128, 128]); b_hT = fw.buf(); b_wT = fw.buf()
        wz = S("wz", [128, 4, 255], F32R); b_wz = [fw.buf() for _ in range(4)]
        qz = wz; b_qz = b_wz
        lnst = S("lnst", [128, 8]); b_lnst = fw.buf()

        ldr(identR[:], c_ident, [b_ident]); ldr(permR[:], c_perm, [b_perm])
        ld(band[:], c_band.rearrange("a p k -> p a k"), [b_band]); ld(caus[:], c_caus, [b_caus])
        ld(reset[:], c_reset, [b_reset]); ld(misc[:], c_misc, [b_misc])
        ld(lbt[:], lbT, [b_lb]); ld(nba[:], nbaT, [b_nba])
        G(lambda e: e.memset(zeros[:], 0.0), (), [b_zeros])
        V(lambda e: e.tensor_copy(wz[:].rearrange("p a b -> p (a b)"), zeros[:, 0:1].to_broadcast([128, 4 * 255])), [b_zeros], b_wz)
        G(lambda e: e.memset(Sb[:], 0.0), (), [b for r in b_Sb for b in r])
        G(lambda e: e.memset(Sc[:], 0.0), (), [b for r in b_Sc for b in r])
        V(lambda e: e.tensor_copy(kcar[:].rearrange("p a b c -> p (a b c)"), zeros[:, 0:1].to_broadcast([128, 1024])), [b_zeros], b_kcar)
        V(lambda e: e.tensor_copy(vcar[:].rearrange("p a b -> p (a b)"), zeros[:, 0:1].to_broadcast([128, 512])), [b_zeros], b_vcar)
        iota16 = misc[:, 0:16]
        hmask = misc[:, 16:18]
        kvs = misc[:, 18:20]
        tsc(nba[:], nba[:], -1.0, None, ALU.mult, None, [b_nba], [b_nba])
        act(lbw[:], lbt[:], AF.Exp, [b_lb], [b_lb])
        V(lambda e: e.reduce_sum(out=lbs[:], in_=lbw[:].rearrange("p (h l) -> p h l", l=4), axis=AX.X), [b_lb], [b_lb])
        V(lambda e: e.reciprocal(out=lbs[:], in_=lbs[:]), [b_lb], [b_lb])
        tt(lbw[:].rearrange("p (h l) -> p h l", l=4), lbw[:].rearrange("p (h l) -> p h l", l=4),
           lbs[:].unsqueeze(2).to_broadcast([128, 4, 4]), ALU.mult, [b_lb], [b_lb])
        lw3 = lbw[:].rearrange("p (h l) -> p h l", l=4)
        lo3 = lowb[:].rearrange("p (h l) -> p h l", l=4)
        G(lambda e: e.memset(lowb[:], 0.0), (), [b_lb])
        for l in range(1, 4):
            tt(lo3[:, :, l:l + 1], lo3[:, :, l - 1:l], lw3[:, :, l:l + 1], ALU.add, [b_lb], [b_lb])
        tsc(oml[:], lowb[:], -1.0, 1.0, ALU.mult, ALU.add, [b_lb], [b_lb])

        for l in range(DEPTH):
            for b in range(NS):
                fw.dma("act", lambda e, l=l, b=b: e.dma_start(out=nks[l, b, 0:127, :], in_=ck[l, b, 1:128, :]), (), (), out=True)
                fw.dma("act", lambda e, l=l, b=b: e.dma_start(out=nvs[l, b, 0:127, :], in_=cv[l, b, 1:128, :]), (), (), out=True)

        g_ring = [[fw.buf(), fw.buf()] for _ in range(NRING)]
        gslots = [(big[:, i, :], b_big[i]) for i in range(4)]
        for s_ in range(NRING):
            for h_ in range(2):
                gslots.append((ring[:, s_, h_ * 2048:(h_ + 1) * 2048], g_ring[s_][h_]))
        for t_ in range(2):
            gslots.append((tm[:, t_, 0:2048], b_tm[t_]))
        NSLOT = len(gslots)
        ucount = [0]

        def load_unit(l, u):
            slot = ucount[0] % NRING
            ucount[0] += 1
            ldr(ring[:, slot, :], wst[l, u], [b_ring[slot]])
            return ring[:, slot, :].rearrange("p (c n) -> p c n", n=256), b_ring[slot]

        def transpose_to_xT(src_f32, bsrc, T):
            for t in range(T):
                for c4 in range(4):
                    pk, pkb = nb()
                    for cc in range(4):
                        c = c4 * 4 + cc
                        fw.op("pe", lambda e, t=t, c=c, cc=cc, pk=pk: e.transpose(pk[:, cc * 128:(cc + 1) * 128],
                                                                               src_f32(t)[:, c * 128:(c + 1) * 128], ident),
                              [bsrc[t], b_ident], [pkb])
                    cp(xT[:, c4 * 4:(c4 + 1) * 4, t * 128:(t + 1) * 128],
                       pk.rearrange("p (c n) -> p c n", n=128), [pkb], [b_xT])

        def fm_block(wv, wb, blk, N, ncols=128):
            pk, pkb = nb()
            for c in range(16):
                mm(pk[0:ncols, 0:N], wv[:, c, blk * 128:blk * 128 + ncols], xT[:, c, 0:N], c == 0, c == 15,
                   [wb, b_xT], [pkb])
            return pk, pkb

        def layer_norm(t, gl, bl, l):
            xs = x_g[:, t, :]
            bx = b_x[t]
            V(lambda e: e.memset(lnst[:, 0:2], 0.0), (), [b_lnst])
            V(lambda e: e.reduce_sum(out=lnst[:, 0:1], in_=xs, axis=AX.X), [bx], [b_lnst])
            act(big[:, 3, :], xs, AF.Square, [bx], [b_big[3], b_lnst], accum=lnst[:, 1:2])
            tsc(lnst[:, 2:3], lnst[:, 0:1], 1.0 / D, None, ALU.mult, None, [b_lnst], [b_lnst])
            tt(lnst[:, 3:4], lnst[:, 2:3], lnst[:, 2:3], ALU.mult, [b_lnst], [b_lnst])
            stt(lnst[:, 4:5], lnst[:, 1:2], 1.0 / D, lnst[:, 3:4], ALU.mult, ALU.subtract, [b_lnst], [b_lnst])
            tsc(lnst[:, 5:6], lnst[:, 4:5], 1e-5, None, ALU.add, None, [b_lnst], [b_lnst])
            act(lnst[:, 5:6], lnst[:, 5:6], AF.Sqrt, [b_lnst], [b_lnst])
            V(lambda e: e.reciprocal(out=lnst[:, 5:6], in_=lnst[:, 5:6]), [b_lnst], [b_lnst])
            tsc(xs, xs, lnst[:, 2:3], lnst[:, 5:6], ALU.subtract, ALU.mult, [bx, b_lnst], [bx])
            tt(xs, xs, lng[:], ALU.mult, [bx, b_lng], [bx])
            tt(xs, xs, lnb[:], ALU.add, [bx, b_lnb], [bx])

        def layer_group(l, T, sample, g):
            N = T * 128
            ld(sink_bc[:], sinks[l:l + 1, :].partition_broadcast(128), [b_sink])
            ld(nwb[:, 0, :], hnw[l:l + 1, :].partition_broadcast(128), [b_nwb])
            ld(nwb[:, 1, :], gnw[l:l + 1, :].partition_broadcast(128), [b_nwb])
            ldr(wa2s[:], wa2[l], [b_wa2])
            ld(lng[:], l1g[l:l + 1, :].partition_broadcast(128), [b_lng])
            ld(lnb[:], l1b[l:l + 1, :].partition_broadcast(128), [b_lnb])
            transpose_to_xT(lambda t: x_g[:, t, :], b_x, T)
            for u in range(9):
                wv, wb = load_unit(l, U_TM + u)
                for t in range(T):
                    if u < 8:
                        pk, pkb = nb()
                        for c in range(16):
                            mm(pk[:, 0:256], xT[:, c, t * 128:(t + 1) * 128], wv[:, c, :], c == 0, c == 15, [b_xT, wb], [pkb])
                        cp(tm[:, t, u * 256:(u + 1) * 256], pk[:, 0:256], [pkb], [b_tm[t]])
                    else:
                        pk, pkb = nb()
                        for c in range(16):
                            mm(pk[:, 0:128], xT[:, c, t * 128:(t + 1) * 128], wv[:, c, 0:128], c == 0, c == 15, [b_xT, wb], [pkb])
                        cp(tm[:, t, 2048:2176], pk[:, 0:128], [pkb], [b_tm[t]])
                if u == 8:
                    pk, pkb = fm_block(wv, wb, 1, N, 16)
                    cp(acT[:, 0:N], pk[0:16, 0:N], [pkb], [b_acT])
            for t in range(T):
                for lo in (640, 1664):
                    act(tm[:, t, lo:lo + 512], f32(tm[:, t, lo:lo + 512]), AF.Silu, [b_tm[t]], [b_tm[t]])
            wv, wb = load_unit(l, U_FM + 0)
            if not sample:
                V(lambda e: e.tensor_copy(kTd[:, :, 0:128], kcar[:, l, :, :]), [b_kcar[l]], [b_kTd])
                V(lambda e: e.tensor_copy(vbuf[:, 0, :], vcar[:, l, :]), [b_vcar[l]], [b_vbuf])
                for t in range(T):
                    cp(vbuf[:, 1 + t, :], f32(tm[:, t, 0:128]), [b_tm[t]], [b_vbuf])
            for kv in range(2):
                pk, pkb = fm_block(wv, wb, kv, N)
                cp(qraw[:, 0:N], pk[:, 0:N], [pkb], [b_qraw])
                p2, p2b = nb()
                mm(p2[:, 0:N], permR[:], qraw[:, 0:N], True, True, [b_perm, b_qraw], [p2b])
                tt(t1[:, 0:N], f32(qraw[:, 0:N]), cs[:, 0, 0:N], ALU.mult, [b_qraw, b_cs], [b_t1])
                tt(qrot[:, 0:N], p2[:, 0:N], cs[:, 1, 0:N], ALU.mult, [p2b, b_cs], [b_qrot])
                tt(kTd[:, kv, 128:128 + N], t1[:, 0:N], f32(qrot[:, 0:N]), ALU.add, [b_t1, b_qrot], [b_kTd])
            for kv in range(2):
                pk, pkb = nb()
                fw.op("pe", lambda e, kv=kv, pk=pk: e.transpose(pk[:, 0:128], f32(kTd[:, kv, 128 + (T - 1) * 128:128 + T * 128]), ident),
                      [b_kTd, b_ident], [pkb])
                cp(ktok[:, kv, 0:64], pk[:, 0:64], [pkb], [b_ktok])
            if not sample:
                if g == NG - 1:
                    st(nkp[l].rearrange("p (k d) -> p k d", d=64), ktok[:, :, 0:64], [b_ktok])
                    st(nvp[l], f32(tm[:, T - 1, 0:128]), [b_tm[T - 1]])
                V(lambda e: e.tensor_copy(kcar[:, l, :, :], kTd[:, :, 128 + (T - 1) * 128:128 + T * 128]), [b_kTd], [b_kcar[l]])
                V(lambda e: e.tensor_copy(vcar[:, l, :], tm[:, T - 1, 0:128]), [b_tm[T - 1]], [b_vcar[l]])
            else:
                for b in range(NS):
                    ld(skd[:], ckd[l, b], [b_skd])
                    ldr(sv[:, b, :], cvr[l, b], [b_sv])
                    for kv in range(2):
                        pk, pkb = nb()
                        fw.op("pe", lambda e, pk=pk, kv=kv: e.transpose(pk[:, 0:128], skd[:, kv * 128:(kv + 1) * 128], ident), [b_skd, b_ident], [pkb])
                        cp(skTv(b, kv)[:, 0:128], pk[:, 0:128], [pkb], [b_big[2 + b // 4]])
                        cp(skTv(b, kv)[:, 128:129], f32(kTd[:, kv, 128 + b:129 + b]), [b_kTd], [b_big[2 + b // 4]])
                        cp(skTv(b, kv)[:, 129:130], f32(kTd[:, kv, 128 + b:129 + b]), [b_kTd], [b_big[2 + b // 4]])
                for b in range(NS):
                    st(nks[l, b, 127:128, :].rearrange("p (k d) -> p k d", d=64), ktok[b:b + 1, :, 0:64], [b_ktok])
                    st(nvs[l, b, 127:128, :], f32(tm[b:b + 1, 0, 0:128]), [b_tm[0]])
            for j in range(8):
                if j % 2 == 0:
                    wv, wb = load_unit(l, U_FM + 1 + j // 2)
                pk, pkb = fm_block(wv, wb, j % 2, N)
                cp(qraw[:, 0:N], pk[:, 0:N], [pkb], [b_qraw])
                p2, p2b = nb()
                mm(p2[:, 0:N], permR[:], qraw[:, 0:N], True, True, [b_perm, b_qraw], [p2b])
                tt(t1[:, 0:N], f32(qraw[:, 0:N]), cs[:, 0, 0:N], ALU.mult, [b_qraw, b_cs], [b_t1])
                tt(ssb[:, 0:N], p2[:, 0:N], cs[:, 1, 0:N], ALU.mult, [p2b, b_cs], [b_ssb])
                tt(qrot[:, 0:N], t1[:, 0:N], ssb[:, 0:N], ALU.add, [b_t1, b_ssb], [b_qrot])
                kv = j // 4
                if not sample:
                    for t in range(T):
                        for hh in range(2):
                            h = 2 * j + hh
                            ps = slice(hh * 64, hh * 64 + 64)
                            pk, pkb = nb()
                            mm(pk[:, 0:256], qrot[ps, t * 128:(t + 1) * 128], kTd[ps, kv, t * 128:t * 128 + 256], True, True,
                               [b_qrot, b_kTd], [pkb])
                            mi = 0 if (g == 0 and t == 0) else 1
                            stt(ssb[:], pk[:, 0:256], 0.125, band[:, mi, :], ALU.mult, ALU.add, [pkb, b_band], [b_ssb])
                            V(lambda e: e.reduce_max(out=sm[:, 0:1], in_=ssb[:], axis=AX.X), [b_ssb], [b_sm])
                            tsc(sm[:, 1:2], sm[:, 0:1], sink_bc[:, h:h + 1], -1.0, ALU.max, ALU.mult, [b_sm, b_sink], [b_sm])
                            V(lambda e: e.memset(sm[:, 2:3], 0.0), (), [b_sm])
                            act(psb[:], ssb[:], AF.Exp, [b_ssb, b_sm], [b_psb, b_sm], bias=sm[:, 1:2], accum=sm[:, 2:3])
                            act(sm[:, 3:4], sink_bc[:, h:h + 1], AF.Exp, [b_sink, b_sm], [b_sm], bias=sm[:, 1:2])
                            tt(sm[:, 4:5], sm[:, 2:3], sm[:, 3:4], ALU.add, [b_sm], [b_sm])
                            V(lambda e: e.reciprocal(out=sm[:, 5:6], in_=sm[:, 4:5]), [b_sm], [b_sm])
                            p3, p3b = nb()
                            for blk in range(2):
                                fw.op("pe", lambda e, blk=blk, p3=p3: e.transpose(p3[:, blk * 128:(blk + 1) * 128],
                                                                                  psb[:, blk * 128:(blk + 1) * 128], ident),
                                      [b_psb, b_ident], [p3b])
                            cp(pT[:], p3[:, 0:256].rearrange("p (b n) -> p b n", n=128), [p3b], [b_pT])
                            p4, p4b = nb()
                            for blk in range(2):
                                mm(p4[:, 0:64], pT[:, blk, :], vbuf[:, t + blk, kv * 64:(kv + 1) * 64], blk == 0, blk == 1,
                                   [b_pT, b_vbuf], [p4b])
                            tsc(big[:, t, h * 64:(h + 1) * 64], p4[:, 0:64], sm[:, 5:6], None, ALU.mult, None,
                                [p4b, b_sm], [b_big[t]])
                else:
                    sample_attn_block(l, j)
            for h in range(4):
                wv, wb = load_unit(l, U_FM + 5 + h)
                pq_, pqb = fm_block(wv, wb, 0, N)
                pf_, pfb = fm_block(wv, wb, 1, N)
                R = rt
                act(R[:, 0, 0:N], pq_[:, 0:N], AF.Silu, [pqb], [b_rt[0]])
                act(R[:, 1, 0:N], pf_[:, 0:N], AF.Sigmoid, [pfb], [b_rt[1]])
                tsc(R[:, 1, 0:N], R[:, 1, 0:N], oml[:, h * 4 + l:h * 4 + l + 1], lowb[:, h * 4 + l:h * 4 + l + 1],
                    ALU.mult, ALU.add, [b_rt[1], b_lb], [b_rt[1]])
                tsc(R[:, 2, 0:N], R[:, 1, 0:N], -1.0, 1.0, ALU.mult, ALU.add, [b_rt[1]], [b_rt[2]])
                if sample:
                    sample_rec(l, h, None, R[:, 0, :], b_rt[0], R[:, 1, :], b_rt[1], R[:, 2, :], b_rt[2], 640 - 512)
                else:
                    tsc(R[:, 3, 0:N], R[:, 1, 0:N], 1e-30, None, ALU.max, None, [b_rt[1]], [b_rt[3]])
                    act(R[:, 3, 0:N], R[:, 3, 0:N], AF.Ln, [b_rt[3]], [b_rt[3]])
                    chunk_rec(l, N, T, R[:, 0, :], b_rt[0], R[:, 2, :], b_rt[2], 3,
                              [(0, 128, Sb[:, l, h, :], b_Sb[l][h], 128 + h * 128, 0, h)], 1.0)
            for j in range(2):
                wv, wb = load_unit(l, U_FM + 9 + j)
                pq_, pqb = fm_block(wv, wb, 0, N)
                pk_, pkb_ = fm_block(wv, wb, 1, N)
                R = rt
                pz, pzb = nb()
                mm(pz[:, 0:N], wa2s[:, j * 128:(j + 1) * 128], acT[:, 0:N], True, True, [b_wa2, b_acT], [pzb])
                act(R[:, 3, 0:N], pz[:, 0:N], AF.Exp, [pzb, b_nba], [b_rt[3]], bias=nba[:, l * 2 + j:l * 2 + j + 1], scale=-1.0)
                act(R[:, 3, 0:N], R[:, 3, 0:N], AF.Ln, [b_rt[3]], [b_rt[3]], bias=1.0)
                tsc(R[:, 3, 0:N], R[:, 3, 0:N], -1.0 / 16.0, None, ALU.mult, None, [b_rt[3]], [b_rt[3]])
                tsc(R[:, 0, 0:N], pq_[:, 0:N], 0.125, None, ALU.mult, None, [pqb], [b_rt[0]])
                cp(R[:, 2, 0:N], pk_[:, 0:N], [pkb_], [b_rt[2]])
                if sample:
                    act(R[:, 1, 0:N], R[:, 3, 0:N], AF.Exp, [b_rt[3]], [b_rt[1]])
                    sample_rec(l, None, j, R[:, 0, :], b_rt[0], R[:, 1, :], b_rt[1], R[:, 2, :], b_rt[2], 1152)
                else:
                    chunk_rec(l, N, T, R[:, 0, :], b_rt[0], R[:, 2, :], b_rt[2], 3,
                              [(0, 64, Sc[0:64, l, j, :], b_Sc[l][j], 1152 + (2 * j) * 128, 1, 2 * j),
                               (64, 64, Sc[64:128, l, j, :], b_Sc[l][j], 1152 + (2 * j + 1) * 128, 1, 2 * j + 1)], 1.0)
            for t in range(T):
                for mxr in range(2):
                    ovw = obv(mxr, t)
                    o3 = ovw.rearrange("p (h d) -> p h d", d=128)
                    bo = b_ob[mxr][t]
                    rsq = rt[:, 4:6, :].rearrange("p a n -> p (a n)")
                    tt(rsq, f32(ovw), f32(ovw), ALU.mult, [bo], [b_rt[4], b_rt[5]])
                    V(lambda e, rsq=rsq: e.reduce_sum(out=sm[:, 8:12], in_=rsq.rearrange("p (h d) -> p h d", d=128), axis=AX.X),
                      [b_rt[4], b_rt[5]], [b_sm])
                    tsc(sm[:, 8:12], sm[:, 8:12], 1.0 / 128.0, 1e-6, ALU.mult, ALU.add, [b_sm], [b_sm])
                    act(sm[:, 8:12], sm[:, 8:12], AF.Sqrt, [b_sm], [b_sm])
                    V(lambda e: e.reciprocal(out=sm[:, 8:12], in_=sm[:, 8:12]), [b_sm], [b_sm])
                    tt(o3, f32(o3), sm[:, 8:12].unsqueeze(2).to_broadcast([128, 4, 128]), ALU.mult, [bo, b_sm], [bo])
                    tt(ovw, f32(ovw), nwb[:, mxr, :], ALU.mult, [bo, b_nwb], [bo])
                    glo = 640 if mxr == 0 else 1664
                    tt(ovw, f32(ovw), f32(tm[:, t, glo:glo + 512]), ALU.mult, [bo, b_tm[t]], [bo])
            transpose_to_xT(lambda t: f32(big[:, t, :]), b_big, T)
            for u in range(8):
                wv, wb = load_unit(l, U_WO + u)
                for t in range(T):
                    pk, pkb = nb()
                    for c in range(16):
                        mm(pk[:, 0:256], xT[:, c, t * 128:(t + 1) * 128], wv[:, c, :], c == 0, c == 15, [b_xT, wb], [pkb])
                    stt(x_g[:, t, u * 256:(u + 1) * 256], x_g[:, t, u * 256:(u + 1) * 256], ALPHA, pk[:, 0:256], ALU.mult, ALU.add,
                        [b_x[t], pkb], [b_x[t]])
            for t in range(T):
                layer_norm(t, lng, lnb, l)
            transpose_to_xT(lambda t: x_g[:, t, :], b_x, T)
            ldr(ring[:, NRING - 1, 0:2048], keysT[l], [b_ring[NRING - 1]])
            peer(l, T, sample)

        def chunk_rec(l, N, T, qv, bq, kv_, bk, lai, heads, _):
            R = rt
            la = R[:, lai, :]
            bla = b_rt[lai]
            bc = R[:, 4, :]; bbc = b_rt[4]
            V(lambda e: e.tensor_tensor_scan(out=bc[:, 0:N], data0=reset[:, 0:N], data1=la[:, 0:N], initial=0.0,
                                             op0=ALU.mult, op1=ALU.add), [b_reset, bla], [bbc])
            for t in range(T):
                lo = t * 128
                tsc(R[:, 5, lo:lo + 128], bc[:, lo:lo + 128], bc[:, lo + 64:lo + 65], None, ALU.subtract, None, [bbc], [b_rt[5]])
                act(R[:, 6, lo:lo + 128], bc[:, lo:lo + 128], AF.Exp, [bbc], [b_rt[6]], bias=bc[:, lo + 127:lo + 128], scale=-1.0)
                act(cols[:, 2 * t:2 * t + 1], bc[:, lo + 64:lo + 65], AF.Exp, [bbc], [b_cols])
                act(cols[:, 2 * t + 1:2 * t + 2], bc[:, lo + 127:lo + 128], AF.Exp, [bbc], [b_cols])
            act(R[:, 7, 0:N], R[:, 5, 0:N], AF.Exp, [b_rt[5]], [b_rt[7]])
            act(R[:, 8, 0:N], R[:, 5, 0:N], AF.Exp, [b_rt[5]], [b_rt[8]], scale=-1.0)
            tt(qt[:, 0:N], qv[:, 0:N], R[:, 7, 0:N], ALU.mult, [bq, b_rt[7]], [b_qt])
            tt(kt[:, 0:N], kv_[:, 0:N], R[:, 8, 0:N], ALU.mult, [bk, b_rt[8]], [b_kt])
            tt(R[:, 9, 0:N], kv_[:, 0:N], R[:, 6, 0:N], ALU.mult, [bk, b_rt[6]], [b_rt[9]])
            for t in range(T):
                lo = t * 128
                pk, pkb = nb()
                fw.op("pe", lambda e, pk=pk, lo=lo: e.transpose(pk[:, 0:128], R[:, 9, lo:lo + 128], ident), [b_rt[9], b_ident], [pkb])
                cp(kh[:, t, :], pk[:, 0:128], [pkb], [b_kh])
            for t in range(T):
                lo = t * 128
                for (p0, pn, Sap, Sbuf_, vlo, mxr, hidx) in heads:
                    ps = slice(p0, p0 + pn)
                    tsc(Ssc[ps, :], Sap, cols[ps, 2 * t:2 * t + 1], None, ALU.mult, None, [Sbuf_, b_cols], [b_Ssc])
                    pk, pkb = nb()
                    mm(pk[:, 0:128], kt[ps, lo:lo + 128], qt[ps, lo:lo + 128], True, True, [b_kt, b_qt], [pkb])
                    V(lambda e, pk=pk: e.select(out=atf[:], mask=caus[:], on_true=pk[:, 0:128], on_false=zeros[:]),
                      [pkb, b_caus, b_zeros], [b_atf])
                    cp(attm[:], atf[:], [b_atf], [b_attm])
                    p2, p2b = nb()
                    mm(p2[:, 0:128], qt[ps, lo:lo + 128], Ssc[ps, :], True, False, [b_qt, b_Ssc], [p2b])
                    mm(p2[:, 0:128], attm[:], tm[:, t, vlo:vlo + 128], False, True, [b_attm, b_tm[t]], [p2b])
                    cp(obv(mxr, t)[:, (hidx % 4) * 128:(hidx % 4) * 128 + 128], p2[:, 0:128], [p2b], [b_ob[mxr][t]])
                    p3, p3b = nb()
                    mm(p3[:, 0:128], kh[:, t, :], tm[:, t, vlo:vlo + 128], True, True, [b_kh, b_tm[t]], [p3b])
                    stt(Sap, Sap, cols[ps, 2 * t + 1:2 * t + 2], p3[ps, 0:128], ALU.mult, ALU.add, [Sbuf_, b_cols, p3b], [Sbuf_])

        def sample_rec(l, h, j, qv, bq, av, ba, kv_, bk, _unused):
            hg = h is not None
            vlo = 128 if hg else 1152
            accs = []
            nh = 1 if hg else 2
            for hh in range(nh):
                accs.append((bank(6 + hh), pbuf[6 + hh]))
            for b in range(NS):
                src = sh[l, b, h] if hg else sg[l, b, j]
                ld(sS[:], src, [b_sS])
                pv_, pvb = bank(b % 6), pbuf[b % 6]
                mm(pv_[:, 0:512], identR[:, b:b + 1].to_broadcast([128, 128]), tm[:, 0, vlo:vlo + 512], True, True,
                   [b_ident, b_tm[0]], [pvb])
                tsc(sSn[:], sS[:], av[:, b:b + 1], None, ALU.mult, None, [b_sS, ba], [b_sSn])
                for hh in range(nh):
                    ps = slice(0, 128) if hg else slice(hh * 64, hh * 64 + 64)
                    hd = h if hg else 2 * j + hh
                    stt(sSn[ps, :], pv_[ps, hd * 128:(hd + 1) * 128], kv_[ps, b:b + 1], sSn[ps, :], ALU.mult, ALU.add,
                        [pvb, bk, b_sSn], [b_sSn])
                dst = nhs[l, b, h] if hg else ngs[l, b, j]
                st(dst, sSn[:], [b_sSn])
                cp(sSr[:], sSn[:], [b_sSn], [b_sSr])
                zi = b % 4
                cp(qz[:, zi, 127:128], qv[:, b:b + 1], [bq], [b_qz[zi]])
                for hh in range(nh):
                    ps = slice(0, 128) if hg else slice(hh * 64, hh * 64 + 64)
                    pk, pkb = accs[hh]
                    mm(pk[:, 0:128], qz[ps, zi, 127 - b:255 - b], sSr[ps, :], b == 0, b == NS - 1, [b_qz[zi], b_sSr], [pkb])
            for hh in range(nh):
                hd = h if hg else 2 * j + hh
                pk, pkb = accs[hh]
                cp(obv(0 if hg else 1, 0)[:, (hd % 4) * 128:(hd % 4) * 128 + 128], pk[:, 0:128], [pkb], [b_ob[0 if hg else 1][0]])

        def sample_attn_block(l, j):
            kv = j // 4
            for b in range(NS):
                tt(slh[:, 0:2], f32(qrot[:, b:b + 1]).to_broadcast([128, 2]), hmask, ALU.mult, [b_qrot, b_misc], [b_slh])
                p1, p1b = nb()
                mm(p1[0:2, 0:130], slh[:, 0:2], skTv(b, kv), True, True, [b_slh, b_big[2 + b // 4]], [p1b])
                tsc(ssb[0:2, 0:129], p1[0:2, 0:129], 0.125, None, ALU.mult, None, [p1b], [b_ssb])
                V(lambda e: e.reduce_max(out=sm[0:2, 0:1], in_=ssb[0:2, 0:129], axis=AX.X), [b_ssb], [b_sm])
                tt(sm[0:2, 6:8], sink_bc[0:2, 2 * j:2 * j + 2], misc[0:2, 20:22], ALU.mult, [b_sink, b_misc], [b_sm])
                V(lambda e: e.reduce_sum(out=sm[0:2, 7:8], in_=sm[0:2, 6:8], axis=AX.X), [b_sm], [b_sm])
                tsc(sm[0:2, 1:2], sm[0:2, 0:1], sm[0:2, 7:8], -1.0, ALU.max, ALU.mult, [b_sm], [b_sm])
                V(lambda e: e.memset(sm[0:2, 2:3], 0.0), (), [b_sm])
                act(psb[0:2, 0:129], ssb[0:2, 0:129], AF.Exp, [b_ssb, b_sm], [b_psb, b_sm], bias=sm[0:2, 1:2], accum=sm[0:2, 2:3])
                act(sm[0:2, 3:4], sm[0:2, 7:8], AF.Exp, [b_sm], [b_sm], bias=sm[0:2, 1:2])
                tt(sm[0:2, 4:5], sm[0:2, 2:3], sm[0:2, 3:4], ALU.add, [b_sm], [b_sm])
                V(lambda e: e.reciprocal(out=sm[0:2, 5:6], in_=sm[0:2, 4:5]), [b_sm], [b_sm])
                p3, p3b = nb()
                fw.op("pe", lambda e, p3=p3: e.transpose(p3[:, 0:2], psb[0:2, 0:128], ident[0:2, 0:2]), [b_psb, b_ident], [p3b])
                cp(pT[:, 0, 0:2], p3[:, 0:2], [p3b], [b_pT])
                p4, p4b = nb()
                mm(p4[0:2, 0:128], pT[:, 0, 0:2], sv[:, b, :], True, True, [b_pT, b_sv], [p4b])
                p5, p5b = nb()
                mm(p5[0:2, 0:128], identR[:, b:b + 1].to_broadcast([128, 2]), tm[:, 0, 0:128], True, True, [b_ident, b_tm[0]], [p5b])
                cp(ssb[0:2, 0:128], p4[0:2, 0:128], [p4b], [b_ssb])
                stt(t1[0:2, 0:128], p5[0:2, 0:128], psb[0:2, 128:129], ssb[0:2, 0:128], ALU.mult, ALU.add, [p5b, b_ssb, b_psb], [b_t1])
                tsc(osel[0:2, :], t1[0:2, kv * 64:(kv + 1) * 64], sm[0:2, 5:6], None, ALU.mult, None, [b_t1, b_sm], [b_osel])
                for hh in range(2):
                    fw.dma("pool", lambda e, b=b, j=j, hh=hh: e.dma_start(out=big[b:b + 1, 0, (2 * j + hh) * 64:(2 * j + hh + 1) * 64],
                                                                      in_=osel[hh:hh + 1, :]), [b_osel], [b_big[0]])

        def peer(l, T, sample):
            keyv = ring[:, NRING - 1, 0:2048].rearrange("p (g n) -> p g n", n=128)
            bkey = b_ring[NRING - 1]
            for u in range(8):
                slot = u % (NRING - 1)
                ldr(ring[:, slot, :], wst[l, U_WQ + u], [b_ring[slot]])
                wv = ring[:, slot, :].rearrange("p (c n) -> p c n", n=256)
                for blk in range(2):
                    hp = u * 2 + blk
                    pk, pkb = fm_block(wv, b_ring[slot], blk, T * 128)
                    cp(pq[:, 0:T * 128], pk[:, 0:T * 128], [pkb], [b_pq])
                    for t in range(T):
                        p2, p2b = nb()
                        mm(p2[:, 0:128], pq[:, t * 128:(t + 1) * 128], keyv[:, hp, :], True, True, [b_pq, bkey], [p2b])
                        cp(big[:, t, hp * 128:(hp + 1) * 128], p2[:, 0:128], [p2b], [b_big[t]])
            for t in range(T):
                sc = f32(big[:, t, :]); sc2 = lnb[:]; sc2w = lnb[:]; cand = lng[:]
                bsc = b_big[t]
                sc3 = sc.rearrange("p (g n) -> p g n", n=128)
                s23 = sc2.rearrange("p (g n) -> p g n", n=128)
                s23w = sc2w.rearrange("p (g n) -> p g n", n=128)
                for hp in range(16):
                    V(lambda e, hp=hp, sc3=sc3: e.max(out=tv[:, hp, 0:8], in_=sc3[:, hp, :]), [bsc], [b_tv])
                    V(lambda e, hp=hp, sc3=sc3: e.max_index(out=ti[:, hp, 0:8], in_max=tv[:, hp, 0:8], in_values=sc3[:, hp, :]), [bsc, b_tv], [b_ti])
                    V(lambda e, hp=hp, sc3=sc3, s23=s23w: e.match_replace(out=s23[:, hp, :], in_to_replace=tv[:, hp, 0:8], in_values=sc3[:, hp, :], imm_value=NEG),
                      [bsc, b_tv], [b_lnb])
                    V(lambda e, hp=hp, s23=s23: e.max(out=tv[:, hp, 8:16], in_=s23[:, hp, :]), [b_lnb], [b_tv])
                    V(lambda e, hp=hp, s23=s23: e.max_index(out=ti[:, hp, 8:16], in_max=tv[:, hp, 8:16], in_values=s23[:, hp, :]), [b_lnb, b_tv], [b_ti])
                V(lambda e: e.tensor_copy(tif[:], ti[:]), [b_ti], [b_tif])
                tv4 = tv[:].rearrange("p (h two) k -> p h two k", two=2)
                tif4 = tif[:].rearrange("p (h two) k -> p h two k", two=2)
                c4 = cand.rearrange("p (h i j) -> p h i j", i=16, j=16)
                tt(c4, tv4[:, :, 0, :].unsqueeze(3).to_broadcast([128, 8, 16, 16]),
                   tv4[:, :, 1, :].unsqueeze(2).to_broadcast([128, 8, 16, 16]), ALU.add, [b_tv], [b_lng])
                c3 = cand.rearrange("p (h n) -> p h n", n=256)
                o3 = sc2.rearrange("p (h n) -> p h n", n=256)
                o3w = sc2w.rearrange("p (h n) -> p h n", n=256)
                for h in range(8):
                    V(lambda e, h=h, c3=c3: e.max(out=bv[:, h, 0:8], in_=c3[:, h, :]), [b_lng], [b_bv])
                    V(lambda e, h=h, c3=c3: e.max_index(out=bi[:, h, 0:8], in_max=bv[:, h, 0:8], in_values=c3[:, h, :]), [b_lng, b_bv], [b_bi])
                    V(lambda e, h=h, c3=c3, o3=o3w: e.match_replace(out=o3[:, h, :], in_to_replace=bv[:, h, 0:8], in_values=c3[:, h, :], imm_value=NEG),
                      [b_lng, b_bv], [b_lnb])
                    V(lambda e, h=h, o3=o3: e.max(out=bv[:, h, 8:16], in_=o3[:, h, :]), [b_lnb], [b_bv])
                    V(lambda e, h=h, o3=o3: e.max_index(out=bi[:, h, 8:16], in_max=bv[:, h, 8:16], in_values=o3[:, h, :]), [b_lnb, b_bv], [b_bi])
                g3 = gate[:, t, :].rearrange("p (h k) -> p h k", k=16)
                tt(g3, bv[:], bv[:, :, 0:1].to_broadcast([128, 8, 16]), ALU.subtract, [b_bv], [b_gate])
                act(gate[:, t, :], gate[:, t, :], AF.Exp, [b_gate], [b_gate])
                V(lambda e, g3=g3: e.reduce_sum(out=sm[:, 8:16], in_=g3, axis=AX.X), [b_gate], [b_sm])
                V(lambda e: e.reciprocal(out=sm[:, 8:16], in_=sm[:, 8:16]), [b_sm], [b_sm])
                tt(g3, g3, sm[:, 8:16].unsqueeze(2).to_broadcast([128, 8, 16]), ALU.mult, [b_gate, b_sm], [b_gate])
                bi2 = bi[:].rearrange("p h k -> p (h k)")
                V(lambda e, bi2=bi2: e.tensor_single_scalar(out=bti[:, 0, :], in_=bi2, scalar=4, op=ALU.logical_shift_right), [b_bi], [b_bt])
                V(lambda e, bi2=bi2: e.tensor_single_scalar(out=bti[:, 1, :], in_=bi2, scalar=15, op=ALU.bitwise_and), [b_bi], [b_bt])
                V(lambda e: e.tensor_copy(bt[:, 0:2, :], bti[:]), [b_bt], [b_bt])
                for which in range(2):
                    posf = bt[:, which, :].rearrange("p (h k) -> p h k", k=16)
                    e4 = cand.rearrange("p (h k i) -> p h k i", k=16, i=16)
                    tt(e4, posf.unsqueeze(3).to_broadcast([128, 8, 16, 16]),
                       iota16.unsqueeze(1).unsqueeze(1).to_broadcast([128, 8, 16, 16]), ALU.is_equal, [b_bt, b_misc], [b_lng])
                    tt(e4, e4, tif4[:, :, which, :].unsqueeze(2).to_broadcast([128, 8, 16, 16]), ALU.mult, [b_lng, b_tif], [b_lng])
                    V(lambda e, which=which, e4=e4: e.reduce_sum(out=bt[:, 2 + which, :].rearrange("p (h k) -> p h k", k=16), in_=e4, axis=AX.X),
                      [b_lng], [b_bt])
                stt(idxf[:], bt[:, 2, :], 128.0, bt[:, 3, :], ALU.mult, ALU.add, [b_bt], [b_idxf])
                tsc(idxf[:], idxf[:], float(l * 16384), None, ALU.add, None, [b_idxf], [b_idxf])
                pk, pkb = nb()
                fw.op("pe", lambda e, pk=pk: e.transpose(pk[:, 0:128], idxf[:], ident), [b_idxf, b_ident], [pkb])
                V(lambda e, pk=pk, t=t: e.tensor_copy(idxT[:, t, :], pk[:, 0:128]), [pkb], [b_idxT])
                pk2, pk2b = nb()
                fw.op("pe", lambda e, pk2=pk2, t=t: e.transpose(pk2[:, 0:128], gate[:, t, :], ident), [b_gate, b_ident], [pk2b])
                cp(gateT[:, t, :], pk2[:, 0:128], [pk2b], [b_gateT])
            ld(lng[:], l2g[l:l + 1, :].partition_broadcast(128), [b_lng])
            ld(lnb[:], l2b[l:l + 1, :].partition_broadcast(128), [b_lnb])
            for s_ in range(NRING):
                fw.inherit(g_ring[s_], [b_ring[s_]])
            for t in range(T):
                ntok = NS if sample else 128
                xr = xT[:, 0:8, :].rearrange("p c n -> p (c n)")
                cp(xr, x_g[:, t, :], [b_x[t]], [b_xT])
                G(lambda e: e.memset(hT[:], 0.0), (), [b_hT])
                for tok in range(ntok):
                    gsl, gbuf = gslots[tok % NSLOT]
                    fw.dma("pool", lambda e, gsl=gsl, tok=tok, t=t: e.indirect_dma_start(
                        out=gsl, out_offset=None, in_=pu,
                        in_offset=bass.IndirectOffsetOnAxis(ap=idxT[:, t, tok:tok + 1], axis=0)), [b_idxT], [gbuf])
                    PX = PA if tok % 2 == 0 else PB
                    pxb = pbuf[0:4] if tok % 2 == 0 else pbuf[4:8]
                    for c in range(4):
                        mm(PX[:, c * 512:(c + 1) * 512], identR[:, tok:tok + 1].to_broadcast([128, 128]), xr[:, c * 512:(c + 1) * 512],
                           True, True, [b_ident, b_xT], [pxb[c]])
                    stt(gsl, f32(gsl), 1.0, PX[:, :], ALU.mult, ALU.mult, [gbuf] + pxb, [gbuf, b_hT],
                        accum=hT[:, tok:tok + 1])
                act(wT[:], hT[:], AF.Gelu, [b_hT], [b_wT])
                tt(wT[:], wT[:], gateT[:, t, :], ALU.mult, [b_wT, b_gateT], [b_wT])
                for tok in range(ntok):
                    gsl, gbuf = gslots[tok % NSLOT]
                    fw.dma("pool", lambda e, gsl=gsl, tok=tok, t=t: e.indirect_dma_start(
                        out=gsl, out_offset=None, in_=pv,
                        in_offset=bass.IndirectOffsetOnAxis(ap=idxT[:, t, tok:tok + 1], axis=0)), [b_idxT], [gbuf])
                    zi = tok % 4
                    cp(wz[:, zi, 127:128], wT[:, tok:tok + 1], [b_wT], [b_wz[zi]])
                    for c in range(4):
                        mm(PA[:, c * 512:(c + 1) * 512], wz[:, zi, 127 - tok:255 - tok], gsl[:, c * 512:(c + 1) * 512],
                           tok == 0, tok == ntok - 1, [b_wz[zi], gbuf], [pbuf[c]])
                for c in range(4):
                    stt(x_g[:, t, c * 512:(c + 1) * 512], x_g[:, t, c * 512:(c + 1) * 512], ALPHA, PA[:, c * 512:(c + 1) * 512],
                        ALU.mult, ALU.add, [b_x[t], pbuf[c]], [b_x[t]])
                layer_norm(t, lng, lnb, l)
            for s_ in range(NRING):
                fw.inherit([b_ring[s_]], g_ring[s_])

        groups = [(False, g) for g in range(NG)] + [(True, 0)]
        for (sample, g) in groups:
            T = 1 if sample else 2
            if sample:
                ld(x_g[:, 0, :], xsm, [b_x[0]])
                ld(cs[:, :, 0:128], c_cs[:, :, SEQ:SEQ + 128], [b_cs])
            else:
                for t in range(2):
                    ld(x_g[:, t, :], xp[g * 256 + t * 128:g * 256 + (t + 1) * 128, :], [b_x[t]])
                ld(cs[:], c_cs[:, :, g * 256:(g + 1) * 256], [b_cs])
            for l in range(DEPTH):
                layer_group(l, T, sample, g)
            if sample:
                st(y_s, x_g[:, 0, :], [b_x[0]])
            else:
                for t in range(2):
                    st(y_p[g * 256 + t * 128:g * 256 + (t + 1) * 128, :], x_g[:, t, :], [b_x[t]])
        for l in range(DEPTH):
            for h in range(4):
                st(nhp[l, h], Sb[:, l, h, :], [b_Sb[l][h]])
            for j in range(2):
                st(ngp[l, j], Sc[:, l, j, :], [b_Sc[l][j]])
        fw.finish()
        print("ops recorded:", fw.ninst, flush=True)
        fw.emit_all()
    return nc


_CACHE = {}


def _consts():
    ident = np.eye(128, dtype=np.float32)
    perm = np.zeros((128, 128), np.float32)
    for m in range(128):
        d = m % 64
        if d < 32:
            perm[m + 32, m] = -1.0
        else:
            perm[m - 32, m] = 1.0
    half = 32
    inv = (10000.0 ** (-np.arange(half, dtype=np.float32) / half)).astype(np.float32)
    pos = np.concatenate([np.arange(SEQ, dtype=np.float32), np.full((128,), PAST, np.float32)])
    ang = pos[None, :] * inv[:, None]
    cos = np.cos(ang).astype(np.float32); sin = np.sin(ang).astype(np.float32)
    cs = np.zeros((128, 2, SEQ + 128), np.float32)
    for p in range(128):
        cs[p, 0] = cos[p % 32]; cs[p, 1] = sin[p % 32]
    i = np.arange(128)[:, None]; jj = np.arange(256)[None, :]
    diff = i + 128 - jj
    bandv = (diff >= 0) & (diff <= 128)
    band = np.zeros((2, 128, 256), np.float32)
    band[1] = np.where(bandv, 0.0, NEG)
    band[0] = np.where(bandv & (jj >= 128), 0.0, NEG)
    caus = (np.arange(128)[:, None] <= np.arange(128)[None, :]).astype(np.uint32)
    reset = np.ones((128, 256), np.float32); reset[:, 0] = 0.0; reset[:, 128] = 0.0
    misc = np.zeros((128, 64), np.float32)
    misc[:, 0:16] = np.arange(16, dtype=np.float32)[None, :]
    misc[0:64, 16] = 1.0; misc[64:128, 17] = 1.0
    misc[0, 20] = 1.0; misc[1, 21] = 1.0
    return dict(c_ident=ident, c_perm=perm, c_cs=cs, c_band=band, c_caus=caus, c_reset=reset, c_misc=misc)


def _weight_stream(w_in, w_out, peer_wq):
    L = w_in.shape[0]
    units = np.zeros((L, NU, 2048, 256), np.float32)
    tmcols = np.concatenate([np.arange(1152, 1280), np.arange(2304, 2816), np.arange(2816, 3328),
                             np.arange(3840, 4352), np.arange(4352, 4864)])
    for l in range(L):
        W = w_in[l]
        tmw = W[:, tmcols]
        for u in range(8):
            units[l, U_TM + u] = tmw[:, u * 256:(u + 1) * 256]
        units[l, U_TM + 8, :, 0:128] = tmw[:, 2048:2176]
        units[l, U_TM + 8, :, 128:144] = W[:, 4864:4880]
        k0 = W[:, 1024:1088]; k1 = W[:, 1088:1152]
        units[l, U_FM + 0] = np.concatenate([k0, k0, k1, k1], 1)
        for u in range(4):
            units[l, U_FM + 1 + u] = W[:, u * 256:(u + 1) * 256]
        for h in range(4):
            units[l, U_FM + 5 + h] = np.concatenate([W[:, 1280 + h * 128:1280 + (h + 1) * 128],
                                                     W[:, 1792 + h * 128:1792 + (h + 1) * 128]], 1)
        for j in range(2):
            units[l, U_FM + 9 + j] = np.concatenate([W[:, 3328 + j * 128:3328 + (j + 1) * 128],
                                                     W[:, 3584 + j * 128:3584 + (j + 1) * 128]], 1)
        for u in range(8):
            units[l, U_WO + u] = w_out[l][:, u * 256:(u + 1) * 256]
            units[l, U_WQ + u] = peer_wq[l][:, u * 256:(u + 1) * 256]
    units = units.reshape(L, NU, 16, 128, 256).transpose(0, 1, 3, 2, 4).reshape(L, NU, 128, 4096)
    return np.ascontiguousarray(units)


def kernel(x_prompt, x_sample, cache_k, cache_v, state_hgrn, state_gla, w_in, w_out, attn_sinks,
           hgrn_norm_w, lb_logits, gla_wa2, gla_ba, gla_norm_w, ln1_g, ln1_b, ln2_g, ln2_b,
           peer_wq, peer_keys, peer_u, peer_v):
    f = lambda a: np.ascontiguousarray(np.asarray(a, dtype=np.float32))
    x_prompt = f(x_prompt); x_sample = f(x_sample); cache_k = f(cache_k); cache_v = f(cache_v)
    state_hgrn = f(state_hgrn); state_gla = f(state_gla)
    if "nc" not in _CACHE:
        _CACHE["nc"] = build_program()
    nc = _CACHE["nc"]
    consts = _consts()
    wst = _weight_stream(f(w_in), f(w_out), f(peer_wq))
    pu = f(peer_u).reshape(4 * 16384, D); pv = f(peer_v).reshape(4 * 16384, D)
    keysT = np.ascontiguousarray(f(peer_keys).reshape(4, 16, 128, 128).transpose(0, 3, 1, 2).reshape(4, 128, 2048))
    lbT = np.ascontiguousarray(f(lb_logits).reshape(4, 4, 128).transpose(2, 1, 0).reshape(128, 16))
    nbaT = np.ascontiguousarray(f(gla_ba).reshape(4, 2, 128).transpose(2, 0, 1).reshape(128, 8))
    shared = dict(wst=wst, pu=pu, pv=pv, keysT=keysT, sinks=f(attn_sinks), hnw=f(hgrn_norm_w).reshape(4, 512),
                  gnw=f(gla_norm_w).reshape(4, 512), lbT=lbT, wa2=f(gla_wa2), nbaT=nbaT,
                  l1g=f(ln1_g), l1b=f(ln1_b), l2g=f(ln2_g), l2b=f(ln2_b), **consts)
    in_maps = []
    for c in range(NCORES):
        sq = c % 4
        sb = slice(NS * c, NS * c + NS)
        xs = np.zeros((128, D), np.float32); xs[0:NS] = x_sample[sb, 0, :]
        ckc = cache_k[:, sb].reshape(4, NS, 128, 128)
        ckd = np.concatenate([ckc[..., 0:64], ckc[..., 0:64], ckc[..., 64:128], ckc[..., 64:128]], -1)
        cvc = np.ascontiguousarray(cache_v[:, sb].reshape(4, NS, 128, 128))
        m = dict(shared)
        m.update(xp=x_prompt[sq], xsm=xs, ck=np.ascontiguousarray(ckc), ckd=np.ascontiguousarray(ckd),
                 cv=cvc, cvr=cvc,
                 sh=np.ascontiguousarray(state_hgrn[:, sb]), sg=np.ascontiguousarray(state_gla[:, sb].reshape(4, NS, 2, 128, 128)))
        in_maps.append(m)
    res = run_bass_kernel_spmd(nc, in_maps, core_ids=list(range(NCORES))).results
    y_p = np.stack([res[c]["y_p"] for c in range(4)]).reshape(4, SEQ, D)
    y_s = np.concatenate([res[c]["y_s"][0:NS] for c in range(NCORES)]).reshape(32, 1, D)
    nkp = np.stack([res[c]["nkp"] for c in range(4)], 1).reshape(4, 4, 128, 2, 64)
    nvp = np.stack([res[c]["nvp"] for c in range(4)], 1).reshape(4, 4, 128, 2, 64)
    nhp = np.stack([res[c]["nhp"] for c in range(4)], 1).reshape(4, 4, 4, 128, 128)
    ngp = np.stack([res[c]["ngp"] for c in range(4)], 1).reshape(4, 4, 4, 64, 128)
    nks = np.concatenate([res[c]["nks"] for c in range(NCORES)], 1).reshape(4, 32, 128, 2, 64)
    nvs = np.concatenate([res[c]["nvs"] for c in range(NCORES)], 1).reshape(4, 32, 128, 2, 64)
    nhs = np.concatenate([res[c]["nhs"] for c in range(NCORES)], 1).reshape(4, 32, 4, 128, 128)
    ngs = np.concatenate([res[c]["ngs"] for c in range(NCORES)], 1).reshape(4, 32, 4, 64, 128)
    return tuple(np.ascontiguousarray(a.astype(np.float32)) for a in (y_p, y_s, nkp, nvp, nhp, ngp, nks, nvs, nhs, ngs))
```

```python
import math
import os
import numpy as np
from contextlib import ExitStack
import concourse.bass as bass
import concourse.mybir as mybir
from concourse.bass_utils import run_bass_kernel_spmd

F32 = mybir.dt.float32
F32R = mybir.dt.float32r
I32 = mybir.dt.int32
U32 = mybir.dt.uint32
BF16 = mybir.dt.bfloat16
AF = mybir.ActivationFunctionType
ALU = mybir.AluOpType
AX = mybir.AxisListType

D = 2048
DEPTH = int(os.environ.get("KDEPTH", "4"))
SEQ = 2048
NG = int(os.environ.get("KNG", "8"))
PAST = 16384
ALPHA = (2 * 4) ** 0.25
NU = 36
U_TM, U_FM, U_WO, U_WQ = 0, 9, 20, 28
NEG = -1.0e30
NCORES = 4
NS = 32 // NCORES


class Buf:
    __slots__ = ("name", "w", "r")

    def __init__(self, name):
        self.name = name
        self.w = None
        self.r = []


class Fw:
    ENG = ("pe", "dve", "act", "pool", "sp")

    def __init__(self, nc, es, n_dma_sems=24):
        self.nc = nc
        self.ops = {e: [] for e in self.ENG}
        self.sems = {}
        self.cnt = {}
        self.waited = {e: {} for e in self.ENG}
        for e in self.ENG:
            self.sems[e] = es.enter_context(nc.semaphore("s_" + e))
            self.cnt[e] = 0
        self.dpool = {}
        for q in ("sp", "act", "pool"):
            lst = []
            for i in range(n_dma_sems):
                k = "d_%s_%d" % (q, i)
                self.sems[k] = es.enter_context(nc.semaphore(k))
                self.cnt[k] = 0
                lst.append(k)
            self.dpool[q] = [lst, 0]
        self.nbuf = 0
        self.out_events = []
        self.ninst = 0

    def buf(self, name=None):
        self.nbuf += 1
        return Buf(name or ("b%d" % self.nbuf))

    def _need(self, eng, ev, waits):
        if ev is None:
            return
        k, v = ev
        if self.waited[eng].get(k, 0) >= v:
            return
        if waits.get(k, 0) < v:
            waits[k] = v

    def _deps(self, eng, reads, writes):
        waits = {}
        for b in reads:
            self._need(eng, b.w, waits)
        for b in writes:
            self._need(eng, b.w, waits)
            for ev in b.r:
                self._need(eng, ev, waits)
        return waits

    def _commit(self, ev, reads, writes):
        for b in reads:
            b.r.append(ev)
            if len(b.r) > 48:
                d = {}
                for k, v in b.r:
                    if d.get(k, 0) < v:
                        d[k] = v
                b.r = list(d.items())
        for b in writes:
            b.w = ev
            b.r = []

    def op(self, eng, fn, reads=(), writes=()):
        waits = self._deps(eng, reads, writes)
        for k, v in waits.items():
            self.waited[eng][k] = v
        self.cnt[eng] += 1
        val = self.cnt[eng]
        sem = self.sems[eng]
        wl = [(self.sems[k], v) for k, v in waits.items()]
        self.ninst += 1 + len(wl)

        def emit(e):
            for s, v in wl:
                e.wait_ge(s, v)
            fn(e).then_inc(sem, 1)
        self.ops[eng].append(emit)
        ev = (eng, val)
        self._commit(ev, reads, writes)
        return ev

    def dma(self, q, fn, reads=(), writes=(), out=False):
        lst, idx = self.dpool[q]
        k = lst[idx % len(lst)]
        self.dpool[q][1] = idx + 1
        waits = self._deps(q, reads, writes)
        prev = self.cnt[k]
        if prev > 0 and self.waited[q].get(k, 0) < prev and waits.get(k, 0) < prev:
            waits[k] = prev
        for kk, v in waits.items():
            self.waited[q][kk] = v
        self.cnt[k] = prev + 16
        val = self.cnt[k]
        sem = self.sems[k]
        wl = [(self.sems[kk], v) for kk, v in waits.items()]
        self.ninst += 1 + len(wl)

        def emit(e):
            for s, v in wl:
                e.wait_ge(s, v)
            fn(e).then_inc(sem, 16)
        self.ops[q].append(emit)
        ev = (k, val)
        self._commit(ev, reads, writes)
        if out:
            self.out_events.append(ev)
        return ev

    def inherit(self, dsts, srcs):
        evs = {}
        for b in srcs:
            for ev in ([b.w] if b.w else []) + list(b.r):
                if evs.get(ev[0], 0) < ev[1]:
                    evs[ev[0]] = ev[1]
        for d in dsts:
            d.r.extend(evs.items())

    def finish(self):
        waits = {}
        for ev in self.out_events:
            self._need("sp", ev, waits)
        wl = [(self.sems[k], v) for k, v in waits.items()]

        def emit(e):
            for s, v in wl:
                e.wait_ge(s, v)
        self.ops["sp"].append(emit)

    def emit_all(self):
        nc = self.nc
        ops = self.ops
        with nc.Block() as block:
            @block.tensor
            def _(e):
                for f in ops["pe"]:
                    f(e)

            @block.vector
            def _(e):
                for f in ops["dve"]:
                    f(e)

            @block.scalar
            def _(e):
                for f in ops["act"]:
                    f(e)

            @block.gpsimd
            def _(e):
                for f in ops["pool"]:
                    f(e)

            @block.sync
            def _(e):
                for f in ops["sp"]:
                    f(e)


def build_program():
    nc = bass.Bass("TRN2", target_bir_lowering=False)
    es = ExitStack()

    def DI(n, s, dt=F32):
        return nc.dram_tensor(n, list(s), dt, kind="ExternalInput").ap()

    def DO(n, s, dt=F32):
        return nc.dram_tensor(n, list(s), dt, kind="ExternalOutput").ap()

    xp = DI("xp", [SEQ, D]); xsm = DI("xsm", [128, D])
    wst = DI("wst", [4, NU, 128, 4096], F32)
    pu = DI("pu", [4 * 16384, D], F32R); pv = DI("pv", [4 * 16384, D], F32R)
    keysT = DI("keysT", [4, 128, 16 * 128], F32)
    ck = DI("ck", [4, NS, 128, 128]); cv = DI("cv", [4, NS, 128, 128]); cvr = DI("cvr", [4, NS, 128, 128], F32R)
    ckd = DI("ckd", [4, NS, 128, 256])
    sh = DI("sh", [4, NS, 4, 128, 128]); sg = DI("sg", [4, NS, 2, 128, 128])
    sinks = DI("sinks", [4, 16]); hnw = DI("hnw", [4, 512]); gnw = DI("gnw", [4, 512])
    lbT = DI("lbT", [128, 16])
    wa2 = DI("wa2", [4, 16, 256], F32R); nbaT = DI("nbaT", [128, 8])
    l1g = DI("l1g", [4, D]); l1b = DI("l1b", [4, D]); l2g = DI("l2g", [4, D]); l2b = DI("l2b", [4, D])
    c_ident = DI("c_ident", [128, 128], F32R); c_perm = DI("c_perm", [128, 128], F32R)
    c_cs = DI("c_cs", [128, 2, SEQ + 128])
    c_band = DI("c_band", [2, 128, 256]); c_caus = DI("c_caus", [128, 128], U32)
    c_reset = DI("c_reset", [128, 256]); c_misc = DI("c_misc", [128, 64])

    y_p = DO("y_p", [SEQ, D]); y_s = DO("y_s", [128, D])
    nkp = DO("nkp", [4, 128, 128]); nvp = DO("nvp", [4, 128, 128])
    nhp = DO("nhp", [4, 4, 128, 128]); ngp = DO("ngp", [4, 2, 128, 128])
    nks = DO("nks", [4, NS, 128, 128]); nvs = DO("nvs", [4, NS, 128, 128])
    nhs = DO("nhs", [4, NS, 4, 128, 128]); ngs = DO("ngs", [4, NS, 2, 128, 128])

    with es:
        fw = Fw(nc, es)

        def S(n, s, dt=F32):
            return es.enter_context(nc.sbuf_tensor(n, list(s), dt))

        def f32(ap):
            return ap.bitcast(F32)

        def mm(out, lhsT, rhs, st, sp, R, W):
            return fw.op("pe", lambda e: e.matmul(out, lhsT, rhs, start=st, stop=sp), R, W)

        def V(fn, R, W):
            return fw.op("dve", fn, R, W)

        def A(fn, R, W):
            return fw.op("act", fn, R, W)

        def G(fn, R, W):
            return fw.op("pool", fn, R, W)

        def act(out, in_, func, R, W, bias=None, scale=None, accum=None):
            kw = {}
            if bias is not None:
                kw["bias"] = bias
            if scale is not None:
                kw["scale"] = scale
            if accum is not None:
                kw["accum_out"] = accum
            return A(lambda e: e.activation(out=out, in_=in_, func=func, **kw), R, W)

        def tsc(out, in0, s1, s2, op0, op1, R, W, eng="dve"):
            if s2 is None:
                return fw.op(eng, lambda e: e.tensor_scalar(out=out, in0=in0, scalar1=s1, scalar2=None, op0=op0), R, W)
            return fw.op(eng, lambda e: e.tensor_scalar(out=out, in0=in0, scalar1=s1, scalar2=s2, op0=op0, op1=op1), R, W)

        def tt(out, in0, in1, op, R, W, eng="dve"):
            return fw.op(eng, lambda e: e.tensor_tensor(out=out, in0=in0, in1=in1, op=op), R, W)

        def stt(out, in0, sc, in1, op0, op1, R, W, accum=None):
            if accum is None:
                return V(lambda e: e.scalar_tensor_tensor(out=out, in0=in0, scalar=sc, in1=in1, op0=op0, op1=op1), R, W)
            return V(lambda e: e.scalar_tensor_tensor(out=out, in0=in0, scalar=sc, in1=in1, op0=op0, op1=op1,
                                                      accum_out=accum), R, W)

        cp_rr = [0]

        def cp(out, in_, R, W):
            cp_rr[0] ^= 1
            if cp_rr[0]:
                return A(lambda e: e.copy(out, in_), R, W)
            return V(lambda e: e.tensor_copy(out, in_), R, W)

        dq_rr = [0]

        def ld(out, in_, W, R=()):
            dq_rr[0] ^= 1
            return fw.dma("sp" if dq_rr[0] else "act", lambda e: e.dma_start(out=out, in_=in_), R, W)

        def ldr(out, in_, W, R=()):
            return fw.dma("pool", lambda e: e.dma_start(out=out, in_=in_), R, W)

        def st(out, in_, R):
            return fw.dma("sp", lambda e: e.dma_start(out=out, in_=in_), R, (), out=True)

        PA = es.enter_context(nc.psum_tensor("PA", [128, 2048], F32))
        PB = es.enter_context(nc.psum_tensor("PB", [128, 2048], F32))
        pbuf = [fw.buf("pb%d" % i) for i in range(8)]
        pb_rr = [0]

        def bank(k):
            t = PA if k < 4 else PB
            return t[:, (k % 4) * 512:(k % 4) * 512 + 512]

        def nb():
            k = pb_rr[0] % 8
            pb_rr[0] += 1
            return bank(k), pbuf[k]

        identR = S("identR", [128, 128], F32R); ident = f32(identR[:]); b_ident = fw.buf()
        permR = S("permR", [128, 128], F32R); b_perm = fw.buf()
        band = S("band", [128, 2, 256]); b_band = fw.buf()
        caus = S("caus", [128, 128], U32); b_caus = fw.buf()
        reset = S("reset", [128, 256]); b_reset = fw.buf()
        misc = S("misc", [128, 64]); b_misc = fw.buf()
        zeros = S("zeros", [128, 128]); b_zeros = fw.buf()
        lbt = S("lbt", [128, 16]); lbw = S("lbw", [128, 16]); lbs = S("lbs", [128, 4]); b_lb = fw.buf()
        lowb = S("lowb", [128, 16]); oml = S("oml", [128, 16])
        nba = S("nba", [128, 8]); b_nba = fw.buf()
        x_g = S("x_g", [128, 2, D]); b_x = [fw.buf(), fw.buf()]
        xT = S("xT", [128, 16, 256], BF16); b_xT = fw.buf()
        xrt = S("xrt", [128, D], F32R); b_xr = fw.buf()
        NRING = 4
        ring = S("ring", [128, NRING, 4096], BF16); b_ring = [fw.buf() for _ in range(NRING)]
        big = S("big", [128, 4, D], F32R); b_big = [fw.buf() for _ in range(4)]
        lng = S("lng", [128, D]); lnb = S("lnb", [128, D]); b_lng = fw.buf(); b_lnb = fw.buf()
        tm = S("tm", [128, 2, 2176], F32R); b_tm = [fw.buf(), fw.buf()]
        cs = S("cs", [128, 2, 256]); b_cs = fw.buf()
        sink_bc = S("sink_bc", [128, 16]); b_sink = fw.buf()
        nwb = S("nwb", [128, 2, 512]); b_nwb = fw.buf()
        wa2s = S("wa2s", [16, 256], F32R); b_wa2 = fw.buf()
        acT = S("acT", [16, 256], F32R); b_acT = fw.buf()
        kTd = S("kTd", [128, 2, 384], F32R); b_kTd = fw.buf()
        vbuf = S("vbuf", [128, 3, 128], F32R); b_vbuf = fw.buf()
        kcar = S("kcar", [128, 4, 2, 128], F32R); vcar = S("vcar", [128, 4, 128], F32R)
        b_kcar = [fw.buf() for _ in range(4)]; b_vcar = [fw.buf() for _ in range(4)]
        qraw = S("qraw", [128, 256], F32R); b_qraw = fw.buf()
        qrot = S("qrot", [128, 256], F32R); b_qrot = fw.buf()
        t1 = S("t1", [128, 256]); b_t1 = fw.buf()
        ssb = S("ssb", [128, 256]); b_ssb = fw.buf()
        psb = S("psb", [128, 256]); b_psb = fw.buf()
        pT = S("pT", [128, 2, 128], F32R); b_pT = fw.buf()
        sm = S("sm", [128, 16]); b_sm = fw.buf()
        ktok = S("ktok", [128, 2, 128]); b_ktok = fw.buf()
        rt = S("rt", [128, 10, 256]); b_rt = [fw.buf() for _ in range(10)]
        qt = S("qt", [128, 256], F32R); kt = S("kt", [128, 256], F32R); b_qt = fw.buf(); b_kt = fw.buf()
        kh = S("kh", [128, 2, 128], F32R); b_kh = fw.buf()
        attm = S("attm", [128, 128], F32R); b_attm = fw.buf()
        atf = S("atf", [128, 128]); b_atf = fw.buf()
        Ssc = S("Ssc", [128, 128], F32R); b_Ssc = fw.buf()
        cols = S("cols", [128, 8]); b_cols = fw.buf()
        Sb = S("Sb", [128, 4, 4, 128]); b_Sb = [[fw.buf() for _ in range(4)] for _ in range(4)]
        Sc = S("Sc", [128, 4, 2, 128]); b_Sc = [[fw.buf() for _ in range(2)] for _ in range(4)]
        b_ob = [[b_big[0], b_big[1]], [b_big[0], b_big[1]]]

        def obv(mxr, t):
            return big[:, t, 1024 + mxr * 512:1536 + mxr * 512]
        sS = rt[:, 5, 0:128]; b_sS = b_rt[5]
        sSn = rt[:, 6, 0:128]; b_sSn = b_rt[6]
        sSr = big[:, 3, 1280:1408]; b_sSr = b_big[3]
        skd = rt[:, 7, :]; b_skd = b_rt[7]
        b_skT = b_big[2]
        b_sv = b_big[1]
        sv = big[:, 1, 0:NS * 128].rearrange("p (b n) -> p b n", b=NS)

        def skTv(b, kv):
            lo = (b % 4) * 260 + kv * 130
            return big[:, 2 + b // 4, lo:lo + 130]
        slh = S("slh", [128, 16], F32R); b_slh = fw.buf()
        osel = S("osel", [16, 64]); b_osel = fw.buf()
        pq = S("pq", [128, 256], BF16); b_pq = fw.buf()
        tv = S("tv", [128, 16, 16]); ti = S("ti", [128, 16, 16], U32); b_tv = fw.buf(); b_ti = fw.buf()
        tif = S("tif", [128, 16, 16]); b_tif = fw.buf()
        bv = S("bv", [128, 8, 16]); bi = S("bi", [128, 8, 16], U32); b_bv = fw.buf(); b_bi = fw.buf()
        bt = S("bt", [128, 6, 128]); bti = S("bti", [128, 2, 128], U32); b_bt = fw.buf()
        idxf = S("idxf", [128, 128]); gate = S("gate", [128, 2, 128]); b_idxf = fw.buf(); b_gate = fw.buf()
        idxT = S("idxT", [128, 2, 128], I32); gateT = S("gateT", [128, 2, 128]); b_idxT = fw.buf(); b_gateT = fw.buf()
        hT = S("hT", [128, 128]); wT = S("wT", [128, 128]); b_hT = fw.buf(); b_wT = fw.buf()
        wz = S("wz", [128, 4, 255], F32R); b_wz = [fw.buf() for _ in range(4)]
        qz = wz; b_qz = b_wz
        lnst = S("lnst", [128, 8]); b_lnst = fw.buf()

        ldr(identR[:], c_ident, [b_ident]); ldr(permR[:], c_perm, [b_perm])
        ld(band[:], c_band.rearrange("a p k -> p a k"), [b_band]); ld(caus[:], c_caus, [b_caus])
        ld(reset[:], c_reset, [b_reset]); ld(misc[:], c_misc, [b_misc])
        ld(lbt[:], lbT, [b_lb]); ld(nba[:], nbaT, [b_nba])
        G(lambda e: e.memset(zeros[:], 0.0), (), [b_zeros])
        V(lambda e: e.tensor_copy(wz[:].rearrange("p a b -> p (a b)"), zeros[:, 0:1].to_broadcast([128, 4 * 255])), [b_zeros], b_wz)
        G(lambda e: e.memset(Sb[:], 0.0), (), [b for r in b_Sb for b in r])
        G(lambda e: e.memset(Sc[:], 0.0), (), [b for r in b_Sc for b in r])
        V(lambda e: e.tensor_copy(kcar[:].rearrange("p a b c -> p (a b c)"), zeros[:, 0:1].to_broadcast([128, 1024])), [b_zeros], b_kcar)
        V(lambda e: e.tensor_copy(vcar[:].rearrange("p a b -> p (a b)"), zeros[:, 0:1].to_broadcast([128, 512])), [b_zeros], b_vcar)
        iota16 = misc[:, 0:16]
        hmask = misc[:, 16:18]
        kvs = misc[:, 18:20]
        tsc(nba[:], nba[:], -1.0, None, ALU.mult, None, [b_nba], [b_nba])
        act(lbw[:], lbt[:], AF.Exp, [b_lb], [b_lb])
        V(lambda e: e.reduce_sum(out=lbs[:], in_=lbw[:].rearrange("p (h l) -> p h l", l=4), axis=AX.X), [b_lb], [b_lb])
        V(lambda e: e.reciprocal(out=lbs[:], in_=lbs[:]), [b_lb], [b_lb])
        tt(lbw[:].rearrange("p (h l) -> p h l", l=4), lbw[:].rearrange("p (h l) -> p h l", l=4),
           lbs[:].unsqueeze(2).to_broadcast([128, 4, 4]), ALU.mult, [b_lb], [b_lb])
        lw3 = lbw[:].rearrange("p (h l) -> p h l", l=4)
        lo3 = lowb[:].rearrange("p (h l) -> p h l", l=4)
        G(lambda e: e.memset(lowb[:], 0.0), (), [b_lb])
        for l in range(1, 4):
            tt(lo3[:, :, l:l + 1], lo3[:, :, l - 1:l], lw3[:, :, l:l + 1], ALU.add, [b_lb], [b_lb])
        tsc(oml[:], lowb[:], -1.0, 1.0, ALU.mult, ALU.add, [b_lb], [b_lb])

        for l in range(DEPTH):
            for b in range(NS):
                fw.dma("act", lambda e, l=l, b=b: e.dma_start(out=nks[l, b, 0:127, :], in_=ck[l, b, 1:128, :]), (), (), out=True)
                fw.dma("act", lambda e, l=l, b=b: e.dma_start(out=nvs[l, b, 0:127, :], in_=cv[l, b, 1:128, :]), (), (), out=True)

        g_ring = [[fw.buf(), fw.buf()] for _ in range(NRING)]
        gslots = [(big[:, i, :], b_big[i]) for i in range(4)]
        for t_ in range(2):
            gslots.append((tm[:, t_, 0:2048], b_tm[t_]))
        NSLOT = len(gslots)
        ucount = [0]

        def load_unit(l, u):
            slot = ucount[0] % NRING
            ucount[0] += 1
            ldr(ring[:, slot, :], wst[l, u], [b_ring[slot]])
            return ring[:, slot, :].rearrange("p (c n) -> p c n", n=256), b_ring[slot]

        def transpose_to_xT(src_f32, bsrc, T):
            for t in range(T):
                for c4 in range(4):
                    pk, pkb = nb()
                    for cc in range(4):
                        c = c4 * 4 + cc
                        fw.op("pe", lambda e, t=t, c=c, cc=cc, pk=pk: e.transpose(pk[:, cc * 128:(cc + 1) * 128],
                                                                               src_f32(t)[:, c * 128:(c + 1) * 128], ident),
                              [bsrc[t], b_ident], [pkb])
                    cp(xT[:, c4 * 4:(c4 + 1) * 4, t * 128:(t + 1) * 128],
                       pk.rearrange("p (c n) -> p c n", n=128), [pkb], [b_xT])

        def fm_block(wv, wb, blk, N, ncols=128):
            pk, pkb = nb()
            for c in range(16):
                mm(pk[0:ncols, 0:N], wv[:, c, blk * 128:blk * 128 + ncols], xT[:, c, 0:N], c == 0, c == 15,
                   [wb, b_xT], [pkb])
            return pk, pkb

        def layer_norm(t, gl, bl, l):
            xs = x_g[:, t, :]
            bx = b_x[t]
            V(lambda e: e.memset(lnst[:, 0:2], 0.0), (), [b_lnst])
            V(lambda e: e.reduce_sum(out=lnst[:, 0:1], in_=xs, axis=AX.X), [bx], [b_lnst])
            act(big[:, 3, :], xs, AF.Square, [bx], [b_big[3], b_lnst], accum=lnst[:, 1:2])
            tsc(lnst[:, 2:3], lnst[:, 0:1], 1.0 / D, None, ALU.mult, None, [b_lnst], [b_lnst])
            tt(lnst[:, 3:4], lnst[:, 2:3], lnst[:, 2:3], ALU.mult, [b_lnst], [b_lnst])
            stt(lnst[:, 4:5], lnst[:, 1:2], 1.0 / D, lnst[:, 3:4], ALU.mult, ALU.subtract, [b_lnst], [b_lnst])
            tsc(lnst[:, 5:6], lnst[:, 4:5], 1e-5, None, ALU.add, None, [b_lnst], [b_lnst])
            act(lnst[:, 5:6], lnst[:, 5:6], AF.Sqrt, [b_lnst], [b_lnst])
            V(lambda e: e.reciprocal(out=lnst[:, 5:6], in_=lnst[:, 5:6]), [b_lnst], [b_lnst])
            tsc(xs, xs, lnst[:, 2:3], lnst[:, 5:6], ALU.subtract, ALU.mult, [bx, b_lnst], [bx])
            tt(xs, xs, lng[:], ALU.mult, [bx, b_lng], [bx])
            tt(xs, xs, lnb[:], ALU.add, [bx, b_lnb], [bx])

        def layer_group(l, T, sample, g):
            N = T * 128
            ld(sink_bc[:], sinks[l:l + 1, :].partition_broadcast(128), [b_sink])
            ld(nwb[:, 0, :], hnw[l:l + 1, :].partition_broadcast(128), [b_nwb])
            ld(nwb[:, 1, :], gnw[l:l + 1, :].partition_broadcast(128), [b_nwb])
            ldr(wa2s[:], wa2[l], [b_wa2])
            ld(lng[:], l1g[l:l + 1, :].partition_broadcast(128), [b_lng])
            ld(lnb[:], l1b[l:l + 1, :].partition_broadcast(128), [b_lnb])
            transpose_to_xT(lambda t: x_g[:, t, :], b_x, T)
            for u in range(9):
                wv, wb = load_unit(l, U_TM + u)
                for t in range(T):
                    if u < 8:
                        pk, pkb = nb()
                        for c in range(16):
                            mm(pk[:, 0:256], xT[:, c, t * 128:(t + 1) * 128], wv[:, c, :], c == 0, c == 15, [b_xT, wb], [pkb])
                        cp(tm[:, t, u * 256:(u + 1) * 256], pk[:, 0:256], [pkb], [b_tm[t]])
                    else:
                        pk, pkb = nb()
                        for c in range(16):
                            mm(pk[:, 0:128], xT[:, c, t * 128:(t + 1) * 128], wv[:, c, 0:128], c == 0, c == 15, [b_xT, wb], [pkb])
                        cp(tm[:, t, 2048:2176], pk[:, 0:128], [pkb], [b_tm[t]])
                if u == 8:
                    pk, pkb = fm_block(wv, wb, 1, N, 16)
                    cp(acT[:, 0:N], pk[0:16, 0:N], [pkb], [b_acT])
            for t in range(T):
                for lo in (640, 1664):
                    act(tm[:, t, lo:lo + 512], f32(tm[:, t, lo:lo + 512]), AF.Silu, [b_tm[t]], [b_tm[t]])
            wv, wb = load_unit(l, U_FM + 0)
            if not sample:
                V(lambda e: e.tensor_copy(kTd[:, :, 0:128], kcar[:, l, :, :]), [b_kcar[l]], [b_kTd])
                V(lambda e: e.tensor_copy(vbuf[:, 0, :], vcar[:, l, :]), [b_vcar[l]], [b_vbuf])
                for t in range(T):
                    cp(vbuf[:, 1 + t, :], f32(tm[:, t, 0:128]), [b_tm[t]], [b_vbuf])
            for kv in range(2):
                pk, pkb = fm_block(wv, wb, kv, N)
                cp(qraw[:, 0:N], pk[:, 0:N], [pkb], [b_qraw])
                p2, p2b = nb()
                mm(p2[:, 0:N], permR[:], qraw[:, 0:N], True, True, [b_perm, b_qraw], [p2b])
                tt(t1[:, 0:N], f32(qraw[:, 0:N]), cs[:, 0, 0:N], ALU.mult, [b_qraw, b_cs], [b_t1])
                tt(qrot[:, 0:N], p2[:, 0:N], cs[:, 1, 0:N], ALU.mult, [p2b, b_cs], [b_qrot])
                tt(kTd[:, kv, 128:128 + N], t1[:, 0:N], f32(qrot[:, 0:N]), ALU.add, [b_t1, b_qrot], [b_kTd])
            for kv in range(2):
                pk, pkb = nb()
                fw.op("pe", lambda e, kv=kv, pk=pk: e.transpose(pk[:, 0:128], f32(kTd[:, kv, 128 + (T - 1) * 128:128 + T * 128]), ident),
                      [b_kTd, b_ident], [pkb])
                cp(ktok[:, kv, 0:64], pk[:, 0:64], [pkb], [b_ktok])
            if not sample:
                if g == NG - 1:
                    st(nkp[l].rearrange("p (k d) -> p k d", d=64), ktok[:, :, 0:64], [b_ktok])
                    st(nvp[l], f32(tm[:, T - 1, 0:128]), [b_tm[T - 1]])
                V(lambda e: e.tensor_copy(kcar[:, l, :, :], kTd[:, :, 128 + (T - 1) * 128:128 + T * 128]), [b_kTd], [b_kcar[l]])
                V(lambda e: e.tensor_copy(vcar[:, l, :], tm[:, T - 1, 0:128]), [b_tm[T - 1]], [b_vcar[l]])
            else:
                for b in range(NS):
                    ld(skd[:], ckd[l, b], [b_skd])
                    ldr(sv[:, b, :], cvr[l, b], [b_sv])
                    for kv in range(2):
                        pk, pkb = nb()
                        fw.op("pe", lambda e, pk=pk, kv=kv: e.transpose(pk[:, 0:128], skd[:, kv * 128:(kv + 1) * 128], ident), [b_skd, b_ident], [pkb])
                        cp(skTv(b, kv)[:, 0:128], pk[:, 0:128], [pkb], [b_big[2 + b // 4]])
                        cp(skTv(b, kv)[:, 128:129], f32(kTd[:, kv, 128 + b:129 + b]), [b_kTd], [b_big[2 + b // 4]])
                        cp(skTv(b, kv)[:, 129:130], f32(kTd[:, kv, 128 + b:129 + b]), [b_kTd], [b_big[2 + b // 4]])
                for b in range(NS):
                    st(nks[l, b, 127:128, :].rearrange("p (k d) -> p k d", d=64), ktok[b:b + 1, :, 0:64], [b_ktok])
                    st(nvs[l, b, 127:128, :], f32(tm[b:b + 1, 0, 0:128]), [b_tm[0]])
            for j in range(8):
                if j % 2 == 0:
                    wv, wb = load_unit(l, U_FM + 1 + j // 2)
                pk, pkb = fm_block(wv, wb, j % 2, N)
                cp(qraw[:, 0:N], pk[:, 0:N], [pkb], [b_qraw])
                p2, p2b = nb()
                mm(p2[:, 0:N], permR[:], qraw[:, 0:N], True, True, [b_perm, b_qraw], [p2b])
                tt(t1[:, 0:N], f32(qraw[:, 0:N]), cs[:, 0, 0:N], ALU.mult, [b_qraw, b_cs], [b_t1])
                tt(ssb[:, 0:N], p2[:, 0:N], cs[:, 1, 0:N], ALU.mult, [p2b, b_cs], [b_ssb])
                tt(qrot[:, 0:N], t1[:, 0:N], ssb[:, 0:N], ALU.add, [b_t1, b_ssb], [b_qrot])
                kv = j // 4
                if not sample:
                    for t in range(T):
                        for hh in range(2):
                            h = 2 * j + hh
                            ps = slice(hh * 64, hh * 64 + 64)
                            pk, pkb = nb()
                            mm(pk[:, 0:256], qrot[ps, t * 128:(t + 1) * 128], kTd[ps, kv, t * 128:t * 128 + 256], True, True,
                               [b_qrot, b_kTd], [pkb])
                            mi = 0 if (g == 0 and t == 0) else 1
                            stt(ssb[:], pk[:, 0:256], 0.125, band[:, mi, :], ALU.mult, ALU.add, [pkb, b_band], [b_ssb])
                            V(lambda e: e.reduce_max(out=sm[:, 0:1], in_=ssb[:], axis=AX.X), [b_ssb], [b_sm])
                            tsc(sm[:, 1:2], sm[:, 0:1], sink_bc[:, h:h + 1], -1.0, ALU.max, ALU.mult, [b_sm, b_sink], [b_sm])
                            V(lambda e: e.memset(sm[:, 2:3], 0.0), (), [b_sm])
                            act(psb[:], ssb[:], AF.Exp, [b_ssb, b_sm], [b_psb, b_sm], bias=sm[:, 1:2], accum=sm[:, 2:3])
                            act(sm[:, 3:4], sink_bc[:, h:h + 1], AF.Exp, [b_sink, b_sm], [b_sm], bias=sm[:, 1:2])
                            tt(sm[:, 4:5], sm[:, 2:3], sm[:, 3:4], ALU.add, [b_sm], [b_sm])
                            V(lambda e: e.reciprocal(out=sm[:, 5:6], in_=sm[:, 4:5]), [b_sm], [b_sm])
                            p3, p3b = nb()
                            for blk in range(2):
                                fw.op("pe", lambda e, blk=blk, p3=p3: e.transpose(p3[:, blk * 128:(blk + 1) * 128],
                                                                                  psb[:, blk * 128:(blk + 1) * 128], ident),
                                      [b_psb, b_ident], [p3b])
                            cp(pT[:], p3[:, 0:256].rearrange("p (b n) -> p b n", n=128), [p3b], [b_pT])
                            p4, p4b = nb()
                            for blk in range(2):
                                mm(p4[:, 0:64], pT[:, blk, :], vbuf[:, t + blk, kv * 64:(kv + 1) * 64], blk == 0, blk == 1,
                                   [b_pT, b_vbuf], [p4b])
                            tsc(big[:, t, h * 64:(h + 1) * 64], p4[:, 0:64], sm[:, 5:6], None, ALU.mult, None,
                                [p4b, b_sm], [b_big[t]])
                else:
                    sample_attn_block(l, j)
            for h in range(4):
                wv, wb = load_unit(l, U_FM + 5 + h)
                pq_, pqb = fm_block(wv, wb, 0, N)
                pf_, pfb = fm_block(wv, wb, 1, N)
                R = rt
                act(R[:, 0, 0:N], pq_[:, 0:N], AF.Silu, [pqb], [b_rt[0]])
                act(R[:, 1, 0:N], pf_[:, 0:N], AF.Sigmoid, [pfb], [b_rt[1]])
                tsc(R[:, 1, 0:N], R[:, 1, 0:N], oml[:, h * 4 + l:h * 4 + l + 1], lowb[:, h * 4 + l:h * 4 + l + 1],
                    ALU.mult, ALU.add, [b_rt[1], b_lb], [b_rt[1]])
                tsc(R[:, 2, 0:N], R[:, 1, 0:N], -1.0, 1.0, ALU.mult, ALU.add, [b_rt[1]], [b_rt[2]])
                if sample:
                    sample_rec(l, h, None, R[:, 0, :], b_rt[0], R[:, 1, :], b_rt[1], R[:, 2, :], b_rt[2], 640 - 512)
                else:
                    tsc(R[:, 3, 0:N], R[:, 1, 0:N], 1e-30, None, ALU.max, None, [b_rt[1]], [b_rt[3]])
                    act(R[:, 3, 0:N], R[:, 3, 0:N], AF.Ln, [b_rt[3]], [b_rt[3]])
                    chunk_rec(l, N, T, R[:, 0, :], b_rt[0], R[:, 2, :], b_rt[2], 3,
                              [(0, 128, Sb[:, l, h, :], b_Sb[l][h], 128 + h * 128, 0, h)], 1.0)
            for j in range(2):
                wv, wb = load_unit(l, U_FM + 9 + j)
                pq_, pqb = fm_block(wv, wb, 0, N)
                pk_, pkb_ = fm_block(wv, wb, 1, N)
                R = rt
                pz, pzb = nb()
                mm(pz[:, 0:N], wa2s[:, j * 128:(j + 1) * 128], acT[:, 0:N], True, True, [b_wa2, b_acT], [pzb])
                act(R[:, 3, 0:N], pz[:, 0:N], AF.Exp, [pzb, b_nba], [b_rt[3]], bias=nba[:, l * 2 + j:l * 2 + j + 1], scale=-1.0)
                act(R[:, 3, 0:N], R[:, 3, 0:N], AF.Ln, [b_rt[3]], [b_rt[3]], bias=1.0)
                tsc(R[:, 3, 0:N], R[:, 3, 0:N], -1.0 / 16.0, None, ALU.mult, None, [b_rt[3]], [b_rt[3]])
                tsc(R[:, 0, 0:N], pq_[:, 0:N], 0.125, None, ALU.mult, None, [pqb], [b_rt[0]])
                cp(R[:, 2, 0:N], pk_[:, 0:N], [pkb_], [b_rt[2]])
                if sample:
                    act(R[:, 1, 0:N], R[:, 3, 0:N], AF.Exp, [b_rt[3]], [b_rt[1]])
                    sample_rec(l, None, j, R[:, 0, :], b_rt[0], R[:, 1, :], b_rt[1], R[:, 2, :], b_rt[2], 1152)
                else:
                    chunk_rec(l, N, T, R[:, 0, :], b_rt[0], R[:, 2, :], b_rt[2], 3,
                              [(0, 64, Sc[0:64, l, j, :], b_Sc[l][j], 1152 + (2 * j) * 128, 1, 2 * j),
                               (64, 64, Sc[64:128, l, j, :], b_Sc[l][j], 1152 + (2 * j + 1) * 128, 1, 2 * j + 1)], 1.0)
            for t in range(T):
                for mxr in range(2):
                    ovw = obv(mxr, t)
                    o3 = ovw.rearrange("p (h d) -> p h d", d=128)
                    bo = b_ob[mxr][t]
                    rsq = rt[:, 4:6, :].rearrange("p a n -> p (a n)")
                    tt(rsq, f32(ovw), f32(ovw), ALU.mult, [bo], [b_rt[4], b_rt[5]])
                    V(lambda e, rsq=rsq: e.reduce_sum(out=sm[:, 8:12], in_=rsq.rearrange("p (h d) -> p h d", d=128), axis=AX.X),
                      [b_rt[4], b_rt[5]], [b_sm])
                    tsc(sm[:, 8:12], sm[:, 8:12], 1.0 / 128.0, 1e-6, ALU.mult, ALU.add, [b_sm], [b_sm])
                    act(sm[:, 8:12], sm[:, 8:12], AF.Sqrt, [b_sm], [b_sm])
                    V(lambda e: e.reciprocal(out=sm[:, 8:12], in_=sm[:, 8:12]), [b_sm], [b_sm])
                    tt(o3, f32(o3), sm[:, 8:12].unsqueeze(2).to_broadcast([128, 4, 128]), ALU.mult, [bo, b_sm], [bo])
                    tt(ovw, f32(ovw), nwb[:, mxr, :], ALU.mult, [bo, b_nwb], [bo])
                    glo = 640 if mxr == 0 else 1664
                    tt(ovw, f32(ovw), f32(tm[:, t, glo:glo + 512]), ALU.mult, [bo, b_tm[t]], [bo])
            transpose_to_xT(lambda t: f32(big[:, t, :]), b_big, T)
            for u in range(8):
                wv, wb = load_unit(l, U_WO + u)
                for t in range(T):
                    pk, pkb = nb()
                    for c in range(16):
                        mm(pk[:, 0:256], xT[:, c, t * 128:(t + 1) * 128], wv[:, c, :], c == 0, c == 15, [b_xT, wb], [pkb])
                    stt(x_g[:, t, u * 256:(u + 1) * 256], x_g[:, t, u * 256:(u + 1) * 256], ALPHA, pk[:, 0:256], ALU.mult, ALU.add,
                        [b_x[t], pkb], [b_x[t]])
            for t in range(T):
                layer_norm(t, lng, lnb, l)
            transpose_to_xT(lambda t: x_g[:, t, :], b_x, T)
            ldr(ring[:, NRING - 1, 0:2048], keysT[l], [b_ring[NRING - 1]])
            peer(l, T, sample)

        def chunk_rec(l, N, T, qv, bq, kv_, bk, lai, heads, _):
            R = rt
            la = R[:, lai, :]
            bla = b_rt[lai]
            bc = R[:, 4, :]; bbc = b_rt[4]
            V(lambda e: e.tensor_tensor_scan(out=bc[:, 0:N], data0=reset[:, 0:N], data1=la[:, 0:N], initial=0.0,
                                             op0=ALU.mult, op1=ALU.add), [b_reset, bla], [bbc])
            for t in range(T):
                lo = t * 128
                tsc(R[:, 5, lo:lo + 128], bc[:, lo:lo + 128], bc[:, lo + 64:lo + 65], None, ALU.subtract, None, [bbc], [b_rt[5]])
                act(R[:, 6, lo:lo + 128], bc[:, lo:lo + 128], AF.Exp, [bbc], [b_rt[6]], bias=bc[:, lo + 127:lo + 128], scale=-1.0)
                act(cols[:, 2 * t:2 * t + 1], bc[:, lo + 64:lo + 65], AF.Exp, [bbc], [b_cols])
                act(cols[:, 2 * t + 1:2 * t + 2], bc[:, lo + 127:lo + 128], AF.Exp, [bbc], [b_cols])
            act(R[:, 7, 0:N], R[:, 5, 0:N], AF.Exp, [b_rt[5]], [b_rt[7]])
            act(R[:, 8, 0:N], R[:, 5, 0:N], AF.Exp, [b_rt[5]], [b_rt[8]], scale=-1.0)
            tt(qt[:, 0:N], qv[:, 0:N], R[:, 7, 0:N], ALU.mult, [bq, b_rt[7]], [b_qt])
            tt(kt[:, 0:N], kv_[:, 0:N], R[:, 8, 0:N], ALU.mult, [bk, b_rt[8]], [b_kt])
            tt(R[:, 9, 0:N], kv_[:, 0:N], R[:, 6, 0:N], ALU.mult, [bk, b_rt[6]], [b_rt[9]])
            for t in range(T):
                lo = t * 128
                pk, pkb = nb()
                fw.op("pe", lambda e, pk=pk, lo=lo: e.transpose(pk[:, 0:128], R[:, 9, lo:lo + 128], ident), [b_rt[9], b_ident], [pkb])
                cp(kh[:, t, :], pk[:, 0:128], [pkb], [b_kh])
            for t in range(T):
                lo = t * 128
                for (p0, pn, Sap, Sbuf_, vlo, mxr, hidx) in heads:
                    ps = slice(p0, p0 + pn)
                    tsc(Ssc[ps, :], Sap, cols[ps, 2 * t:2 * t + 1], None, ALU.mult, None, [Sbuf_, b_cols], [b_Ssc])
                    pk, pkb = nb()
                    mm(pk[:, 0:128], kt[ps, lo:lo + 128], qt[ps, lo:lo + 128], True, True, [b_kt, b_qt], [pkb])
                    V(lambda e, pk=pk: e.select(out=atf[:], mask=caus[:], on_true=pk[:, 0:128], on_false=zeros[:]),
                      [pkb, b_caus, b_zeros], [b_atf])
                    cp(attm[:], atf[:], [b_atf], [b_attm])
                    p2, p2b = nb()
                    mm(p2[:, 0:128], qt[ps, lo:lo + 128], Ssc[ps, :], True, False, [b_qt, b_Ssc], [p2b])
                    mm(p2[:, 0:128], attm[:], tm[:, t, vlo:vlo + 128], False, True, [b_attm, b_tm[t]], [p2b])
                    cp(obv(mxr, t)[:, (hidx % 4) * 128:(hidx % 4) * 128 + 128], p2[:, 0:128], [p2b], [b_ob[mxr][t]])
                    p3, p3b = nb()
                    mm(p3[:, 0:128], kh[:, t, :], tm[:, t, vlo:vlo + 128], True, True, [b_kh, b_tm[t]], [p3b])
                    stt(Sap, Sap, cols[ps, 2 * t + 1:2 * t + 2], p3[ps, 0:128], ALU.mult, ALU.add, [Sbuf_, b_cols, p3b], [Sbuf_])

        def sample_rec(l, h, j, qv, bq, av, ba, kv_, bk, _unused):
            hg = h is not None
            vlo = 128 if hg else 1152
            accs = []
            nh = 1 if hg else 2
            for hh in range(nh):
                accs.append((bank(6 + hh), pbuf[6 + hh]))
            for b in range(NS):
                src = sh[l, b, h] if hg else sg[l, b, j]
                ld(sS[:], src, [b_sS])
                pv_, pvb = bank(b % 6), pbuf[b % 6]
                mm(pv_[:, 0:512], identR[:, b:b + 1].to_broadcast([128, 128]), tm[:, 0, vlo:vlo + 512], True, True,
                   [b_ident, b_tm[0]], [pvb])
                tsc(sSn[:], sS[:], av[:, b:b + 1], None, ALU.mult, None, [b_sS, ba], [b_sSn])
                for hh in range(nh):
                    ps = slice(0, 128) if hg else slice(hh * 64, hh * 64 + 64)
                    hd = h if hg else 2 * j + hh
                    stt(sSn[ps, :], pv_[ps, hd * 128:(hd + 1) * 128], kv_[ps, b:b + 1], sSn[ps, :], ALU.mult, ALU.add,
                        [pvb, bk, b_sSn], [b_sSn])
                dst = nhs[l, b, h] if hg else ngs[l, b, j]
                st(dst, sSn[:], [b_sSn])
                cp(sSr[:], sSn[:], [b_sSn], [b_sSr])
                zi = b % 4
                cp(qz[:, zi, 127:128], qv[:, b:b + 1], [bq], [b_qz[zi]])
                for hh in range(nh):
                    ps = slice(0, 128) if hg else slice(hh * 64, hh * 64 + 64)
                    pk, pkb = accs[hh]
                    mm(pk[:, 0:128], qz[ps, zi, 127 - b:255 - b], sSr[ps, :], b == 0, b == NS - 1, [b_qz[zi], b_sSr], [pkb])
            for hh in range(nh):
                hd = h if hg else 2 * j + hh
                pk, pkb = accs[hh]
                cp(obv(0 if hg else 1, 0)[:, (hd % 4) * 128:(hd % 4) * 128 + 128], pk[:, 0:128], [pkb], [b_ob[0 if hg else 1][0]])

        def sample_attn_block(l, j):
            kv = j // 4
            for b in range(NS):
                tt(slh[:, 0:2], f32(qrot[:, b:b + 1]).to_broadcast([128, 2]), hmask, ALU.mult, [b_qrot, b_misc], [b_slh])
                p1, p1b = nb()
                mm(p1[0:2, 0:130], slh[:, 0:2], skTv(b, kv), True, True, [b_slh, b_big[2 + b // 4]], [p1b])
                tsc(ssb[0:2, 0:129], p1[0:2, 0:129], 0.125, None, ALU.mult, None, [p1b], [b_ssb])
                V(lambda e: e.reduce_max(out=sm[0:2, 0:1], in_=ssb[0:2, 0:129], axis=AX.X), [b_ssb], [b_sm])
                tt(sm[0:2, 6:8], sink_bc[0:2, 2 * j:2 * j + 2], misc[0:2, 20:22], ALU.mult, [b_sink, b_misc], [b_sm])
                V(lambda e: e.reduce_sum(out=sm[0:2, 7:8], in_=sm[0:2, 6:8], axis=AX.X), [b_sm], [b_sm])
                tsc(sm[0:2, 1:2], sm[0:2, 0:1], sm[0:2, 7:8], -1.0, ALU.max, ALU.mult, [b_sm], [b_sm])
                V(lambda e: e.memset(sm[0:2, 2:3], 0.0), (), [b_sm])
                act(psb[0:2, 0:129], ssb[0:2, 0:129], AF.Exp, [b_ssb, b_sm], [b_psb, b_sm], bias=sm[0:2, 1:2], accum=sm[0:2, 2:3])
                act(sm[0:2, 3:4], sm[0:2, 7:8], AF.Exp, [b_sm], [b_sm], bias=sm[0:2, 1:2])
                tt(sm[0:2, 4:5], sm[0:2, 2:3], sm[0:2, 3:4], ALU.add, [b_sm], [b_sm])
                V(lambda e: e.reciprocal(out=sm[0:2, 5:6], in_=sm[0:2, 4:5]), [b_sm], [b_sm])
                p3, p3b = nb()
                fw.op("pe", lambda e, p3=p3: e.transpose(p3[:, 0:2], psb[0:2, 0:128], ident[0:2, 0:2]), [b_psb, b_ident], [p3b])
                cp(pT[:, 0, 0:2], p3[:, 0:2], [p3b], [b_pT])
                p4, p4b = nb()
                mm(p4[0:2, 0:128], pT[:, 0, 0:2], sv[:, b, :], True, True, [b_pT, b_sv], [p4b])
                p5, p5b = nb()
                mm(p5[0:2, 0:128], identR[:, b:b + 1].to_broadcast([128, 2]), tm[:, 0, 0:128], True, True, [b_ident, b_tm[0]], [p5b])
                cp(ssb[0:2, 0:128], p4[0:2, 0:128], [p4b], [b_ssb])
                stt(t1[0:2, 0:128], p5[0:2, 0:128], psb[0:2, 128:129], ssb[0:2, 0:128], ALU.mult, ALU.add, [p5b, b_ssb, b_psb], [b_t1])
                tsc(osel[0:2, :], t1[0:2, kv * 64:(kv + 1) * 64], sm[0:2, 5:6], None, ALU.mult, None, [b_t1, b_sm], [b_osel])
                for hh in range(2):
                    fw.dma("pool", lambda e, b=b, j=j, hh=hh: e.dma_start(out=big[b:b + 1, 0, (2 * j + hh) * 64:(2 * j + hh + 1) * 64],
                                                                      in_=osel[hh:hh + 1, :]), [b_osel], [b_big[0]])

        def peer(l, T, sample):
            keyv = ring[:, NRING - 1, 0:2048].rearrange("p (g n) -> p g n", n=128)
            bkey = b_ring[NRING - 1]
            for u in range(8):
                slot = u % (NRING - 1)
                ldr(ring[:, slot, :], wst[l, U_WQ + u], [b_ring[slot]])
                wv = ring[:, slot, :].rearrange("p (c n) -> p c n", n=256)
                for blk in range(2):
                    hp = u * 2 + blk
                    pk, pkb = fm_block(wv, b_ring[slot], blk, T * 128)
                    cp(pq[:, 0:T * 128], pk[:, 0:T * 128], [pkb], [b_pq])
                    for t in range(T):
                        p2, p2b = nb()
                        mm(p2[:, 0:128], pq[:, t * 128:(t + 1) * 128], keyv[:, hp, :], True, True, [b_pq, bkey], [p2b])
                        cp(big[:, t, hp * 128:(hp + 1) * 128], p2[:, 0:128], [p2b], [b_big[t]])
            for t in range(T):
                sc = f32(big[:, t, :]); sc2 = lnb[:]; sc2w = lnb[:]; cand = lng[:]
                bsc = b_big[t]
                sc3 = sc.rearrange("p (g n) -> p g n", n=128)
                s23 = sc2.rearrange("p (g n) -> p g n", n=128)
                s23w = sc2w.rearrange("p (g n) -> p g n", n=128)
                for hp in range(16):
                    V(lambda e, hp=hp, sc3=sc3: e.max(out=tv[:, hp, 0:8], in_=sc3[:, hp, :]), [bsc], [b_tv])
                    V(lambda e, hp=hp, sc3=sc3: e.max_index(out=ti[:, hp, 0:8], in_max=tv[:, hp, 0:8], in_values=sc3[:, hp, :]), [bsc, b_tv], [b_ti])
                    V(lambda e, hp=hp, sc3=sc3, s23=s23w: e.match_replace(out=s23[:, hp, :], in_to_replace=tv[:, hp, 0:8], in_values=sc3[:, hp, :], imm_value=NEG),
                      [bsc, b_tv], [b_lnb])
                    V(lambda e, hp=hp, s23=s23: e.max(out=tv[:, hp, 8:16], in_=s23[:, hp, :]), [b_lnb], [b_tv])
                    V(lambda e, hp=hp, s23=s23: e.max_index(out=ti[:, hp, 8:16], in_max=tv[:, hp, 8:16], in_values=s23[:, hp, :]), [b_lnb, b_tv], [b_ti])
                V(lambda e: e.tensor_copy(tif[:], ti[:]), [b_ti], [b_tif])
                tv4 = tv[:].rearrange("p (h two) k -> p h two k", two=2)
                tif4 = tif[:].rearrange("p (h two) k -> p h two k", two=2)
                c4 = cand.rearrange("p (h i j) -> p h i j", i=16, j=16)
                tt(c4, tv4[:, :, 0, :].unsqueeze(3).to_broadcast([128, 8, 16, 16]),
                   tv4[:, :, 1, :].unsqueeze(2).to_broadcast([128, 8, 16, 16]), ALU.add, [b_tv], [b_lng])
                c3 = cand.rearrange("p (h n) -> p h n", n=256)
                o3 = sc2.rearrange("p (h n) -> p h n", n=256)
                o3w = sc2w.rearrange("p (h n) -> p h n", n=256)
                for h in range(8):
                    V(lambda e, h=h, c3=c3: e.max(out=bv[:, h, 0:8], in_=c3[:, h, :]), [b_lng], [b_bv])
                    V(lambda e, h=h, c3=c3: e.max_index(out=bi[:, h, 0:8], in_max=bv[:, h, 0:8], in_values=c3[:, h, :]), [b_lng, b_bv], [b_bi])
                    V(lambda e, h=h, c3=c3, o3=o3w: e.match_replace(out=o3[:, h, :], in_to_replace=bv[:, h, 0:8], in_values=c3[:, h, :], imm_value=NEG),
                      [b_lng, b_bv], [b_lnb])
                    V(lambda e, h=h, o3=o3: e.max(out=bv[:, h, 8:16], in_=o3[:, h, :]), [b_lnb], [b_bv])
                    V(lambda e, h=h, o3=o3: e.max_index(out=bi[:, h, 8:16], in_max=bv[:, h, 8:16], in_values=o3[:, h, :]), [b_lnb, b_bv], [b_bi])
                g3 = gate[:, t, :].rearrange("p (h k) -> p h k", k=16)
                tt(g3, bv[:], bv[:, :, 0:1].to_broadcast([128, 8, 16]), ALU.subtract, [b_bv], [b_gate])
                act(gate[:, t, :], gate[:, t, :], AF.Exp, [b_gate], [b_gate])
                V(lambda e, g3=g3: e.reduce_sum(out=sm[:, 8:16], in_=g3, axis=AX.X), [b_gate], [b_sm])
                V(lambda e: e.reciprocal(out=sm[:, 8:16], in_=sm[:, 8:16]), [b_sm], [b_sm])
                tt(g3, g3, sm[:, 8:16].unsqueeze(2).to_broadcast([128, 8, 16]), ALU.mult, [b_gate, b_sm], [b_gate])
                bi2 = bi[:].rearrange("p h k -> p (h k)")
                V(lambda e, bi2=bi2: e.tensor_single_scalar(out=bti[:, 0, :], in_=bi2, scalar=4, op=ALU.logical_shift_right), [b_bi], [b_bt])
                V(lambda e, bi2=bi2: e.tensor_single_scalar(out=bti[:, 1, :], in_=bi2, scalar=15, op=ALU.bitwise_and), [b_bi], [b_bt])
                V(lambda e: e.tensor_copy(bt[:, 0:2, :], bti[:]), [b_bt], [b_bt])
                for which in range(2):
                    posf = bt[:, which, :].rearrange("p (h k) -> p h k", k=16)
                    e4 = cand.rearrange("p (h k i) -> p h k i", k=16, i=16)
                    tt(e4, posf.unsqueeze(3).to_broadcast([128, 8, 16, 16]),
                       iota16.unsqueeze(1).unsqueeze(1).to_broadcast([128, 8, 16, 16]), ALU.is_equal, [b_bt, b_misc], [b_lng])
                    tt(e4, e4, tif4[:, :, which, :].unsqueeze(2).to_broadcast([128, 8, 16, 16]), ALU.mult, [b_lng, b_tif], [b_lng])
                    V(lambda e, which=which, e4=e4: e.reduce_sum(out=bt[:, 2 + which, :].rearrange("p (h k) -> p h k", k=16), in_=e4, axis=AX.X),
                      [b_lng], [b_bt])
                stt(idxf[:], bt[:, 2, :], 128.0, bt[:, 3, :], ALU.mult, ALU.add, [b_bt], [b_idxf])
                tsc(idxf[:], idxf[:], float(l * 16384), None, ALU.add, None, [b_idxf], [b_idxf])
                pk, pkb = nb()
                fw.op("pe", lambda e, pk=pk: e.transpose(pk[:, 0:128], idxf[:], ident), [b_idxf, b_ident], [pkb])
                V(lambda e, pk=pk, t=t: e.tensor_copy(idxT[:, t, :], pk[:, 0:128]), [pkb], [b_idxT])
                pk2, pk2b = nb()
                fw.op("pe", lambda e, pk2=pk2, t=t: e.transpose(pk2[:, 0:128], gate[:, t, :], ident), [b_gate, b_ident], [pk2b])
                cp(gateT[:, t, :], pk2[:, 0:128], [pk2b], [b_gateT])
            ld(lng[:], l2g[l:l + 1, :].partition_broadcast(128), [b_lng])
            ld(lnb[:], l2b[l:l + 1, :].partition_broadcast(128), [b_lnb])
            for t in range(T):
                ntok = NS if sample else 128
                xr = xrt[:]
                cp(xr, x_g[:, t, :], [b_x[t]], [b_xr])
                G(lambda e: e.memset(hT[:], 0.0), (), [b_hT])
                for tok in range(ntok):
                    gsl, gbuf = gslots[tok % NSLOT]
                    fw.dma("pool", lambda e, gsl=gsl, tok=tok, t=t: e.indirect_dma_start(
                        out=gsl, out_offset=None, in_=pu,
                        in_offset=bass.IndirectOffsetOnAxis(ap=idxT[:, t, tok:tok + 1], axis=0)), [b_idxT], [gbuf])
                    PX = PA if tok % 2 == 0 else PB
                    pxb = pbuf[0:4] if tok % 2 == 0 else pbuf[4:8]
                    for c in range(4):
                        mm(PX[:, c * 512:(c + 1) * 512], identR[:, tok:tok + 1].to_broadcast([128, 128]), xr[:, c * 512:(c + 1) * 512],
                           True, True, [b_ident, b_xr], [pxb[c]])
                    stt(gsl, f32(gsl), 1.0, PX[:, :], ALU.mult, ALU.mult, [gbuf] + pxb, [gbuf, b_hT],
                        accum=hT[:, tok:tok + 1])
                act(wT[:], hT[:], AF.Gelu, [b_hT], [b_wT])
                tt(wT[:], wT[:], gateT[:, t, :], ALU.mult, [b_wT, b_gateT], [b_wT])
                for tok in range(ntok):
                    gsl, gbuf = gslots[tok % NSLOT]
                    fw.dma("pool", lambda e, gsl=gsl, tok=tok, t=t: e.indirect_dma_start(
                        out=gsl, out_offset=None, in_=pv,
                        in_offset=bass.IndirectOffsetOnAxis(ap=idxT[:, t, tok:tok + 1], axis=0)), [b_idxT], [gbuf])
                    zi = tok % 4
                    cp(wz[:, zi, 127:128], wT[:, tok:tok + 1], [b_wT], [b_wz[zi]])
                    for c in range(4):
                        mm(PA[:, c * 512:(c + 1) * 512], wz[:, zi, 127 - tok:255 - tok], gsl[:, c * 512:(c + 1) * 512],
                           tok == 0, tok == ntok - 1, [b_wz[zi], gbuf], [pbuf[c]])
                for c in range(4):
                    stt(x_g[:, t, c * 512:(c + 1) * 512], x_g[:, t, c * 512:(c + 1) * 512], ALPHA, PA[:, c * 512:(c + 1) * 512],
                        ALU.mult, ALU.add, [b_x[t], pbuf[c]], [b_x[t]])
                layer_norm(t, lng, lnb, l)

        groups = [(False, g) for g in range(NG)] + [(True, 0)]
        for (sample, g) in groups:
            T = 1 if sample else 2
            if sample:
                ld(x_g[:, 0, :], xsm, [b_x[0]])
                ld(cs[:, :, 0:128], c_cs[:, :, SEQ:SEQ + 128], [b_cs])
            else:
                for t in range(2):
                    ld(x_g[:, t, :], xp[g * 256 + t * 128:g * 256 + (t + 1) * 128, :], [b_x[t]])
                ld(cs[:], c_cs[:, :, g * 256:(g + 1) * 256], [b_cs])
            for l in range(DEPTH):
                layer_group(l, T, sample, g)
            if sample:
                st(y_s, x_g[:, 0, :], [b_x[0]])
            else:
                for t in range(2):
                    st(y_p[g * 256 + t * 128:g * 256 + (t + 1) * 128, :], x_g[:, t, :], [b_x[t]])
        for l in range(DEPTH):
            for h in range(4):
                st(nhp[l, h], Sb[:, l, h, :], [b_Sb[l][h]])
            for j in range(2):
                st(ngp[l, j], Sc[:, l, j, :], [b_Sc[l][j]])
        fw.finish()
        print("ops recorded:", fw.ninst, flush=True)
        fw.emit_all()
    return nc


_CACHE = {}


def _consts():
    ident = np.eye(128, dtype=np.float32)
    perm = np.zeros((128, 128), np.float32)
    for m in range(128):
        d = m % 64
        if d < 32:
            perm[m + 32, m] = -1.0
        else:
            perm[m - 32, m] = 1.0
    half = 32
    inv = (10000.0 ** (-np.arange(half, dtype=np.float32) / half)).astype(np.float32)
    pos = np.concatenate([np.arange(SEQ, dtype=np.float32), np.full((128,), PAST, np.float32)])
    ang = pos[None, :] * inv[:, None]
    cos = np.cos(ang).astype(np.float32); sin = np.sin(ang).astype(np.float32)
    cs = np.zeros((128, 2, SEQ + 128), np.float32)
    for p in range(128):
        cs[p, 0] = cos[p % 32]; cs[p, 1] = sin[p % 32]
    i = np.arange(128)[:, None]; jj = np.arange(256)[None, :]
    diff = i + 128 - jj
    bandv = (diff >= 0) & (diff <= 128)
    band = np.zeros((2, 128, 256), np.float32)
    band[1] = np.where(bandv, 0.0, NEG)
    band[0] = np.where(bandv & (jj >= 128), 0.0, NEG)
    caus = (np.arange(128)[:, None] <= np.arange(128)[None, :]).astype(np.uint32)
    reset = np.ones((128, 256), np.float32); reset[:, 0] = 0.0; reset[:, 128] = 0.0
    misc = np.zeros((128, 64), np.float32)
    misc[:, 0:16] = np.arange(16, dtype=np.float32)[None, :]
    misc[0:64, 16] = 1.0; misc[64:128, 17] = 1.0
    misc[0, 20] = 1.0; misc[1, 21] = 1.0
    return dict(c_ident=ident, c_perm=perm, c_cs=cs, c_band=band, c_caus=caus, c_reset=reset, c_misc=misc)


def _weight_stream(w_in, w_out, peer_wq):
    L = w_in.shape[0]
    units = np.zeros((L, NU, 2048, 256), np.float32)
    tmcols = np.concatenate([np.arange(1152, 1280), np.arange(2304, 2816), np.arange(2816, 3328),
                             np.arange(3840, 4352), np.arange(4352, 4864)])
    for l in range(L):
        W = w_in[l]
        tmw = W[:, tmcols]
        for u in range(8):
            units[l, U_TM + u] = tmw[:, u * 256:(u + 1) * 256]
        units[l, U_TM + 8, :, 0:128] = tmw[:, 2048:2176]
        units[l, U_TM + 8, :, 128:144] = W[:, 4864:4880]
        k0 = W[:, 1024:1088]; k1 = W[:, 1088:1152]
        units[l, U_FM + 0] = np.concatenate([k0, k0, k1, k1], 1)
        for u in range(4):
            units[l, U_FM + 1 + u] = W[:, u * 256:(u + 1) * 256]
        for h in range(4):
            units[l, U_FM + 5 + h] = np.concatenate([W[:, 1280 + h * 128:1280 + (h + 1) * 128],
                                                     W[:, 1792 + h * 128:1792 + (h + 1) * 128]], 1)
        for j in range(2):
            units[l, U_FM + 9 + j] = np.concatenate([W[:, 3328 + j * 128:3328 + (j + 1) * 128],
                                                     W[:, 3584 + j * 128:3584 + (j + 1) * 128]], 1)
        for u in range(8):
            units[l, U_WO + u] = w_out[l][:, u * 256:(u + 1) * 256]
            units[l, U_WQ + u] = peer_wq[l][:, u * 256:(u + 1) * 256]
    units = units.reshape(L, NU, 16, 128, 256).transpose(0, 1, 3, 2, 4).reshape(L, NU, 128, 4096)
    return np.ascontiguousarray(units)


def kernel(x_prompt, x_sample, cache_k, cache_v, state_hgrn, state_gla, w_in, w_out, attn_sinks,
           hgrn_norm_w, lb_logits, gla_wa2, gla_ba, gla_norm_w, ln1_g, ln1_b, ln2_g, ln2_b,
           peer_wq, peer_keys, peer_u, peer_v):
    f = lambda a: np.ascontiguousarray(np.asarray(a, dtype=np.float32))
    x_prompt = f(x_prompt); x_sample = f(x_sample); cache_k = f(cache_k); cache_v = f(cache_v)
    state_hgrn = f(state_hgrn); state_gla = f(state_gla)
    if "nc" not in _CACHE:
        _CACHE["nc"] = build_program()
    nc = _CACHE["nc"]
    consts = _consts()
    wst = _weight_stream(f(w_in), f(w_out), f(peer_wq))
    pu = f(peer_u).reshape(4 * 16384, D); pv = f(peer_v).reshape(4 * 16384, D)
    keysT = np.ascontiguousarray(f(peer_keys).reshape(4, 16, 128, 128).transpose(0, 3, 1, 2).reshape(4, 128, 2048))
    lbT = np.ascontiguousarray(f(lb_logits).reshape(4, 4, 128).transpose(2, 1, 0).reshape(128, 16))
    nbaT = np.ascontiguousarray(f(gla_ba).reshape(4, 2, 128).transpose(2, 0, 1).reshape(128, 8))
    shared = dict(wst=wst, pu=pu, pv=pv, keysT=keysT, sinks=f(attn_sinks), hnw=f(hgrn_norm_w).reshape(4, 512),
                  gnw=f(gla_norm_w).reshape(4, 512), lbT=lbT, wa2=f(gla_wa2), nbaT=nbaT,
                  l1g=f(ln1_g), l1b=f(ln1_b), l2g=f(ln2_g), l2b=f(ln2_b), **consts)
    in_maps = []
    for c in range(NCORES):
        sq = c % 4
        sb = slice(NS * c, NS * c + NS)
        xs = np.zeros((128, D), np.float32); xs[0:NS] = x_sample[sb, 0, :]
        ckc = cache_k[:, sb].reshape(4, NS, 128, 128)
        ckd = np.concatenate([ckc[..., 0:64], ckc[..., 0:64], ckc[..., 64:128], ckc[..., 64:128]], -1)
        cvc = np.ascontiguousarray(cache_v[:, sb].reshape(4, NS, 128, 128))
        m = dict(shared)
        m.update(xp=x_prompt[sq], xsm=xs, ck=np.ascontiguousarray(ckc), ckd=np.ascontiguousarray(ckd),
                 cv=cvc, cvr=cvc,
                 sh=np.ascontiguousarray(state_hgrn[:, sb]), sg=np.ascontiguousarray(state_gla[:, sb].reshape(4, NS, 2, 128, 128)))
        in_maps.append(m)
    res = run_bass_kernel_spmd(nc, in_maps, core_ids=list(range(NCORES))).results
    y_p = np.stack([res[c]["y_p"] for c in range(4)]).reshape(4, SEQ, D)
    y_s = np.concatenate([res[c]["y_s"][0:NS] for c in range(NCORES)]).reshape(32, 1, D)
    nkp = np.stack([res[c]["nkp"] for c in range(4)], 1).reshape(4, 4, 128, 2, 64)
    nvp = np.stack([res[c]["nvp"] for c in range(4)], 1).reshape(4, 4, 128, 2, 64)
    nhp = np.stack([res[c]["nhp"] for c in range(4)], 1).reshape(4, 4, 4, 128, 128)
    ngp = np.stack([res[c]["ngp"] for c in range(4)], 1).reshape(4, 4, 4, 64, 128)
    nks = np.concatenate([res[c]["nks"] for c in range(NCORES)], 1).reshape(4, 32, 128, 2, 64)
    nvs = np.concatenate([res[c]["nvs"] for c in range(NCORES)], 1).reshape(4, 32, 128, 2, 64)
    nhs = np.concatenate([res[c]["nhs"] for c in range(NCORES)], 1).reshape(4, 32, 4, 128, 128)
    ngs = np.concatenate([res[c]["ngs"] for c in range(NCORES)], 1).reshape(4, 32, 4, 64, 128)
    return tuple(np.ascontiguousarray(a.astype(np.float32)) for a in (y_p, y_s, nkp, nvp, nhp, ngp, nks, nvs, nhs, ngs))
```

```python
import math
import os
import numpy as np
from contextlib import ExitStack
import concourse.bass as bass
import concourse.mybir as mybir
from concourse.bass_utils import run_bass_kernel_spmd

F32 = mybir.dt.float32
F32R = mybir.dt.float32r
I32 = mybir.dt.int32
U32 = mybir.dt.uint32
BF16 = mybir.dt.bfloat16
AF = mybir.ActivationFunctionType
ALU = mybir.AluOpType
AX = mybir.AxisListType

D = 2048
DEPTH = int(os.environ.get("KDEPTH", "4"))
SEQ = 2048
NG = int(os.environ.get("KNG", "8"))
PAST = 16384
ALPHA = (2 * 4) ** 0.25
NU = 36
U_TM, U_FM, U_WO, U_WQ = 0, 9, 20, 28
NEG = -1.0e30
NCORES = 4
NS = 32 // NCORES


class Buf:
    __slots__ = ("name", "w", "r")

    def __init__(self, name):
        self.name = name
        self.w = None
        self.r = []


class Fw:
    ENG = ("pe", "dve", "act", "pool", "sp")

    def __init__(self, nc, es, n_dma_sems=24):
        self.nc = nc
        self.ops = {e: [] for e in self.ENG}
        self.sems = {}
        self.cnt = {}
        self.waited = {e: {} for e in self.ENG}
        for e in self.ENG:
            self.sems[e] = es.enter_context(nc.semaphore("s_" + e))
            self.cnt[e] = 0
        self.dpool = {}
        for q in ("sp", "act", "pool"):
            lst = []
            for i in range(n_dma_sems):
                k = "d_%s_%d" % (q, i)
                self.sems[k] = es.enter_context(nc.semaphore(k))
                self.cnt[k] = 0
                lst.append(k)
            self.dpool[q] = [lst, 0]
        self.nbuf = 0
        self.out_events = []
        self.ninst = 0

    def buf(self, name=None):
        self.nbuf += 1
        return Buf(name or ("b%d" % self.nbuf))

    def _need(self, eng, ev, waits):
        if ev is None:
            return
        k, v = ev
        if self.waited[eng].get(k, 0) >= v:
            return
        if waits.get(k, 0) < v:
            waits[k] = v

    def _deps(self, eng, reads, writes):
        waits = {}
        for b in reads:
            self._need(eng, b.w, waits)
        for b in writes:
            self._need(eng, b.w, waits)
            for ev in b.r:
                self._need(eng, ev, waits)
        return waits

    def _commit(self, ev, reads, writes):
        for b in reads:
            b.r.append(ev)
            if len(b.r) > 48:
                d = {}
                for k, v in b.r:
                    if d.get(k, 0) < v:
                        d[k] = v
                b.r = list(d.items())
        for b in writes:
            b.w = ev
            b.r = []

    def op(self, eng, fn, reads=(), writes=()):
        waits = self._deps(eng, reads, writes)
        for k, v in waits.items():
            self.waited[eng][k] = v
        self.cnt[eng] += 1
        val = self.cnt[eng]
        sem = self.sems[eng]
        wl = [(self.sems[k], v) for k, v in waits.items()]
        self.ninst += 1 + len(wl)

        def emit(e):
            for s, v in wl:
                e.wait_ge(s, v)
            fn(e).then_inc(sem, 1)
        self.ops[eng].append(emit)
        ev = (eng, val)
        self._commit(ev, reads, writes)
        return ev

    def dma(self, q, fn, reads=(), writes=(), out=False):
        lst, idx = self.dpool[q]
        k = lst[idx % len(lst)]
        self.dpool[q][1] = idx + 1
        waits = self._deps(q, reads, writes)
        prev = self.cnt[k]
        if prev > 0 and self.waited[q].get(k, 0) < prev and waits.get(k, 0) < prev:
            waits[k] = prev
        for kk, v in waits.items():
            self.waited[q][kk] = v
        self.cnt[k] = prev + 16
        val = self.cnt[k]
        sem = self.sems[k]
        wl = [(self.sems[kk], v) for kk, v in waits.items()]
        self.ninst += 1 + len(wl)

        def emit(e):
            for s, v in wl:
                e.wait_ge(s, v)
            fn(e).then_inc(sem, 16)
        self.ops[q].append(emit)
        ev = (k, val)
        self._commit(ev, reads, writes)
        if out:
            self.out_events.append(ev)
        return ev

    def inherit(self, dsts, srcs):
        evs = {}
        for b in srcs:
            for ev in ([b.w] if b.w else []) + list(b.r):
                if evs.get(ev[0], 0) < ev[1]:
                    evs[ev[0]] = ev[1]
        for d in dsts:
            d.r.extend(evs.items())

    def finish(self):
        waits = {}
        for ev in self.out_events:
            self._need("sp", ev, waits)
        wl = [(self.sems[k], v) for k, v in waits.items()]

        def emit(e):
            for s, v in wl:
                e.wait_ge(s, v)
        self.ops["sp"].append(emit)

    def emit_all(self):
        nc = self.nc
        ops = self.ops
        with nc.Block() as block:
            @block.tensor
            def _(e):
                for f in ops["pe"]:
                    f(e)

            @block.vector
            def _(e):
                for f in ops["dve"]:
                    f(e)

            @block.scalar
            def _(e):
                for f in ops["act"]:
                    f(e)

            @block.gpsimd
            def _(e):
                for f in ops["pool"]:
                    f(e)

            @block.sync
            def _(e):
                for f in ops["sp"]:
                    f(e)


def build_program():
    nc = bass.Bass("TRN2", target_bir_lowering=False)
    es = ExitStack()

    def DI(n, s, dt=F32):
        return nc.dram_tensor(n, list(s), dt, kind="ExternalInput").ap()

    def DO(n, s, dt=F32):
        return nc.dram_tensor(n, list(s), dt, kind="ExternalOutput").ap()

    xp = DI("xp", [SEQ, D]); xsm = DI("xsm", [128, D])
    wst = DI("wst", [4, NU, 128, 4096], F32)
    pu = DI("pu", [4 * 16384, D], F32R); pv = DI("pv", [4 * 16384, D], F32R)
    keysT = DI("keysT", [4, 128, 16 * 128], F32)
    pub = nc.dram_tensor("pub", [4 * 16384, D], BF16, kind="Internal").ap()
    pvb = nc.dram_tensor("pvb", [4 * 16384, D], BF16, kind="Internal").ap()
    ck = DI("ck", [4, NS, 128, 128]); cv = DI("cv", [4, NS, 128, 128]); cvr = DI("cvr", [4, NS, 128, 128], F32R)
    ckd = DI("ckd", [4, NS, 128, 256])
    sh = DI("sh", [4, NS, 4, 128, 128]); sg = DI("sg", [4, NS, 2, 128, 128])
    sinks = DI("sinks", [4, 16]); hnw = DI("hnw", [4, 512]); gnw = DI("gnw", [4, 512])
    lbT = DI("lbT", [128, 16])
    wa2 = DI("wa2", [4, 16, 256], F32R); nbaT = DI("nbaT", [128, 8])
    l1g = DI("l1g", [4, D]); l1b = DI("l1b", [4, D]); l2g = DI("l2g", [4, D]); l2b = DI("l2b", [4, D])
    c_ident = DI("c_ident", [128, 128], F32R); c_perm = DI("c_perm", [128, 128], F32R)
    c_cs = DI("c_cs", [128, 2, SEQ + 128])
    c_band = DI("c_band", [2, 128, 256]); c_caus = DI("c_caus", [128, 128], U32)
    c_reset = DI("c_reset", [128, 256]); c_misc = DI("c_misc", [128, 64])

    y_p = DO("y_p", [SEQ, D]); y_s = DO("y_s", [128, D])
    nkp = DO("nkp", [4, 128, 128]); nvp = DO("nvp", [4, 128, 128])
    nhp = DO("nhp", [4, 4, 128, 128]); ngp = DO("ngp", [4, 2, 128, 128])
    nks = DO("nks", [4, NS, 128, 128]); nvs = DO("nvs", [4, NS, 128, 128])
    nhs = DO("nhs", [4, NS, 4, 128, 128]); ngs = DO("ngs", [4, NS, 2, 128, 128])

    with es:
        fw = Fw(nc, es)

        def S(n, s, dt=F32):
            return es.enter_context(nc.sbuf_tensor(n, list(s), dt))

        def f32(ap):
            return ap.bitcast(F32)

        def mm(out, lhsT, rhs, st, sp, R, W):
            return fw.op("pe", lambda e: e.matmul(out, lhsT, rhs, start=st, stop=sp), R, W)

        def V(fn, R, W):
            return fw.op("dve", fn, R, W)

        def A(fn, R, W):
            return fw.op("act", fn, R, W)

        def G(fn, R, W):
            return fw.op("pool", fn, R, W)

        def act(out, in_, func, R, W, bias=None, scale=None, accum=None):
            kw = {}
            if bias is not None:
                kw["bias"] = bias
            if scale is not None:
                kw["scale"] = scale
            if accum is not None:
                kw["accum_out"] = accum
            return A(lambda e: e.activation(out=out, in_=in_, func=func, **kw), R, W)

        def tsc(out, in0, s1, s2, op0, op1, R, W, eng="dve"):
            if s2 is None:
                return fw.op(eng, lambda e: e.tensor_scalar(out=out, in0=in0, scalar1=s1, scalar2=None, op0=op0), R, W)
            return fw.op(eng, lambda e: e.tensor_scalar(out=out, in0=in0, scalar1=s1, scalar2=s2, op0=op0, op1=op1), R, W)

        def tt(out, in0, in1, op, R, W, eng="dve"):
            return fw.op(eng, lambda e: e.tensor_tensor(out=out, in0=in0, in1=in1, op=op), R, W)

        def stt(out, in0, sc, in1, op0, op1, R, W, accum=None):
            if accum is None:
                return V(lambda e: e.scalar_tensor_tensor(out=out, in0=in0, scalar=sc, in1=in1, op0=op0, op1=op1), R, W)
            return V(lambda e: e.scalar_tensor_tensor(out=out, in0=in0, scalar=sc, in1=in1, op0=op0, op1=op1,
                                                      accum_out=accum), R, W)

        cp_rr = [0]

        def cp(out, in_, R, W):
            cp_rr[0] ^= 1
            if cp_rr[0]:
                return A(lambda e: e.copy(out, in_), R, W)
            return V(lambda e: e.tensor_copy(out, in_), R, W)

        dq_rr = [0]

        def ld(out, in_, W, R=()):
            dq_rr[0] ^= 1
            return fw.dma("sp" if dq_rr[0] else "act", lambda e: e.dma_start(out=out, in_=in_), R, W)

        def ldr(out, in_, W, R=()):
            return fw.dma("pool", lambda e: e.dma_start(out=out, in_=in_), R, W)

        def st(out, in_, R):
            return fw.dma("sp", lambda e: e.dma_start(out=out, in_=in_), R, (), out=True)

        PA = es.enter_context(nc.psum_tensor("PA", [128, 2048], F32))
        PB = es.enter_context(nc.psum_tensor("PB", [128, 2048], F32))
        pbuf = [fw.buf("pb%d" % i) for i in range(8)]
        pb_rr = [0]

        def bank(k):
            t = PA if k < 4 else PB
            return t[:, (k % 4) * 512:(k % 4) * 512 + 512]

        def nb():
            k = pb_rr[0] % 8
            pb_rr[0] += 1
            return bank(k), pbuf[k]

        identR = S("identR", [128, 128], F32R); ident = f32(identR[:]); b_ident = fw.buf()
        permR = S("permR", [128, 128], F32R); b_perm = fw.buf()
        band = S("band", [128, 2, 256]); b_band = fw.buf()
        caus = S("caus", [128, 128], U32); b_caus = fw.buf()
        reset = S("reset", [128, 256]); b_reset = fw.buf()
        misc = S("misc", [128, 64]); b_misc = fw.buf()
        zeros = S("zeros", [128, 128]); b_zeros = fw.buf()
        lbt = S("lbt", [128, 16]); lbw = S("lbw", [128, 16]); lbs = S("lbs", [128, 4]); b_lb = fw.buf()
        lowb = S("lowb", [128, 16]); oml = S("oml", [128, 16])
        nba = S("nba", [128, 8]); b_nba = fw.buf()
        x_g = S("x_g", [128, 2, D]); b_x = [fw.buf(), fw.buf()]
        xT = S("xT", [128, 16, 256], BF16); b_xT = fw.buf()
        xrt = S("xrt", [128, D], F32R); b_xr = fw.buf()
        NRING = 2
        ring = S("ring", [128, NRING, 4096], BF16); b_ring = [fw.buf() for _ in range(NRING)]
        big = S("big", [128, 4, D], F32R); b_big = [fw.buf() for _ in range(4)]
        lng = S("lng", [128, D]); lnb = S("lnb", [128, D]); b_lng = fw.buf(); b_lnb = fw.buf()
        tm = S("tm", [128, 2, 2176], F32R); b_tm = [fw.buf(), fw.buf()]
        cs = S("cs", [128, 2, 256]); b_cs = fw.buf()
        sink_bc = S("sink_bc", [128, 16]); b_sink = fw.buf()
        nwb = S("nwb", [128, 2, 512]); b_nwb = fw.buf()
        wa2s = S("wa2s", [16, 256], F32R); b_wa2 = fw.buf()
        acT = S("acT", [16, 256], F32R); b_acT = fw.buf()
        kTd = S("kTd", [128, 2, 384], F32R); b_kTd = fw.buf()
        vbuf = S("vbuf", [128, 3, 128], F32R); b_vbuf = fw.buf()
        kcar = S("kcar", [128, 4, 2, 128], F32R); vcar = S("vcar", [128, 4, 128], F32R)
        b_kcar = [fw.buf() for _ in range(4)]; b_vcar = [fw.buf() for _ in range(4)]
        qraw = S("qraw", [128, 256], F32R); b_qraw = fw.buf()
        qrot = S("qrot", [128, 256], F32R); b_qrot = fw.buf()
        t1 = S("t1", [128, 256]); b_t1 = fw.buf()
        ssb = S("ssb", [128, 256]); b_ssb = fw.buf()
        psb = S("psb", [128, 256]); b_psb = fw.buf()
        pT = S("pT", [128, 2, 128], F32R); b_pT = fw.buf()
        sm = S("sm", [128, 16]); b_sm = fw.buf()
        ktok = S("ktok", [128, 2, 128]); b_ktok = fw.buf()
        rt = S("rt", [128, 10, 256]); b_rt = [fw.buf() for _ in range(10)]
        qt = S("qt", [128, 256], F32R); kt = S("kt", [128, 256], F32R); b_qt = fw.buf(); b_kt = fw.buf()
        kh = S("kh", [128, 2, 128], F32R); b_kh = fw.buf()
        attm = S("attm", [128, 128], F32R); b_attm = fw.buf()
        atf = S("atf", [128, 128]); b_atf = fw.buf()
        Ssc = S("Ssc", [128, 128], F32R); b_Ssc = fw.buf()
        cols = S("cols", [128, 8]); b_cols = fw.buf()
        Sb = S("Sb", [128, 4, 4, 128]); b_Sb = [[fw.buf() for _ in range(4)] for _ in range(4)]
        Sc = S("Sc", [128, 4, 2, 128]); b_Sc = [[fw.buf() for _ in range(2)] for _ in range(4)]
        b_ob = [[b_big[0], b_big[1]], [b_big[0], b_big[1]]]

        def obv(mxr, t):
            return big[:, t, 1024 + mxr * 512:1536 + mxr * 512]
        sS = rt[:, 5, 0:128]; b_sS = b_rt[5]
        sSn = rt[:, 6, 0:128]; b_sSn = b_rt[6]
        sSr = big[:, 3, 1280:1408]; b_sSr = b_big[3]
        skd = rt[:, 7, :]; b_skd = b_rt[7]
        b_skT = b_big[2]
        b_sv = b_big[1]
        sv = big[:, 1, 0:NS * 128].rearrange("p (b n) -> p b n", b=NS)

        def skTv(b, kv):
            lo = (b % 4) * 260 + kv * 130
            return big[:, 2 + b // 4, lo:lo + 130]
        slh = S("slh", [128, 16], F32R); b_slh = fw.buf()
        osel = S("osel", [16, 64]); b_osel = fw.buf()
        pq = S("pq", [128, 256], BF16); b_pq = fw.buf()
        tv = S("tv", [128, 16, 16]); ti = S("ti", [128, 16, 16], U32); b_tv = fw.buf(); b_ti = fw.buf()
        tif = S("tif", [128, 16, 16]); b_tif = fw.buf()
        bv = S("bv", [128, 8, 16]); bi = S("bi", [128, 8, 16], U32); b_bv = fw.buf(); b_bi = fw.buf()
        bt = S("bt", [128, 6, 128]); bti = S("bti", [128, 2, 128], U32); b_bt = fw.buf()
        idxf = S("idxf", [128, 128]); gate = S("gate", [128, 2, 128]); b_idxf = fw.buf(); b_gate = fw.buf()
        idxT = S("idxT", [128, 2, 128], I32); gateT = S("gateT", [128, 2, 128]); b_idxT = fw.buf(); b_gateT = fw.buf()
        hT = S("hT", [128, 128]); wT = S("wT", [128, 128]); b_hT = fw.buf(); b_wT = fw.buf()
        wz = S("wz", [128, 4, 255], F32R); b_wz = [fw.buf() for _ in range(4)]
        qz = wz; b_qz = b_wz
        lnst = S("lnst", [128, 8]); b_lnst = fw.buf()

        ldr(identR[:], c_ident, [b_ident]); ldr(permR[:], c_perm, [b_perm])
        ld(band[:], c_band.rearrange("a p k -> p a k"), [b_band]); ld(caus[:], c_caus, [b_caus])
        ld(reset[:], c_reset, [b_reset]); ld(misc[:], c_misc, [b_misc])
        ld(lbt[:], lbT, [b_lb]); ld(nba[:], nbaT, [b_nba])
        G(lambda e: e.memset(zeros[:], 0.0), (), [b_zeros])
        V(lambda e: e.tensor_copy(wz[:].rearrange("p a b -> p (a b)"), zeros[:, 0:1].to_broadcast([128, 4 * 255])), [b_zeros], b_wz)
        G(lambda e: e.memset(Sb[:], 0.0), (), [b for r in b_Sb for b in r])
        G(lambda e: e.memset(Sc[:], 0.0), (), [b for r in b_Sc for b in r])
        V(lambda e: e.tensor_copy(kcar[:].rearrange("p a b c -> p (a b c)"), zeros[:, 0:1].to_broadcast([128, 1024])), [b_zeros], b_kcar)
        V(lambda e: e.tensor_copy(vcar[:].rearrange("p a b -> p (a b)"), zeros[:, 0:1].to_broadcast([128, 512])), [b_zeros], b_vcar)
        iota16 = misc[:, 0:16]
        hmask = misc[:, 16:18]
        kvs = misc[:, 18:20]
        tsc(nba[:], nba[:], -1.0, None, ALU.mult, None, [b_nba], [b_nba])
        act(lbw[:], lbt[:], AF.Exp, [b_lb], [b_lb])
        V(lambda e: e.reduce_sum(out=lbs[:], in_=lbw[:].rearrange("p (h l) -> p h l", l=4), axis=AX.X), [b_lb], [b_lb])
        V(lambda e: e.reciprocal(out=lbs[:], in_=lbs[:]), [b_lb], [b_lb])
        tt(lbw[:].rearrange("p (h l) -> p h l", l=4), lbw[:].rearrange("p (h l) -> p h l", l=4),
           lbs[:].unsqueeze(2).to_broadcast([128, 4, 4]), ALU.mult, [b_lb], [b_lb])
        lw3 = lbw[:].rearrange("p (h l) -> p h l", l=4)
        lo3 = lowb[:].rearrange("p (h l) -> p h l", l=4)
        G(lambda e: e.memset(lowb[:], 0.0), (), [b_lb])
        for l in range(1, 4):
            tt(lo3[:, :, l:l + 1], lo3[:, :, l - 1:l], lw3[:, :, l:l + 1], ALU.add, [b_lb], [b_lb])
        tsc(oml[:], lowb[:], -1.0, 1.0, ALU.mult, ALU.add, [b_lb], [b_lb])

        for l in range(DEPTH):
            for b in range(NS):
                fw.dma("act", lambda e, l=l, b=b: e.dma_start(out=nks[l, b, 0:127, :], in_=ck[l, b, 1:128, :]), (), (), out=True)
                fw.dma("act", lambda e, l=l, b=b: e.dma_start(out=nvs[l, b, 0:127, :], in_=cv[l, b, 1:128, :]), (), (), out=True)

        gsl = S("gsl", [128, 4, D], BF16); b_gsl = [fw.buf() for _ in range(4)]
        wzb = S("wzb", [128, 4, 255], BF16); b_wzb = [fw.buf() for _ in range(4)]
        V(lambda e: e.tensor_copy(wzb[:].rearrange("p a b -> p (a b)"), zeros[:, 0:1].to_broadcast([128, 4 * 255])), [b_zeros], b_wzb)
        b_pub = [fw.buf() for _ in range(4)]; b_pvb = [fw.buf() for _ in range(4)]
        gslots = [(gsl[:, i, :], [b_gsl[i]]) for i in range(4)]
        gslots.append((rt[:, 0:4, :].rearrange("p a n -> p (a n)").bitcast(BF16), b_rt[0:4]))
        gslots.append((rt[:, 4:8, :].rearrange("p a n -> p (a n)").bitcast(BF16), b_rt[4:8]))
        NSLOT = len(gslots)
        ucount = [0]

        def load_unit(l, u):
            slot = ucount[0] % NRING
            ucount[0] += 1
            ldr(ring[:, slot, :], wst[l, u], [b_ring[slot]])
            return ring[:, slot, :].rearrange("p (c n) -> p c n", n=256), b_ring[slot]

        def transpose_to_xT(src_f32, bsrc, T):
            for t in range(T):
                for c4 in range(4):
                    pk, pkb = nb()
                    for cc in range(4):
                        c = c4 * 4 + cc
                        fw.op("pe", lambda e, t=t, c=c, cc=cc, pk=pk: e.transpose(pk[:, cc * 128:(cc + 1) * 128],
                                                                               src_f32(t)[:, c * 128:(c + 1) * 128], ident),
                              [bsrc[t], b_ident], [pkb])
                    cp(xT[:, c4 * 4:(c4 + 1) * 4, t * 128:(t + 1) * 128],
                       pk.rearrange("p (c n) -> p c n", n=128), [pkb], [b_xT])

        def fm_block(wv, wb, blk, N, ncols=128):
            pk, pkb = nb()
            for c in range(16):
                mm(pk[0:ncols, 0:N], wv[:, c, blk * 128:blk * 128 + ncols], xT[:, c, 0:N], c == 0, c == 15,
                   [wb, b_xT], [pkb])
            return pk, pkb

        def layer_norm(t, gl, bl, l):
            xs = x_g[:, t, :]
            bx = b_x[t]
            V(lambda e: e.memset(lnst[:, 0:2], 0.0), (), [b_lnst])
            V(lambda e: e.reduce_sum(out=lnst[:, 0:1], in_=xs, axis=AX.X), [bx], [b_lnst])
            act(big[:, 3, :], xs, AF.Square, [bx], [b_big[3], b_lnst], accum=lnst[:, 1:2])
            tsc(lnst[:, 2:3], lnst[:, 0:1], 1.0 / D, None, ALU.mult, None, [b_lnst], [b_lnst])
            tt(lnst[:, 3:4], lnst[:, 2:3], lnst[:, 2:3], ALU.mult, [b_lnst], [b_lnst])
            stt(lnst[:, 4:5], lnst[:, 1:2], 1.0 / D, lnst[:, 3:4], ALU.mult, ALU.subtract, [b_lnst], [b_lnst])
            tsc(lnst[:, 5:6], lnst[:, 4:5], 1e-5, None, ALU.add, None, [b_lnst], [b_lnst])
            act(lnst[:, 5:6], lnst[:, 5:6], AF.Sqrt, [b_lnst], [b_lnst])
            V(lambda e: e.reciprocal(out=lnst[:, 5:6], in_=lnst[:, 5:6]), [b_lnst], [b_lnst])
            tsc(xs, xs, lnst[:, 2:3], lnst[:, 5:6], ALU.subtract, ALU.mult, [bx, b_lnst], [bx])
            tt(xs, xs, lng[:], ALU.mult, [bx, b_lng], [bx])
            tt(xs, xs, lnb[:], ALU.add, [bx, b_lnb], [bx])

        def layer_group(l, T, sample, g):
            N = T * 128
            ld(sink_bc[:], sinks[l:l + 1, :].partition_broadcast(128), [b_sink])
            ld(nwb[:, 0, :], hnw[l:l + 1, :].partition_broadcast(128), [b_nwb])
            ld(nwb[:, 1, :], gnw[l:l + 1, :].partition_broadcast(128), [b_nwb])
            ldr(wa2s[:], wa2[l], [b_wa2])
            ld(lng[:], l1g[l:l + 1, :].partition_broadcast(128), [b_lng])
            ld(lnb[:], l1b[l:l + 1, :].partition_broadcast(128), [b_lnb])
            transpose_to_xT(lambda t: x_g[:, t, :], b_x, T)
            for u in range(9):
                wv, wb = load_unit(l, U_TM + u)
                for t in range(T):
                    if u < 8:
                        pk, pkb = nb()
                        for c in range(16):
                            mm(pk[:, 0:256], xT[:, c, t * 128:(t + 1) * 128], wv[:, c, :], c == 0, c == 15, [b_xT, wb], [pkb])
                        cp(tm[:, t, u * 256:(u + 1) * 256], pk[:, 0:256], [pkb], [b_tm[t]])
                    else:
                        pk, pkb = nb()
                        for c in range(16):
                            mm(pk[:, 0:128], xT[:, c, t * 128:(t + 1) * 128], wv[:, c, 0:128], c == 0, c == 15, [b_xT, wb], [pkb])
                        cp(tm[:, t, 2048:2176], pk[:, 0:128], [pkb], [b_tm[t]])
                if u == 8:
                    pk, pkb = fm_block(wv, wb, 1, N, 16)
                    cp(acT[:, 0:N], pk[0:16, 0:N], [pkb], [b_acT])
            if (not sample) and g == 0:
                for ch in range(16):
                    r0 = l * 16384 + ch * 1024
                    fw.dma("pool", lambda e, r0=r0: e.dma_start(out=pub[r0:r0 + 1024, :], in_=pu[r0:r0 + 1024, :]), (), [b_pub[l]])
                    fw.dma("pool", lambda e, r0=r0: e.dma_start(out=pvb[r0:r0 + 1024, :], in_=pv[r0:r0 + 1024, :]), (), [b_pvb[l]])
            for t in range(T):
                for lo in (640, 1664):
                    act(tm[:, t, lo:lo + 512], f32(tm[:, t, lo:lo + 512]), AF.Silu, [b_tm[t]], [b_tm[t]])
            wv, wb = load_unit(l, U_FM + 0)
            if not sample:
                V(lambda e: e.tensor_copy(kTd[:, :, 0:128], kcar[:, l, :, :]), [b_kcar[l]], [b_kTd])
                V(lambda e: e.tensor_copy(vbuf[:, 0, :], vcar[:, l, :]), [b_vcar[l]], [b_vbuf])
                for t in range(T):
                    cp(vbuf[:, 1 + t, :], f32(tm[:, t, 0:128]), [b_tm[t]], [b_vbuf])
            for kv in range(2):
                pk, pkb = fm_block(wv, wb, kv, N)
                cp(qraw[:, 0:N], pk[:, 0:N], [pkb], [b_qraw])
                p2, p2b = nb()
                mm(p2[:, 0:N], permR[:], qraw[:, 0:N], True, True, [b_perm, b_qraw], [p2b])
                tt(t1[:, 0:N], f32(qraw[:, 0:N]), cs[:, 0, 0:N], ALU.mult, [b_qraw, b_cs], [b_t1])
                tt(qrot[:, 0:N], p2[:, 0:N], cs[:, 1, 0:N], ALU.mult, [p2b, b_cs], [b_qrot])
                tt(kTd[:, kv, 128:128 + N], t1[:, 0:N], f32(qrot[:, 0:N]), ALU.add, [b_t1, b_qrot], [b_kTd])
            for kv in range(2):
                pk, pkb = nb()
                fw.op("pe", lambda e, kv=kv, pk=pk: e.transpose(pk[:, 0:128], f32(kTd[:, kv, 128 + (T - 1) * 128:128 + T * 128]), ident),
                      [b_kTd, b_ident], [pkb])
                cp(ktok[:, kv, 0:64], pk[:, 0:64], [pkb], [b_ktok])
            if not sample:
                if g == NG - 1:
                    st(nkp[l].rearrange("p (k d) -> p k d", d=64), ktok[:, :, 0:64], [b_ktok])
                    st(nvp[l], f32(tm[:, T - 1, 0:128]), [b_tm[T - 1]])
                V(lambda e: e.tensor_copy(kcar[:, l, :, :], kTd[:, :, 128 + (T - 1) * 128:128 + T * 128]), [b_kTd], [b_kcar[l]])
                V(lambda e: e.tensor_copy(vcar[:, l, :], tm[:, T - 1, 0:128]), [b_tm[T - 1]], [b_vcar[l]])
            else:
                for b in range(NS):
                    ld(skd[:], ckd[l, b], [b_skd])
                    ldr(sv[:, b, :], cvr[l, b], [b_sv])
                    for kv in range(2):
                        pk, pkb = nb()
                        fw.op("pe", lambda e, pk=pk, kv=kv: e.transpose(pk[:, 0:128], skd[:, kv * 128:(kv + 1) * 128], ident), [b_skd, b_ident], [pkb])
                        cp(skTv(b, kv)[:, 0:128], pk[:, 0:128], [pkb], [b_big[2 + b // 4]])
                        cp(skTv(b, kv)[:, 128:129], f32(kTd[:, kv, 128 + b:129 + b]), [b_kTd], [b_big[2 + b // 4]])
                        cp(skTv(b, kv)[:, 129:130], f32(kTd[:, kv, 128 + b:129 + b]), [b_kTd], [b_big[2 + b // 4]])
                for b in range(NS):
                    st(nks[l, b, 127:128, :].rearrange("p (k d) -> p k d", d=64), ktok[b:b + 1, :, 0:64], [b_ktok])
                    st(nvs[l, b, 127:128, :], f32(tm[b:b + 1, 0, 0:128]), [b_tm[0]])
            for j in range(8):
                if j % 2 == 0:
                    wv, wb = load_unit(l, U_FM + 1 + j // 2)
                pk, pkb = fm_block(wv, wb, j % 2, N)
                cp(qraw[:, 0:N], pk[:, 0:N], [pkb], [b_qraw])
                p2, p2b = nb()
                mm(p2[:, 0:N], permR[:], qraw[:, 0:N], True, True, [b_perm, b_qraw], [p2b])
                tt(t1[:, 0:N], f32(qraw[:, 0:N]), cs[:, 0, 0:N], ALU.mult, [b_qraw, b_cs], [b_t1])
                tt(ssb[:, 0:N], p2[:, 0:N], cs[:, 1, 0:N], ALU.mult, [p2b, b_cs], [b_ssb])
                tt(qrot[:, 0:N], t1[:, 0:N], ssb[:, 0:N], ALU.add, [b_t1, b_ssb], [b_qrot])
                kv = j // 4
                if not sample:
                    for t in range(T):
                        for hh in range(2):
                            h = 2 * j + hh
                            ps = slice(hh * 64, hh * 64 + 64)
                            pk, pkb = nb()
                            mm(pk[:, 0:256], qrot[ps, t * 128:(t + 1) * 128], kTd[ps, kv, t * 128:t * 128 + 256], True, True,
                               [b_qrot, b_kTd], [pkb])
                            mi = 0 if (g == 0 and t == 0) else 1
                            stt(ssb[:], pk[:, 0:256], 0.125, band[:, mi, :], ALU.mult, ALU.add, [pkb, b_band], [b_ssb])
                            V(lambda e: e.reduce_max(out=sm[:, 0:1], in_=ssb[:], axis=AX.X), [b_ssb], [b_sm])
                            tsc(sm[:, 1:2], sm[:, 0:1], sink_bc[:, h:h + 1], -1.0, ALU.max, ALU.mult, [b_sm, b_sink], [b_sm])
                            V(lambda e: e.memset(sm[:, 2:3], 0.0), (), [b_sm])
                            act(psb[:], ssb[:], AF.Exp, [b_ssb, b_sm], [b_psb, b_sm], bias=sm[:, 1:2], accum=sm[:, 2:3])
                            act(sm[:, 3:4], sink_bc[:, h:h + 1], AF.Exp, [b_sink, b_sm], [b_sm], bias=sm[:, 1:2])
                            tt(sm[:, 4:5], sm[:, 2:3], sm[:, 3:4], ALU.add, [b_sm], [b_sm])
                            V(lambda e: e.reciprocal(out=sm[:, 5:6], in_=sm[:, 4:5]), [b_sm], [b_sm])
                            p3, p3b = nb()
                            for blk in range(2):
                                fw.op("pe", lambda e, blk=blk, p3=p3: e.transpose(p3[:, blk * 128:(blk + 1) * 128],
                                                                                  psb[:, blk * 128:(blk + 1) * 128], ident),
                                      [b_psb, b_ident], [p3b])
                            cp(pT[:], p3[:, 0:256].rearrange("p (b n) -> p b n", n=128), [p3b], [b_pT])
                            p4, p4b = nb()
                            for blk in range(2):
                                mm(p4[:, 0:64], pT[:, blk, :], vbuf[:, t + blk, kv * 64:(kv + 1) * 64], blk == 0, blk == 1,
                                   [b_pT, b_vbuf], [p4b])
                            tsc(big[:, t, h * 64:(h + 1) * 64], p4[:, 0:64], sm[:, 5:6], None, ALU.mult, None,
                                [p4b, b_sm], [b_big[t]])
                else:
                    sample_attn_block(l, j)
            for h in range(4):
                wv, wb = load_unit(l, U_FM + 5 + h)
                pq_, pqb = fm_block(wv, wb, 0, N)
                pf_, pfb = fm_block(wv, wb, 1, N)
                R = rt
                act(R[:, 0, 0:N], pq_[:, 0:N], AF.Silu, [pqb], [b_rt[0]])
                act(R[:, 1, 0:N], pf_[:, 0:N], AF.Sigmoid, [pfb], [b_rt[1]])
                tsc(R[:, 1, 0:N], R[:, 1, 0:N], oml[:, h * 4 + l:h * 4 + l + 1], lowb[:, h * 4 + l:h * 4 + l + 1],
                    ALU.mult, ALU.add, [b_rt[1], b_lb], [b_rt[1]])
                tsc(R[:, 2, 0:N], R[:, 1, 0:N], -1.0, 1.0, ALU.mult, ALU.add, [b_rt[1]], [b_rt[2]])
                if sample:
                    sample_rec(l, h, None, R[:, 0, :], b_rt[0], R[:, 1, :], b_rt[1], R[:, 2, :], b_rt[2], 640 - 512)
                else:
                    tsc(R[:, 3, 0:N], R[:, 1, 0:N], 1e-30, None, ALU.max, None, [b_rt[1]], [b_rt[3]])
                    act(R[:, 3, 0:N], R[:, 3, 0:N], AF.Ln, [b_rt[3]], [b_rt[3]])
                    chunk_rec(l, N, T, R[:, 0, :], b_rt[0], R[:, 2, :], b_rt[2], 3,
                              [(0, 128, Sb[:, l, h, :], b_Sb[l][h], 128 + h * 128, 0, h)], 1.0)
            for j in range(2):
                wv, wb = load_unit(l, U_FM + 9 + j)
                pq_, pqb = fm_block(wv, wb, 0, N)
                pk_, pkb_ = fm_block(wv, wb, 1, N)
                R = rt
                pz, pzb = nb()
                mm(pz[:, 0:N], wa2s[:, j * 128:(j + 1) * 128], acT[:, 0:N], True, True, [b_wa2, b_acT], [pzb])
                act(R[:, 3, 0:N], pz[:, 0:N], AF.Exp, [pzb, b_nba], [b_rt[3]], bias=nba[:, l * 2 + j:l * 2 + j + 1], scale=-1.0)
                act(R[:, 3, 0:N], R[:, 3, 0:N], AF.Ln, [b_rt[3]], [b_rt[3]], bias=1.0)
                tsc(R[:, 3, 0:N], R[:, 3, 0:N], -1.0 / 16.0, None, ALU.mult, None, [b_rt[3]], [b_rt[3]])
                tsc(R[:, 0, 0:N], pq_[:, 0:N], 0.125, None, ALU.mult, None, [pqb], [b_rt[0]])
                cp(R[:, 2, 0:N], pk_[:, 0:N], [pkb_], [b_rt[2]])
                if sample:
                    act(R[:, 1, 0:N], R[:, 3, 0:N], AF.Exp, [b_rt[3]], [b_rt[1]])
                    sample_rec(l, None, j, R[:, 0, :], b_rt[0], R[:, 1, :], b_rt[1], R[:, 2, :], b_rt[2], 1152)
                else:
                    chunk_rec(l, N, T, R[:, 0, :], b_rt[0], R[:, 2, :], b_rt[2], 3,
                              [(0, 64, Sc[0:64, l, j, :], b_Sc[l][j], 1152 + (2 * j) * 128, 1, 2 * j),
                               (64, 64, Sc[64:128, l, j, :], b_Sc[l][j], 1152 + (2 * j + 1) * 128, 1, 2 * j + 1)], 1.0)
            for t in range(T):
                for mxr in range(2):
                    ovw = obv(mxr, t)
                    o3 = ovw.rearrange("p (h d) -> p h d", d=128)
                    bo = b_ob[mxr][t]
                    rsq = rt[:, 4:6, :].rearrange("p a n -> p (a n)")
                    tt(rsq, f32(ovw), f32(ovw), ALU.mult, [bo], [b_rt[4], b_rt[5]])
                    V(lambda e, rsq=rsq: e.reduce_sum(out=sm[:, 8:12], in_=rsq.rearrange("p (h d) -> p h d", d=128), axis=AX.X),
                      [b_rt[4], b_rt[5]], [b_sm])
                    tsc(sm[:, 8:12], sm[:, 8:12], 1.0 / 128.0, 1e-6, ALU.mult, ALU.add, [b_sm], [b_sm])
                    act(sm[:, 8:12], sm[:, 8:12], AF.Sqrt, [b_sm], [b_sm])
                    V(lambda e: e.reciprocal(out=sm[:, 8:12], in_=sm[:, 8:12]), [b_sm], [b_sm])
                    tt(o3, f32(o3), sm[:, 8:12].unsqueeze(2).to_broadcast([128, 4, 128]), ALU.mult, [bo, b_sm], [bo])
                    tt(ovw, f32(ovw), nwb[:, mxr, :], ALU.mult, [bo, b_nwb], [bo])
                    glo = 640 if mxr == 0 else 1664
                    tt(ovw, f32(ovw), f32(tm[:, t, glo:glo + 512]), ALU.mult, [bo, b_tm[t]], [bo])
            transpose_to_xT(lambda t: f32(big[:, t, :]), b_big, T)
            for u in range(8):
                wv, wb = load_unit(l, U_WO + u)
                for t in range(T):
                    pk, pkb = nb()
                    for c in range(16):
                        mm(pk[:, 0:256], xT[:, c, t * 128:(t + 1) * 128], wv[:, c, :], c == 0, c == 15, [b_xT, wb], [pkb])
                    stt(x_g[:, t, u * 256:(u + 1) * 256], x_g[:, t, u * 256:(u + 1) * 256], ALPHA, pk[:, 0:256], ALU.mult, ALU.add,
                        [b_x[t], pkb], [b_x[t]])
            for t in range(T):
                layer_norm(t, lng, lnb, l)
            transpose_to_xT(lambda t: x_g[:, t, :], b_x, T)
            ldr(ring[:, NRING - 1, 0:2048], keysT[l], [b_ring[NRING - 1]])
            peer(l, T, sample)

        def chunk_rec(l, N, T, qv, bq, kv_, bk, lai, heads, _):
            R = rt
            la = R[:, lai, :]
            bla = b_rt[lai]
            bc = R[:, 4, :]; bbc = b_rt[4]
            V(lambda e: e.tensor_tensor_scan(out=bc[:, 0:N], data0=reset[:, 0:N], data1=la[:, 0:N], initial=0.0,
                                             op0=ALU.mult, op1=ALU.add), [b_reset, bla], [bbc])
            for t in range(T):
                lo = t * 128
                tsc(R[:, 5, lo:lo + 128], bc[:, lo:lo + 128], bc[:, lo + 64:lo + 65], None, ALU.subtract, None, [bbc], [b_rt[5]])
                act(R[:, 6, lo:lo + 128], bc[:, lo:lo + 128], AF.Exp, [bbc], [b_rt[6]], bias=bc[:, lo + 127:lo + 128], scale=-1.0)
                act(cols[:, 2 * t:2 * t + 1], bc[:, lo + 64:lo + 65], AF.Exp, [bbc], [b_cols])
                act(cols[:, 2 * t + 1:2 * t + 2], bc[:, lo + 127:lo + 128], AF.Exp, [bbc], [b_cols])
            act(R[:, 7, 0:N], R[:, 5, 0:N], AF.Exp, [b_rt[5]], [b_rt[7]])
            act(R[:, 8, 0:N], R[:, 5, 0:N], AF.Exp, [b_rt[5]], [b_rt[8]], scale=-1.0)
            tt(qt[:, 0:N], qv[:, 0:N], R[:, 7, 0:N], ALU.mult, [bq, b_rt[7]], [b_qt])
            tt(kt[:, 0:N], kv_[:, 0:N], R[:, 8, 0:N], ALU.mult, [bk, b_rt[8]], [b_kt])
            tt(R[:, 9, 0:N], kv_[:, 0:N], R[:, 6, 0:N], ALU.mult, [bk, b_rt[6]], [b_rt[9]])
            for t in range(T):
                lo = t * 128
                pk, pkb = nb()
                fw.op("pe", lambda e, pk=pk, lo=lo: e.transpose(pk[:, 0:128], R[:, 9, lo:lo + 128], ident), [b_rt[9], b_ident], [pkb])
                cp(kh[:, t, :], pk[:, 0:128], [pkb], [b_kh])
            for t in range(T):
                lo = t * 128
                for (p0, pn, Sap, Sbuf_, vlo, mxr, hidx) in heads:
                    ps = slice(p0, p0 + pn)
                    tsc(Ssc[ps, :], Sap, cols[ps, 2 * t:2 * t + 1], None, ALU.mult, None, [Sbuf_, b_cols], [b_Ssc])
                    pk, pkb = nb()
                    mm(pk[:, 0:128], kt[ps, lo:lo + 128], qt[ps, lo:lo + 128], True, True, [b_kt, b_qt], [pkb])
                    V(lambda e, pk=pk: e.select(out=atf[:], mask=caus[:], on_true=pk[:, 0:128], on_false=zeros[:]),
                      [pkb, b_caus, b_zeros], [b_atf])
                    cp(attm[:], atf[:], [b_atf], [b_attm])
                    p2, p2b = nb()
                    mm(p2[:, 0:128], qt[ps, lo:lo + 128], Ssc[ps, :], True, False, [b_qt, b_Ssc], [p2b])
                    mm(p2[:, 0:128], attm[:], tm[:, t, vlo:vlo + 128], False, True, [b_attm, b_tm[t]], [p2b])
                    cp(obv(mxr, t)[:, (hidx % 4) * 128:(hidx % 4) * 128 + 128], p2[:, 0:128], [p2b], [b_ob[mxr][t]])
                    p3, p3b = nb()
                    mm(p3[:, 0:128], kh[:, t, :], tm[:, t, vlo:vlo + 128], True, True, [b_kh, b_tm[t]], [p3b])
                    stt(Sap, Sap, cols[ps, 2 * t + 1:2 * t + 2], p3[ps, 0:128], ALU.mult, ALU.add, [Sbuf_, b_cols, p3b], [Sbuf_])

        def sample_rec(l, h, j, qv, bq, av, ba, kv_, bk, _unused):
            hg = h is not None
            vlo = 128 if hg else 1152
            accs = []
            nh = 1 if hg else 2
            for hh in range(nh):
                accs.append((bank(6 + hh), pbuf[6 + hh]))
            for b in range(NS):
                src = sh[l, b, h] if hg else sg[l, b, j]
                ld(sS[:], src, [b_sS])
                pv_, pvb = bank(b % 6), pbuf[b % 6]
                mm(pv_[:, 0:512], identR[:, b:b + 1].to_broadcast([128, 128]), tm[:, 0, vlo:vlo + 512], True, True,
                   [b_ident, b_tm[0]], [pvb])
                tsc(sSn[:], sS[:], av[:, b:b + 1], None, ALU.mult, None, [b_sS, ba], [b_sSn])
                for hh in range(nh):
                    ps = slice(0, 128) if hg else slice(hh * 64, hh * 64 + 64)
                    hd = h if hg else 2 * j + hh
                    stt(sSn[ps, :], pv_[ps, hd * 128:(hd + 1) * 128], kv_[ps, b:b + 1], sSn[ps, :], ALU.mult, ALU.add,
                        [pvb, bk, b_sSn], [b_sSn])
                dst = nhs[l, b, h] if hg else ngs[l, b, j]
                st(dst, sSn[:], [b_sSn])
                cp(sSr[:], sSn[:], [b_sSn], [b_sSr])
                zi = b % 4
                cp(qz[:, zi, 127:128], qv[:, b:b + 1], [bq], [b_qz[zi]])
                for hh in range(nh):
                    ps = slice(0, 128) if hg else slice(hh * 64, hh * 64 + 64)
                    pk, pkb = accs[hh]
                    mm(pk[:, 0:128], qz[ps, zi, 127 - b:255 - b], sSr[ps, :], b == 0, b == NS - 1, [b_qz[zi], b_sSr], [pkb])
            for hh in range(nh):
                hd = h if hg else 2 * j + hh
                pk, pkb = accs[hh]
                cp(obv(0 if hg else 1, 0)[:, (hd % 4) * 128:(hd % 4) * 128 + 128], pk[:, 0:128], [pkb], [b_ob[0 if hg else 1][0]])

        def sample_attn_block(l, j):
            kv = j // 4
            for b in range(NS):
                tt(slh[:, 0:2], f32(qrot[:, b:b + 1]).to_broadcast([128, 2]), hmask, ALU.mult, [b_qrot, b_misc], [b_slh])
                p1, p1b = nb()
                mm(p1[0:2, 0:130], slh[:, 0:2], skTv(b, kv), True, True, [b_slh, b_big[2 + b // 4]], [p1b])
                tsc(ssb[0:2, 0:129], p1[0:2, 0:129], 0.125, None, ALU.mult, None, [p1b], [b_ssb])
                V(lambda e: e.reduce_max(out=sm[0:2, 0:1], in_=ssb[0:2, 0:129], axis=AX.X), [b_ssb], [b_sm])
                tt(sm[0:2, 6:8], sink_bc[0:2, 2 * j:2 * j + 2], misc[0:2, 20:22], ALU.mult, [b_sink, b_misc], [b_sm])
                V(lambda e: e.reduce_sum(out=sm[0:2, 7:8], in_=sm[0:2, 6:8], axis=AX.X), [b_sm], [b_sm])
                tsc(sm[0:2, 1:2], sm[0:2, 0:1], sm[0:2, 7:8], -1.0, ALU.max, ALU.mult, [b_sm], [b_sm])
                V(lambda e: e.memset(sm[0:2, 2:3], 0.0), (), [b_sm])
                act(psb[0:2, 0:129], ssb[0:2, 0:129], AF.Exp, [b_ssb, b_sm], [b_psb, b_sm], bias=sm[0:2, 1:2], accum=sm[0:2, 2:3])
                act(sm[0:2, 3:4], sm[0:2, 7:8], AF.Exp, [b_sm], [b_sm], bias=sm[0:2, 1:2])
                tt(sm[0:2, 4:5], sm[0:2, 2:3], sm[0:2, 3:4], ALU.add, [b_sm], [b_sm])
                V(lambda e: e.reciprocal(out=sm[0:2, 5:6], in_=sm[0:2, 4:5]), [b_sm], [b_sm])
                p3, p3b = nb()
                fw.op("pe", lambda e, p3=p3: e.transpose(p3[:, 0:2], psb[0:2, 0:128], ident[0:2, 0:2]), [b_psb, b_ident], [p3b])
                cp(pT[:, 0, 0:2], p3[:, 0:2], [p3b], [b_pT])
                p4, p4b = nb()
                mm(p4[0:2, 0:128], pT[:, 0, 0:2], sv[:, b, :], True, True, [b_pT, b_sv], [p4b])
                p5, p5b = nb()
                mm(p5[0:2, 0:128], identR[:, b:b + 1].to_broadcast([128, 2]), tm[:, 0, 0:128], True, True, [b_ident, b_tm[0]], [p5b])
                cp(ssb[0:2, 0:128], p4[0:2, 0:128], [p4b], [b_ssb])
                stt(t1[0:2, 0:128], p5[0:2, 0:128], psb[0:2, 128:129], ssb[0:2, 0:128], ALU.mult, ALU.add, [p5b, b_ssb, b_psb], [b_t1])
                tsc(osel[0:2, :], t1[0:2, kv * 64:(kv + 1) * 64], sm[0:2, 5:6], None, ALU.mult, None, [b_t1, b_sm], [b_osel])
                for hh in range(2):
                    fw.dma("pool", lambda e, b=b, j=j, hh=hh: e.dma_start(out=big[b:b + 1, 0, (2 * j + hh) * 64:(2 * j + hh + 1) * 64],
                                                                      in_=osel[hh:hh + 1, :]), [b_osel], [b_big[0]])

        def peer(l, T, sample):
            keyv = ring[:, NRING - 1, 0:2048].rearrange("p (g n) -> p g n", n=128)
            bkey = b_ring[NRING - 1]
            for u in range(8):
                slot = u % (NRING - 1)
                ldr(ring[:, slot, :], wst[l, U_WQ + u], [b_ring[slot]])
                wv = ring[:, slot, :].rearrange("p (c n) -> p c n", n=256)
                for blk in range(2):
                    hp = u * 2 + blk
                    pk, pkb = fm_block(wv, b_ring[slot], blk, T * 128)
                    cp(pq[:, 0:T * 128], pk[:, 0:T * 128], [pkb], [b_pq])
                    for t in range(T):
                        p2, p2b = nb()
                        mm(p2[:, 0:128], pq[:, t * 128:(t + 1) * 128], keyv[:, hp, :], True, True, [b_pq, bkey], [p2b])
                        cp(big[:, t, hp * 128:(hp + 1) * 128], p2[:, 0:128], [p2b], [b_big[t]])
            for t in range(T):
                sc = f32(big[:, t, :]); sc2 = lnb[:]; sc2w = lnb[:]; cand = lng[:]
                bsc = b_big[t]
                sc3 = sc.rearrange("p (g n) -> p g n", n=128)
                s23 = sc2.rearrange("p (g n) -> p g n", n=128)
                s23w = sc2w.rearrange("p (g n) -> p g n", n=128)
                for hp in range(16):
                    V(lambda e, hp=hp, sc3=sc3: e.max(out=tv[:, hp, 0:8], in_=sc3[:, hp, :]), [bsc], [b_tv])
                    V(lambda e, hp=hp, sc3=sc3: e.max_index(out=ti[:, hp, 0:8], in_max=tv[:, hp, 0:8], in_values=sc3[:, hp, :]), [bsc, b_tv], [b_ti])
                    V(lambda e, hp=hp, sc3=sc3, s23=s23w: e.match_replace(out=s23[:, hp, :], in_to_replace=tv[:, hp, 0:8], in_values=sc3[:, hp, :], imm_value=NEG),
                      [bsc, b_tv], [b_lnb])
                    V(lambda e, hp=hp, s23=s23: e.max(out=tv[:, hp, 8:16], in_=s23[:, hp, :]), [b_lnb], [b_tv])
                    V(lambda e, hp=hp, s23=s23: e.max_index(out=ti[:, hp, 8:16], in_max=tv[:, hp, 8:16], in_values=s23[:, hp, :]), [b_lnb, b_tv], [b_ti])
                V(lambda e: e.tensor_copy(tif[:], ti[:]), [b_ti], [b_tif])
                tv4 = tv[:].rearrange("p (h two) k -> p h two k", two=2)
                tif4 = tif[:].rearrange("p (h two) k -> p h two k", two=2)
                c4 = cand.rearrange("p (h i j) -> p h i j", i=16, j=16)
                tt(c4, tv4[:, :, 0, :].unsqueeze(3).to_broadcast([128, 8, 16, 16]),
                   tv4[:, :, 1, :].unsqueeze(2).to_broadcast([128, 8, 16, 16]), ALU.add, [b_tv], [b_lng])
                c3 = cand.rearrange("p (h n) -> p h n", n=256)
                o3 = sc2.rearrange("p (h n) -> p h n", n=256)
                o3w = sc2w.rearrange("p (h n) -> p h n", n=256)
                for h in range(8):
                    V(lambda e, h=h, c3=c3: e.max(out=bv[:, h, 0:8], in_=c3[:, h, :]), [b_lng], [b_bv])
                    V(lambda e, h=h, c3=c3: e.max_index(out=bi[:, h, 0:8], in_max=bv[:, h, 0:8], in_values=c3[:, h, :]), [b_lng, b_bv], [b_bi])
                    V(lambda e, h=h, c3=c3, o3=o3w: e.match_replace(out=o3[:, h, :], in_to_replace=bv[:, h, 0:8], in_values=c3[:, h, :], imm_value=NEG),
                      [b_lng, b_bv], [b_lnb])
                    V(lambda e, h=h, o3=o3: e.max(out=bv[:, h, 8:16], in_=o3[:, h, :]), [b_lnb], [b_bv])
                    V(lambda e, h=h, o3=o3: e.max_index(out=bi[:, h, 8:16], in_max=bv[:, h, 8:16], in_values=o3[:, h, :]), [b_lnb, b_bv], [b_bi])
                g3 = gate[:, t, :].rearrange("p (h k) -> p h k", k=16)
                tt(g3, bv[:], bv[:, :, 0:1].to_broadcast([128, 8, 16]), ALU.subtract, [b_bv], [b_gate])
                act(gate[:, t, :], gate[:, t, :], AF.Exp, [b_gate], [b_gate])
                V(lambda e, g3=g3: e.reduce_sum(out=sm[:, 8:16], in_=g3, axis=AX.X), [b_gate], [b_sm])
                V(lambda e: e.reciprocal(out=sm[:, 8:16], in_=sm[:, 8:16]), [b_sm], [b_sm])
                tt(g3, g3, sm[:, 8:16].unsqueeze(2).to_broadcast([128, 8, 16]), ALU.mult, [b_gate, b_sm], [b_gate])
                bi2 = bi[:].rearrange("p h k -> p (h k)")
                V(lambda e, bi2=bi2: e.tensor_single_scalar(out=bti[:, 0, :], in_=bi2, scalar=4, op=ALU.logical_shift_right), [b_bi], [b_bt])
                V(lambda e, bi2=bi2: e.tensor_single_scalar(out=bti[:, 1, :], in_=bi2, scalar=15, op=ALU.bitwise_and), [b_bi], [b_bt])
                V(lambda e: e.tensor_copy(bt[:, 0:2, :], bti[:]), [b_bt], [b_bt])
                for which in range(2):
                    posf = bt[:, which, :].rearrange("p (h k) -> p h k", k=16)
                    e4 = cand.rearrange("p (h k i) -> p h k i", k=16, i=16)
                    tt(e4, posf.unsqueeze(3).to_broadcast([128, 8, 16, 16]),
                       iota16.unsqueeze(1).unsqueeze(1).to_broadcast([128, 8, 16, 16]), ALU.is_equal, [b_bt, b_misc], [b_lng])
                    tt(e4, e4, tif4[:, :, which, :].unsqueeze(2).to_broadcast([128, 8, 16, 16]), ALU.mult, [b_lng, b_tif], [b_lng])
                    V(lambda e, which=which, e4=e4: e.reduce_sum(out=bt[:, 2 + which, :].rearrange("p (h k) -> p h k", k=16), in_=e4, axis=AX.X),
                      [b_lng], [b_bt])
                stt(idxf[:], bt[:, 2, :], 128.0, bt[:, 3, :], ALU.mult, ALU.add, [b_bt], [b_idxf])
                tsc(idxf[:], idxf[:], float(l * 16384), None, ALU.add, None, [b_idxf], [b_idxf])
                pk, pkb = nb()
                fw.op("pe", lambda e, pk=pk: e.transpose(pk[:, 0:128], idxf[:], ident), [b_idxf, b_ident], [pkb])
                V(lambda e, pk=pk, t=t: e.tensor_copy(idxT[:, t, :], pk[:, 0:128]), [pkb], [b_idxT])
                pk2, pk2b = nb()
                fw.op("pe", lambda e, pk2=pk2, t=t: e.transpose(pk2[:, 0:128], gate[:, t, :], ident), [b_gate, b_ident], [pk2b])
                cp(gateT[:, t, :], pk2[:, 0:128], [pk2b], [b_gateT])
            ld(lng[:], l2g[l:l + 1, :].partition_broadcast(128), [b_lng])
            ld(lnb[:], l2b[l:l + 1, :].partition_broadcast(128), [b_lnb])
            for t in range(T):
                ntok = NS if sample else 128
                xr = xrt[:]
                cp(xr, x_g[:, t, :], [b_x[t]], [b_xr])
                G(lambda e: e.memset(hT[:], 0.0), (), [b_hT])
                for tok in range(ntok):
                    gsl, gbuf = gslots[tok % NSLOT]
                    fw.dma("pool", lambda e, gsl=gsl, tok=tok, t=t: e.indirect_dma_start(
                        out=gsl, out_offset=None, in_=pub,
                        in_offset=bass.IndirectOffsetOnAxis(ap=idxT[:, t, tok:tok + 1], axis=0)), [b_idxT, b_pub[l]], gbuf)
                    PX = PA if tok % 2 == 0 else PB
                    pxb = pbuf[0:4] if tok % 2 == 0 else pbuf[4:8]
                    for c in range(4):
                        mm(PX[:, c * 512:(c + 1) * 512], identR[:, tok:tok + 1].to_broadcast([128, 128]), xr[:, c * 512:(c + 1) * 512],
                           True, True, [b_ident, b_xr], [pxb[c]])
                    stt(gsl, gsl, 1.0, PX[:, :], ALU.mult, ALU.mult, gbuf + pxb, gbuf + [b_hT],
                        accum=hT[:, tok:tok + 1])
                act(wT[:], hT[:], AF.Gelu, [b_hT], [b_wT])
                tt(wT[:], wT[:], gateT[:, t, :], ALU.mult, [b_wT, b_gateT], [b_wT])
                for tok in range(ntok):
                    gsl, gbuf = gslots[tok % NSLOT]
                    fw.dma("pool", lambda e, gsl=gsl, tok=tok, t=t: e.indirect_dma_start(
                        out=gsl, out_offset=None, in_=pvb,
                        in_offset=bass.IndirectOffsetOnAxis(ap=idxT[:, t, tok:tok + 1], axis=0)), [b_idxT, b_pvb[l]], gbuf)
                    zi = tok % 4
                    cp(wzb[:, zi, 127:128], wT[:, tok:tok + 1], [b_wT], [b_wzb[zi]])
                    for c in range(4):
                        mm(PA[:, c * 512:(c + 1) * 512], wzb[:, zi, 127 - tok:255 - tok], gsl[:, c * 512:(c + 1) * 512],
                           tok == 0, tok == ntok - 1, [b_wzb[zi]] + gbuf, [pbuf[c]])
                for c in range(4):
                    stt(x_g[:, t, c * 512:(c + 1) * 512], x_g[:, t, c * 512:(c + 1) * 512], ALPHA, PA[:, c * 512:(c + 1) * 512],
                        ALU.mult, ALU.add, [b_x[t], pbuf[c]], [b_x[t]])
                layer_norm(t, lng, lnb, l)

        groups = [(False, g) for g in range(NG)] + [(True, 0)]
        for (sample, g) in groups:
            T = 1 if sample else 2
            if sample:
                ld(x_g[:, 0, :], xsm, [b_x[0]])
                ld(cs[:, :, 0:128], c_cs[:, :, SEQ:SEQ + 128], [b_cs])
            else:
                for t in range(2):
                    ld(x_g[:, t, :], xp[g * 256 + t * 128:g * 256 + (t + 1) * 128, :], [b_x[t]])
                ld(cs[:], c_cs[:, :, g * 256:(g + 1) * 256], [b_cs])
            for l in range(DEPTH):
                layer_group(l, T, sample, g)
            if sample:
                st(y_s, x_g[:, 0, :], [b_x[0]])
            else:
                for t in range(2):
                    st(y_p[g * 256 + t * 128:g * 256 + (t + 1) * 128, :], x_g[:, t, :], [b_x[t]])
        for l in range(DEPTH):
            for h in range(4):
                st(nhp[l, h], Sb[:, l, h, :], [b_Sb[l][h]])
            for j in range(2):
                st(ngp[l, j], Sc[:, l, j, :], [b_Sc[l][j]])
        fw.finish()
        print("ops recorded:", fw.ninst, flush=True)
        fw.emit_all()
    return nc


_CACHE = {}


def _consts():
    ident = np.eye(128, dtype=np.float32)
    perm = np.zeros((128, 128), np.float32)
    for m in range(128):
        d = m % 64
        if d < 32:
            perm[m + 32, m] = -1.0
        else:
            perm[m - 32, m] = 1.0
    half = 32
    inv = (10000.0 ** (-np.arange(half, dtype=np.float32) / half)).astype(np.float32)
    pos = np.concatenate([np.arange(SEQ, dtype=np.float32), np.full((128,), PAST, np.float32)])
    ang = pos[None, :] * inv[:, None]
    cos = np.cos(ang).astype(np.float32); sin = np.sin(ang).astype(np.float32)
    cs = np.zeros((128, 2, SEQ + 128), np.float32)
    for p in range(128):
        cs[p, 0] = cos[p % 32]; cs[p, 1] = sin[p % 32]
    i = np.arange(128)[:, None]; jj = np.arange(256)[None, :]
    diff = i + 128 - jj
    bandv = (diff >= 0) & (diff <= 128)
    band = np.zeros((2, 128, 256), np.float32)
    band[1] = np.where(bandv, 0.0, NEG)
    band[0] = np.where(bandv & (jj >= 128), 0.0, NEG)
    caus = (np.arange(128)[:, None] <= np.arange(128)[None, :]).astype(np.uint32)
    reset = np.ones((128, 256), np.float32); reset[:, 0] = 0.0; reset[:, 128] = 0.0
    misc = np.zeros((128, 64), np.float32)
    misc[:, 0:16] = np.arange(16, dtype=np.float32)[None, :]
    misc[0:64, 16] = 1.0; misc[64:128, 17] = 1.0
    misc[0, 20] = 1.0; misc[1, 21] = 1.0
    return dict(c_ident=ident, c_perm=perm, c_cs=cs, c_band=band, c_caus=caus, c_reset=reset, c_misc=misc)


def _weight_stream(w_in, w_out, peer_wq):
    L = w_in.shape[0]
    units = np.zeros((L, NU, 2048, 256), np.float32)
    tmcols = np.concatenate([np.arange(1152, 1280), np.arange(2304, 2816), np.arange(2816, 3328),
                             np.arange(3840, 4352), np.arange(4352, 4864)])
    for l in range(L):
        W = w_in[l]
        tmw = W[:, tmcols]
        for u in range(8):
            units[l, U_TM + u] = tmw[:, u * 256:(u + 1) * 256]
        units[l, U_TM + 8, :, 0:128] = tmw[:, 2048:2176]
        units[l, U_TM + 8, :, 128:144] = W[:, 4864:4880]
        k0 = W[:, 1024:1088]; k1 = W[:, 1088:1152]
        units[l, U_FM + 0] = np.concatenate([k0, k0, k1, k1], 1)
        for u in range(4):
            units[l, U_FM + 1 + u] = W[:, u * 256:(u + 1) * 256]
        for h in range(4):
            units[l, U_FM + 5 + h] = np.concatenate([W[:, 1280 + h * 128:1280 + (h + 1) * 128],
                                                     W[:, 1792 + h * 128:1792 + (h + 1) * 128]], 1)
        for j in range(2):
            units[l, U_FM + 9 + j] = np.concatenate([W[:, 3328 + j * 128:3328 + (j + 1) * 128],
                                                     W[:, 3584 + j * 128:3584 + (j + 1) * 128]], 1)
        for u in range(8):
            units[l, U_WO + u] = w_out[l][:, u * 256:(u + 1) * 256]
            units[l, U_WQ + u] = peer_wq[l][:, u * 256:(u + 1) * 256]
    units = units.reshape(L, NU, 16, 128, 256).transpose(0, 1, 3, 2, 4).reshape(L, NU, 128, 4096)
    return np.ascontiguousarray(units)


def kernel(x_prompt, x_sample, cache_k, cache_v, state_hgrn, state_gla, w_in, w_out, attn_sinks,
           hgrn_norm_w, lb_logits, gla_wa2, gla_ba, gla_norm_w, ln1_g, ln1_b, ln2_g, ln2_b,
           peer_wq, peer_keys, peer_u, peer_v):
    f = lambda a: np.ascontiguousarray(np.asarray(a, dtype=np.float32))
    x_prompt = f(x_prompt); x_sample = f(x_sample); cache_k = f(cache_k); cache_v = f(cache_v)
    state_hgrn = f(state_hgrn); state_gla = f(state_gla)
    if "nc" not in _CACHE:
        _CACHE["nc"] = build_program()
    nc = _CACHE["nc"]
    consts = _consts()
    wst = _weight_stream(f(w_in), f(w_out), f(peer_wq))
    pu = f(peer_u).reshape(4 * 16384, D); pv = f(peer_v).reshape(4 * 16384, D)
    keysT = np.ascontiguousarray(f(peer_keys).reshape(4, 16, 128, 128).transpose(0, 3, 1, 2).reshape(4, 128, 2048))
    lbT = np.ascontiguousarray(f(lb_logits).reshape(4, 4, 128).transpose(2, 1, 0).reshape(128, 16))
    nbaT = np.ascontiguousarray(f(gla_ba).reshape(4, 2, 128).transpose(2, 0, 1).reshape(128, 8))
    shared = dict(wst=wst, pu=pu, pv=pv, keysT=keysT, sinks=f(attn_sinks), hnw=f(hgrn_norm_w).reshape(4, 512),
                  gnw=f(gla_norm_w).reshape(4, 512), lbT=lbT, wa2=f(gla_wa2), nbaT=nbaT,
                  l1g=f(ln1_g), l1b=f(ln1_b), l2g=f(ln2_g), l2b=f(ln2_b), **consts)
    in_maps = []
    for c in range(NCORES):
        sq = c % 4
        sb = slice(NS * c, NS * c + NS)
        xs = np.zeros((128, D), np.float32); xs[0:NS] = x_sample[sb, 0, :]
        ckc = cache_k[:, sb].reshape(4, NS, 128, 128)
        ckd = np.concatenate([ckc[..., 0:64], ckc[..., 0:64], ckc[..., 64:128], ckc[..., 64:128]], -1)
        cvc = np.ascontiguousarray(cache_v[:, sb].reshape(4, NS, 128, 128))
        m = dict(shared)
        m.update(xp=x_prompt[sq], xsm=xs, ck=np.ascontiguousarray(ckc), ckd=np.ascontiguousarray(ckd),
                 cv=cvc, cvr=cvc,
                 sh=np.ascontiguousarray(state_hgrn[:, sb]), sg=np.ascontiguousarray(state_gla[:, sb].reshape(4, NS, 2, 128, 128)))
        in_maps.append(m)
    res = run_bass_kernel_spmd(nc, in_maps, core_ids=list(range(NCORES))).results
    y_p = np.stack([res[c]["y_p"] for c in range(4)]).reshape(4, SEQ, D)
    y_s = np.concatenate([res[c]["y_s"][0:NS] for c in range(NCORES)]).reshape(32, 1, D)
    nkp = np.stack([res[c]["nkp"] for c in range(4)], 1).reshape(4, 4, 128, 2, 64)
    nvp = np.stack([res[c]["nvp"] for c in range(4)], 1).reshape(4, 4, 128, 2, 64)
    nhp = np.stack([res[c]["nhp"] for c in range(4)], 1).reshape(4, 4, 4, 128, 128)
    ngp = np.stack([res[c]["ngp"] for c in range(4)], 1).reshape(4, 4, 4, 64, 128)
    nks = np.concatenate([res[c]["nks"] for c in range(NCORES)], 1).reshape(4, 32, 128, 2, 64)
    nvs = np.concatenate([res[c]["nvs"] for c in range(NCORES)], 1).reshape(4, 32, 128, 2, 64)
    nhs = np.concatenate([res[c]["nhs"] for c in range(NCORES)], 1).reshape(4, 32, 4, 128, 128)
    ngs = np.concatenate([res[c]["ngs"] for c in range(NCORES)], 1).reshape(4, 32, 4, 64, 128)
    return tuple(np.ascontiguousarray(a.astype(np.float32)) for a in (y_p, y_s, nkp, nvp, nhp, ngp, nks, nvs, nhs, ngs))
```
